# Optimizing a Trainium2 kernel written in Bass

```python
import math
import jax, jax.numpy as jnp
from jax import lax
import numpy as np

D_MODEL = 1024
BATCH = 8
SEQ = 4096
DEPTH = 2

CONV_K = 4
CHUNK = 64
NORM_EPS = 1e-6
DT_MIN = 1e-3
DT_MAX = 1e-1
S5_GROUP = 16
S5_STATE = 64
S5_WIDTH = 3 * D_MODEL // 8
S5_GROUPS = S5_WIDTH // S5_GROUP
GDN_HEADS = 4
GDN_DK = 128
GDN_DV = 128
GDN_QK = GDN_HEADS * GDN_DK
GDN_WIDTH = GDN_HEADS * GDN_DV
SSD_HEAD_DIM = 64
SSD_WIDTH = D_MODEL // 2
SSD_HEADS = SSD_WIDTH // SSD_HEAD_DIM
SSD_GROUPS = 2
SSD_HPG = SSD_HEADS // SSD_GROUPS
SSD_STATE = 64
SSD_BC = SSD_GROUPS * SSD_STATE
LRU_WIDTH = D_MODEL // 2
LRU_BLOCK = 64
LRU_BLOCKS = LRU_WIDTH // LRU_BLOCK
LRU_C = 8.0
N_BRANCH = 4
BRANCH_WIDTHS = (S5_WIDTH, GDN_WIDTH, SSD_WIDTH, LRU_WIDTH)
MIX_WIDTH = sum(BRANCH_WIDTHS)
CONV_SILU_WIDTHS = (GDN_QK, GDN_QK, GDN_WIDTH, SSD_WIDTH, SSD_BC, SSD_BC)
CONV_SILU_CH = sum(CONV_SILU_WIDTHS)
CONV_CH = CONV_SILU_CH + LRU_WIDTH
REST_WIDTHS = (S5_WIDTH, GDN_HEADS, GDN_HEADS, GDN_WIDTH, SSD_WIDTH, SSD_HEADS, LRU_WIDTH, N_BRANCH * D_MODEL)
IN_COLS = CONV_CH + sum(REST_WIDTHS)
FFN_HIDDEN = -(-8 * D_MODEL // (3 * 256)) * 256

kernel_name = "adaln_hybrid_s5_gdn_ssd_rglru_trunk"

F32 = jnp.float32


def _rms(x):
    x32 = x.astype(F32)
    return x32 * lax.rsqrt(jnp.mean(x32 * x32, axis=-1, keepdims=True) + NORM_EPS)


def _split(t, widths, axis=-1):
    idx, acc = [], 0
    for w in widths[:-1]:
        acc += w
        idx.append(acc)
    return jnp.split(t, idx, axis=axis)


def _causal_conv(x, w, b):
    k_taps, s = w.shape[0], x.shape[1]
    xp = jnp.pad(x, ((0, 0), (k_taps - 1, 0), (0, 0)))
    y = b + xp[:, 0:s] * w[0]
    for k in range(1, k_taps):
        y = y + xp[:, k:k + s] * w[k]
    return y


def _causal_masks(n):
    pos = jnp.arange(n)
    return pos[:, None] >= pos[None, :], pos[:, None] > pos[None, :]


def _segment_decay(cs, incl):
    diff = cs[..., :, None] - cs[..., None, :]
    return jnp.exp(jnp.where(incl, diff, -jnp.inf))


def _linear_combine(left, right):
    a_l, b_l = left
    a_r, b_r = right
    return a_r * a_l, a_r * b_l + b_r


def _s5_branch(u, lam_re, lam_im, log_dt, b_re, b_im, c_re, c_im, d, glu_w, glu_b):
    bsz, s, _ = u.shape
    ug = u.astype(F32).reshape(bsz, s, S5_GROUPS, S5_GROUP)
    lam = lax.complex(lam_re.astype(F32), lam_im.astype(F32))
    dt = jnp.exp(log_dt.astype(F32))[:, None]
    lam_bar = jnp.exp(lam * dt)
    b_mat = lax.complex(b_re.astype(F32), b_im.astype(F32))
    b_bar = ((lam_bar - 1.0) / lam)[..., None] * b_mat
    bu = jnp.einsum('bsgh,gph->bsgp', ug.astype(jnp.complex64), b_bar)
    a = jnp.broadcast_to(lam_bar, (1, s) + lam_bar.shape)
    _, states = lax.associative_scan(_linear_combine, (a, bu), axis=1)
    c_mat = lax.complex(c_re.astype(F32), c_im.astype(F32))
    y = jnp.real(jnp.einsum('bsgp,ghp->bsgh', states, c_mat)) + d.astype(F32).reshape(S5_GROUPS, S5_GROUP) * ug
    y = jax.nn.gelu(y.reshape(bsz, s, S5_WIDTH))
    return y * jax.nn.sigmoid(y @ glu_w.astype(F32) + glu_b.astype(F32))


def _chunked_gated_delta(q, k, v, g, beta):
    bsz, s, h, dk = q.shape
    dv = v.shape[-1]
    nc = s // CHUNK

    def blocks(t):
        return jnp.moveaxis(t.reshape((bsz, nc, CHUNK, h) + t.shape[3:]), 3, 2)

    q, k, v, g, beta = blocks(q), blocks(k), blocks(v), blocks(g), blocks(beta)
    g = jnp.cumsum(g, axis=-1)
    incl, strict = _causal_masks(CHUNK)
    decay = _segment_decay(g, incl)
    kb = k * beta[..., None]
    a_strict = jnp.where(strict, jnp.einsum('bnhcd,bnhsd->bnhcs', kb, k) * decay, 0.0)
    t_mat = a_strict + jnp.eye(CHUNK, dtype=F32)
    rhs = jnp.concatenate([kb * jnp.exp(g)[..., None], v * beta[..., None]], axis=-1)
    sol = lax.linalg.triangular_solve(t_mat, rhs, left_side=True, lower=True, unit_diagonal=True)
    w, u = sol[..., :dk], sol[..., dk:]
    attn = jnp.einsum('bnhcd,bnhsd->bnhcs', q, k) * decay
    g_last = g[..., -1]
    q_dec = q * jnp.exp(g)[..., None]
    k_dec = k * jnp.exp(g_last[..., None] - g)[..., None]

    def step(state, inp):
        w_c, u_c, q_c, k_c, attn_c, gl_c = inp
        v_new = u_c - jnp.einsum('bhcd,bhde->bhce', w_c, state)
        o_c = jnp.einsum('bhcd,bhde->bhce', q_c, state) + jnp.einsum('bhcs,bhse->bhce', attn_c, v_new)
        state = state * jnp.exp(gl_c)[..., None, None] + jnp.einsum('bhcd,bhce->bhde', k_c, v_new)
        return state, o_c

    xs = tuple(jnp.moveaxis(t, 1, 0) for t in (w, u, q_dec, k_dec, attn, g_last))
    _, o = lax.scan(step, jnp.zeros((bsz, h, dk, dv), F32), xs)
    return jnp.transpose(o, (1, 0, 3, 2, 4)).reshape(bsz, s, h, dv)


def _gdn_branch(q, k, v, b_raw, a_raw, z, a_log, dt_bias, norm_w):
    bsz, s, _ = q.shape
    q = q.astype(F32).reshape(bsz, s, GDN_HEADS, GDN_DK)
    k = k.astype(F32).reshape(bsz, s, GDN_HEADS, GDN_DK)
    v = v.astype(F32).reshape(bsz, s, GDN_HEADS, GDN_DV)
    q = q * lax.rsqrt(jnp.sum(q * q, axis=-1, keepdims=True) + NORM_EPS) * (GDN_DK ** -0.5)
    k = k * lax.rsqrt(jnp.sum(k * k, axis=-1, keepdims=True) + NORM_EPS)
    beta = jax.nn.sigmoid(b_raw.astype(F32))
    g = -jnp.exp(a_log.astype(F32)) * jax.nn.softplus(a_raw.astype(F32) + dt_bias.astype(F32))
    o = _chunked_gated_delta(q, k, v, g, beta)
    o = _rms(o) * norm_w.astype(F32) * jax.nn.silu(z.astype(F32).reshape(bsz, s, GDN_HEADS, GDN_DV))
    return o.reshape(bsz, s, GDN_WIDTH)


def _chunked_ssd(xdt, la, bm, cm):
    bsz, s = xdt.shape[:2]
    nc = s // CHUNK
    xdt = xdt.reshape(bsz, nc, CHUNK, SSD_GROUPS, SSD_HPG, SSD_HEAD_DIM)
    bm = bm.reshape(bsz, nc, CHUNK, SSD_GROUPS, SSD_STATE)
    cm = cm.reshape(bsz, nc, CHUNK, SSD_GROUPS, SSD_STATE)
    cs = jnp.cumsum(jnp.moveaxis(la.reshape(bsz, nc, CHUNK, SSD_GROUPS, SSD_HPG), 2, -1), axis=-1)
    incl, _ = _causal_masks(CHUNK)
    decay = _segment_decay(cs, incl)
    cb = jnp.einsum('bzlgn,bzmgn->bzglm', cm, bm)
    y_diag = jnp.einsum('bzglm,bzgjlm,bzmgjp->bzlgjp', cb, decay, xdt)
    to_end = jnp.exp(cs[..., -1:] - cs)
    chunk_states = jnp.einsum('bzmgn,bzgjm,bzmgjp->bzgjpn', bm, to_end, xdt)
    chunk_decay = jnp.exp(cs[..., -1])

    def step(state, inp):
        st_c, dec_c = inp
        return state * dec_c[..., None, None] + st_c, state

    state0 = jnp.zeros((bsz, SSD_GROUPS, SSD_HPG, SSD_HEAD_DIM, SSD_STATE), F32)
    _, prev = lax.scan(step, state0, (jnp.moveaxis(chunk_states, 1, 0), jnp.moveaxis(chunk_decay, 1, 0)))
    prev = jnp.moveaxis(prev, 0, 1)
    y_off = jnp.einsum('bzlgn,bzgjpn,bzgjl->bzlgjp', cm, prev, jnp.exp(cs))
    return (y_diag + y_off).reshape(bsz, s, SSD_GROUPS, SSD_HPG, SSD_HEAD_DIM)


def _ssd_branch(xs, bs, cs, z, dt_raw, a_log, dt_bias, d, norm_w):
    bsz, s, _ = xs.shape
    x = xs.astype(F32).reshape(bsz, s, SSD_GROUPS, SSD_HPG, SSD_HEAD_DIM)
    bm = bs.astype(F32).reshape(bsz, s, SSD_GROUPS, SSD_STATE)
    cm = cs.astype(F32).reshape(bsz, s, SSD_GROUPS, SSD_STATE)
    dt = jax.nn.softplus(dt_raw.astype(F32) + dt_bias.astype(F32)).reshape(bsz, s, SSD_GROUPS, SSD_HPG)
    a = -jnp.exp(a_log.astype(F32)).reshape(SSD_GROUPS, SSD_HPG)
    y = _chunked_ssd(x * dt[..., None], dt * a, bm, cm)
    y = y + d.astype(F32).reshape(SSD_GROUPS, SSD_HPG)[..., None] * x
    y = y.reshape(bsz, s, SSD_WIDTH) * jax.nn.silu(z.astype(F32))
    return _rms(y) * norm_w.astype(F32)


def _rglru_branch(x, gate, lam, wr, br, wi, bi):
    bsz, s, _ = x.shape
    x = x.astype(F32)
    xb = x.reshape(bsz, s, LRU_BLOCKS, LRU_BLOCK)
    r = jax.nn.sigmoid(jnp.einsum('bsnd,nde->bsne', xb, wr.astype(F32)).reshape(bsz, s, LRU_WIDTH) + br.astype(F32))
    i = jax.nn.sigmoid(jnp.einsum('bsnd,nde->bsne', xb, wi.astype(F32)).reshape(bsz, s, LRU_WIDTH) + bi.astype(F32))
    log_a = -LRU_C * r * jax.nn.softplus(-lam.astype(F32))
    a = jnp.exp(log_a)
    mult = jnp.sqrt(-jnp.expm1(2.0 * log_a))
    mult = jnp.where((jnp.arange(s) == 0)[None, :, None], 1.0, mult)
    _, h = lax.associative_scan(_linear_combine, (a, mult * i * x), axis=1)
    return h * jax.nn.gelu(gate.astype(F32))


def _swiglu(h, w13, w2):
    a, b = jnp.split(h @ w13, 2, axis=-1)
    return (jax.nn.silu(a) * b) @ w2


def setup_inputs(seed: int = 0) -> dict:
    key = jax.random.key(seed)
    ks = iter(jax.random.split(key, 48))
    L = DEPTH

    def nrm(shape, scale):
        return scale * jax.random.normal(next(ks), shape, F32)

    def unif(shape, lo, hi):
        return jax.random.uniform(next(ks), shape, F32, lo, hi)

    def dt_bias(shape):
        dt = jnp.exp(unif(shape, math.log(DT_MIN), math.log(DT_MAX)))
        return dt + jnp.log(-jnp.expm1(-dt))

    a0 = unif((L, LRU_WIDTH), 0.9, 0.999) ** (1.0 / LRU_C)
    return {
        "x": nrm((BATCH, SEQ, D_MODEL), 1.0),
        "c": nrm((BATCH, D_MODEL), 1.0),
        "ln_mix_g": 1.0 + nrm((L, D_MODEL), 0.02),
        "ln_ffn_g": 1.0 + nrm((L, D_MODEL), 0.02),
        "ln_final_g": 1.0 + nrm((D_MODEL,), 0.02),
        "ada_w": nrm((L, D_MODEL, 6 * D_MODEL), D_MODEL ** -0.5),
        "ada_b": nrm((L, 6 * D_MODEL), 0.01),
        "w_in": nrm((L, D_MODEL, IN_COLS), D_MODEL ** -0.5),
        "conv_w": nrm((L, CONV_K, CONV_CH), CONV_K ** -0.5),
        "conv_b": nrm((L, CONV_CH), 0.01),
        "s5_lambda_re": -0.5 + nrm((L, S5_GROUPS, S5_STATE), 0.01),
        "s5_lambda_im": jnp.pi * jnp.arange(S5_STATE, dtype=F32) + nrm((L, S5_GROUPS, S5_STATE), 0.01),
        "s5_log_dt": unif((L, S5_GROUPS), math.log(DT_MIN), math.log(DT_MAX)),
        "s5_b_re": nrm((L, S5_GROUPS, S5_STATE, S5_GROUP), (2 * S5_GROUP) ** -0.5),
        "s5_b_im": nrm((L, S5_GROUPS, S5_STATE, S5_GROUP), (2 * S5_GROUP) ** -0.5),
        "s5_c_re": nrm((L, S5_GROUPS, S5_GROUP, S5_STATE), (2 * S5_STATE) ** -0.5),
        "s5_c_im": nrm((L, S5_GROUPS, S5_GROUP, S5_STATE), (2 * S5_STATE) ** -0.5),
        "s5_d": nrm((L, S5_WIDTH), 1.0),
        "s5_glu_w": nrm((L, S5_WIDTH, S5_WIDTH), S5_WIDTH ** -0.5),
        "s5_glu_b": nrm((L, S5_WIDTH), 0.01),
        "gdn_a_log": jnp.log(unif((L, GDN_HEADS), 1.0, 16.0)),
        "gdn_dt_bias": dt_bias((L, GDN_HEADS)),
        "gdn_norm_w": 1.0 + nrm((L, GDN_DV), 0.02),
        "ssd_a_log": jnp.log(unif((L, SSD_HEADS), 1.0, 16.0)),
        "ssd_dt_bias": dt_bias((L, SSD_HEADS)),
        "ssd_d": 1.0 + nrm((L, SSD_HEADS), 0.1),
        "ssd_norm_w": 1.0 + nrm((L, SSD_WIDTH), 0.02),
        "lru_lambda": jnp.log(a0) - jnp.log1p(-a0),
        "lru_wr": nrm((L, LRU_BLOCKS, LRU_BLOCK, LRU_BLOCK), LRU_BLOCK ** -0.5),
        "lru_br": nrm((L, LRU_WIDTH), 0.01),
        "lru_wi": nrm((L, LRU_BLOCKS, LRU_BLOCK, LRU_BLOCK), LRU_BLOCK ** -0.5),
        "lru_bi": nrm((L, LRU_WIDTH), 0.01),
        "w_branch": nrm((L, MIX_WIDTH, D_MODEL), GDN_WIDTH ** -0.5),
        "w_out": nrm((L, D_MODEL, D_MODEL), D_MODEL ** -0.5),
        "ffn_w13": nrm((L, D_MODEL, 2 * FFN_HIDDEN), D_MODEL ** -0.5),
        "ffn_w2": nrm((L, FFN_HIDDEN, D_MODEL), FFN_HIDDEN ** -0.5),
    }


def reference(x, c, ln_mix_g, ln_ffn_g, ln_final_g, ada_w, ada_b, w_in, conv_w, conv_b,
              s5_lambda_re, s5_lambda_im, s5_log_dt, s5_b_re, s5_b_im, s5_c_re, s5_c_im,
              s5_d, s5_glu_w, s5_glu_b, gdn_a_log, gdn_dt_bias, gdn_norm_w,
              ssd_a_log, ssd_dt_bias, ssd_d, ssd_norm_w,
              lru_lambda, lru_wr, lru_br, lru_wi, lru_bi,
              w_branch, w_out, ffn_w13, ffn_w2):
    bsz, seq_len, _ = x.shape
    cond = jax.nn.silu(c)
    for l in range(DEPTH):
        mod = (cond @ ada_w[l] + ada_b[l])[:, None, :]
        sh_m, sc_m, gt_m, sh_f, sc_f, gt_f = jnp.split(mod, 6, axis=-1)

        h = (_rms(x) * ln_mix_g[l] * (1.0 + sc_m) + sh_m).astype(x.dtype)
        proj = h @ w_in[l]
        xc = _causal_conv(proj[..., :CONV_CH], conv_w[l], conv_b[l])
        q, k, v, x_ssd, b_ssd, c_ssd = _split(jax.nn.silu(xc[..., :CONV_SILU_CH]), CONV_SILU_WIDTHS)
        x_lru = xc[..., CONV_SILU_CH:]
        u_s5, b_gdn, a_gdn, z_gdn, z_ssd, dt_ssd, g_lru, g_merge = _split(proj[..., CONV_CH:], REST_WIDTHS)

        y_a = _s5_branch(u_s5, s5_lambda_re[l], s5_lambda_im[l], s5_log_dt[l], s5_b_re[l], s5_b_im[l],
                         s5_c_re[l], s5_c_im[l], s5_d[l], s5_glu_w[l], s5_glu_b[l])
        y_b = _gdn_branch(q, k, v, b_gdn, a_gdn, z_gdn, gdn_a_log[l], gdn_dt_bias[l], gdn_norm_w[l])
        y_c = _ssd_branch(x_ssd, b_ssd, c_ssd, z_ssd, dt_ssd, ssd_a_log[l], ssd_dt_bias[l], ssd_d[l], ssd_norm_w[l])
        y_d = _rglru_branch(x_lru, g_lru, lru_lambda[l], lru_wr[l], lru_br[l], lru_wi[l], lru_bi[l])

        gates = jax.nn.sigmoid(g_merge.astype(F32)).reshape(bsz, seq_len, N_BRANCH, D_MODEL)
        rows = _split(w_branch[l], BRANCH_WIDTHS, axis=0)
        merged = (gates[:, :, 0] * (y_a @ rows[0]) + gates[:, :, 1] * (y_b @ rows[1])
                  + gates[:, :, 2] * (y_c @ rows[2]) + gates[:, :, 3] * (y_d @ rows[3]))
        x = x + gt_m * (merged @ w_out[l]).astype(x.dtype)

        h = (_rms(x) * ln_ffn_g[l] * (1.0 + sc_f) + sh_f).astype(x.dtype)
        x = x + gt_f * _swiglu(h, ffn_w13[l], ffn_w2[l]).astype(x.dtype)
    return (_rms(x) * ln_final_g).astype(x.dtype)
```

```python
import contextlib
import numpy as np
import concourse.bass as bass
import concourse.mybir as mybir
from concourse.bass_utils import run_bass_kernel_spmd

AF = mybir.ActivationFunctionType
ALU = mybir.AluOpType
F32 = mybir.dt.float32
BF16 = mybir.dt.bfloat16

D = 1024
NIN = 8848
NEG = -30000.0
TT = 512
C_Q, C_K, C_V, C_XS, C_BS, C_CS, C_XL = 0, 512, 1024, 1536, 2048, 2176, 2304
C_U, C_BG, C_AG, C_ZG, C_ZS, C_DT, C_GL, C_GM = 2816, 3200, 3204, 3208, 3720, 4232, 4240, 4752
YCH = {"a": 0, "b": 3, "c": 7, "d": 11}


class Prog:
    N_DMA_SEMS = 40

    def __init__(self, nc):
        self.nc = nc
        self.es = contextlib.ExitStack()
        self.eng = {"pe": nc.tensor, "act": nc.scalar, "dve": nc.vector, "pool": nc.gpsimd, "sp": nc.sync}
        self.sem = {e: self.es.enter_context(nc.semaphore("s_" + e)) for e in self.eng}
        self.cnt = {e: 0 for e in self.eng}
        self.waited = {e: {} for e in self.eng}
        self.dsem = [self.es.enter_context(nc.semaphore("d%d" % i)) for i in range(self.N_DMA_SEMS)]
        self.dtot = [0] * self.N_DMA_SEMS
        self.dnext = 0
        self.res = {}
        self.stack = [self.es]
        self.uid = 0

    def sb(self, name, shape, dt=F32):
        self.uid += 1
        return self.stack[-1].enter_context(self.nc.sbuf_tensor("%s_%d" % (name, self.uid), list(shape), dt))

    def ps(self, name, shape, dt=F32):
        return self.es.enter_context(self.nc.psum_tensor(name, list(shape), dt))

    @contextlib.contextmanager
    def scope(self):
        st = contextlib.ExitStack()
        self.stack.append(st)
        try:
            yield
        finally:
            self.barrier()
            self.stack.pop()
            st.close()

    def _wait(self, e, ev):
        sem, val, src = ev
        if src == e and e == "pe":
            return
        w = self.waited[e]
        if w.get(sem.name, 0) >= val:
            return
        self.eng[e].wait_ge(sem, val)
        w[sem.name] = val

    def _deps(self, e, reads, writes):
        for k in reads:
            r = self.res.get(k)
            if r and r[0] is not None:
                self._wait(e, r[0])
        for k in writes:
            r = self.res.get(k)
            if r:
                if r[0] is not None:
                    self._wait(e, r[0])
                for ev in r[1]:
                    self._wait(e, ev)

    def _commit(self, ev, reads, writes):
        for k in reads:
            r = self.res.setdefault(k, [None, []])
            r[1].append(ev)
            if len(r[1]) > 12:
                r[1] = r[1][-12:] if False else r[1]
        for k in writes:
            self.res[k] = [ev, []]

    def op(self, e, emit, reads=(), writes=()):
        self._deps(e, reads, writes)
        inst = emit()
        self.cnt[e] += 1
        inst.then_inc(self.sem[e], 1)
        ev = (self.sem[e], self.cnt[e], e)
        self._commit(ev, reads, writes)
        return ev

    def dma(self, q, pairs, reads=(), writes=()):
        self._deps(q, reads, writes)
        k = self.dnext
        self.dnext = (self.dnext + 1) % self.N_DMA_SEMS
        sem = self.dsem[k]
        if self.dtot[k] > 0:
            self._wait(q, (sem, self.dtot[k], None))
        for (o, i) in pairs:
            self.eng[q].dma_start(out=o, in_=i).then_inc(sem, 16)
            self.dtot[k] += 16
        ev = (sem, self.dtot[k], None)
        self._commit(ev, reads, writes)
        return ev

    def barrier(self):
        for e in self.eng:
            for e2 in self.eng:
                if e2 != e and self.cnt[e2] > 0:
                    self._wait(e, (self.sem[e2], self.cnt[e2], e2))
            for k in range(self.N_DMA_SEMS):
                if self.dtot[k] > 0:
                    self._wait(e, (self.dsem[k], self.dtot[k], None))
        self.res = {}


class Ctx:
    pass


def bc(ap, shape, axis):
    return ap.unsqueeze(axis).to_broadcast(list(shape))


def ps_next(K):
    K.ps_i = (K.ps_i + 1) % len(K.psb)
    return K.psb[K.ps_i], "ps%d" % K.ps_i


def load_w(K, dst, dkey, src2d, c0, c1, q="pool", first=False):
    v = src2d.rearrange("(kc p) c -> p kc c", p=128)
    K.p.dma(q, [(dst, v[:, :, c0:c1])], writes=[dkey])


def load_hT(K, t, TT=TT):
    b = K.hbuf[t % 2]
    key = "hbuf%d" % (t % 2)
    K.p.dma("sp", [(b[:], K.hT[:, :, t * TT:(t + 1) * TT].rearrange("kc p t -> p kc t"))], reads=["hT"], writes=[key])
    return b, key


def proj(K, W, wkey, c0, n, hb, hkey, N=TT):
    ps, pk = ps_next(K)
    nc = K.nc

    def emit():
        for kc in range(8):
            inst = nc.tensor.matmul(ps[0:n, 0:N], lhsT=W[:, kc, c0:c0 + n], rhs=hb[:, kc, :], start=(kc == 0), stop=(kc == 7))
        return inst
    K.p.op("pe", emit, reads=[wkey, hkey], writes=[pk])
    return ps, pk


def conv_chunk(K, ps, pk, Pb, pbkey, cw, cb, ci, out_ap, okey, func, n=128, cwk="cw", tails=None, tkey=None):
    nc = K.nc
    p = K.p
    p.op("act", lambda: nc.scalar.copy(out=Pb[0:n, 3:515], in_=ps[0:n, :]), reads=[pk], writes=[pbkey])
    if tails is not None:
        p.op("pool", lambda: nc.gpsimd.tensor_copy(out=Pb[0:n, 0:3], in_=tails[0:n, ci, :]), reads=[tkey, pbkey], writes=[pbkey])
    acc = K.cacc[K.cacc_i % 2]
    akey = "cacc%d" % (K.cacc_i % 2)
    K.cacc_i += 1
    p.op("dve", lambda: nc.vector.tensor_scalar(out=acc[0:n, :], in0=Pb[0:n, 3:515], scalar1=cw[0:n, ci, 3:4], scalar2=cb[0:n, ci:ci + 1],
                                                op0=ALU.mult, op1=ALU.add), reads=[pbkey, cwk], writes=[akey])
    for k in (2, 1, 0):
        p.op("dve", lambda k=k: nc.vector.scalar_tensor_tensor(out=acc[0:n, :], in0=Pb[0:n, k:k + 512], scalar=cw[0:n, ci, k:k + 1], in1=acc[0:n, :],
                                                               op0=ALU.mult, op1=ALU.add), reads=[pbkey, akey, cwk], writes=[akey])
    if tails is not None:
        p.op("pool", lambda: nc.gpsimd.tensor_copy(out=tails[0:n, ci, :], in_=Pb[0:n, 512:515]), reads=[pbkey], writes=[tkey])
    else:
        p.op("pool", lambda: nc.gpsimd.tensor_copy(out=Pb[0:n, 0:3], in_=Pb[0:n, 512:515]), reads=[pbkey], writes=[pbkey])
    p.op("act", lambda: nc.scalar.activation(out=out_ap, in_=acc[0:n, :], func=func), reads=[akey], writes=[okey])


def phase_setup(K):
    p, nc = K.p, K.nc
    G = nc.gpsimd
    K.identb = p.sb("identb", [128, 128], BF16)
    K.identf = p.sb("identf", [128, 128], F32)
    K.onesb = p.sb("onesb", [128, 128], BF16)
    K.onesf = p.sb("onesf", [128, 128], F32)
    for t, k in ((K.identb, "identb"), (K.identf, "identf")):
        p.op("pool", lambda t=t: G.memset(t[:], 1.0), writes=[k])
        p.op("pool", lambda t=t: G.affine_select(out=t[:], in_=t[:], pattern=[[-1, 128]], compare_op=ALU.is_equal, fill=0.0, base=0, channel_multiplier=1), reads=[k], writes=[k])
    p.op("pool", lambda: G.memset(K.onesb[:], 1.0), writes=["onesb"])
    p.op("pool", lambda: G.memset(K.onesf[:], 1.0), writes=["onesf"])
    K.psb = [p.ps("psb%d" % i, [128, 512], F32) for i in range(7)]
    K.ps_i = 0
    K.pst = p.ps("pst", [128, 1024], BF16)
    K.cacc = [p.sb("cacc", [128, 512]) for _ in range(2)]
    K.cacc_i = 0
    K.modT = [p.sb("modT", [128, 48]) for _ in range(K.depth)]
    K.gscm = [p.sb("gscm", [128, 8]) for _ in range(K.depth)]
    K.gscf = [p.sb("gscf", [128, 8]) for _ in range(K.depth)]


def phase_mods(K):
    p, nc = K.p, K.nc
    with p.scope():
        cT = p.sb("cT", [128, 8])
        condT = p.sb("condT", [128, 8])
        p.dma("sp", [(cT[:], K.inp["cT"])], writes=["cT"])
        p.op("act", lambda: nc.scalar.activation(out=condT[:], in_=cT[:], func=AF.Silu), reads=["cT"], writes=["condT"])
        aw = [p.sb("aw", [128, 8, 512]) for _ in range(2)]
        adab = p.sb("adab", [128, 48])
        lng = p.sb("lng", [128, 8])
        for l in range(K.depth):
            pm, pk = ps_next(K)
            for fc in range(12):
                a = aw[fc % 2]
                ak = "aw%d" % (fc % 2)
                p.dma("sp", [(a[:], K.inp["ada_w"][l].rearrange("(kc p) f -> p kc f", p=128)[:, :, fc * 512:(fc + 1) * 512])], writes=[ak])
                for sub in range(4):
                    f = fc * 4 + sub

                    def emit(a=a, sub=sub, f=f):
                        for kc in range(8):
                            inst = nc.tensor.matmul(pm[:, f:f + 1], lhsT=a[:, kc, sub * 128:(sub + 1) * 128], rhs=condT[:, kc:kc + 1], start=(kc == 0), stop=(kc == 7))
                        return inst
                    p.op("pe", emit, reads=[ak, "condT"], writes=[pk])
            p.dma("sp", [(adab[:], K.inp["ada_bT"][l])], writes=["adab"])
            mk = "modT%d" % l
            p.op("dve", lambda l=l: nc.vector.tensor_tensor(out=K.modT[l][:], in0=pm[:, 0:48], in1=adab[:], op=ALU.add), reads=[pk, "adab"], writes=[mk])
            for (dst, dk, gname, sc0) in ((K.gscm[l], "gscm%d" % l, "lnmix", 8), (K.gscf[l], "gscf%d" % l, "lnffn", 32)):
                p.dma("sp", [(lng[:], K.inp[gname][l])], writes=["lng"])
                p.op("dve", lambda dst=dst, sc0=sc0, l=l: nc.vector.scalar_tensor_tensor(out=dst[:], in0=K.modT[l][:, sc0:sc0 + 8], scalar=1.0, in1=lng[:], op0=ALU.add, op1=ALU.mult),
                     reads=[mk, "lng"], writes=[dk])


def make_gt(K, l, g0, name):
    p, nc = K.p, K.nc
    mk = "modT%d" % l
    dst = p.sb(name, [128, 1024])
    diag = p.sb("diag", [128, 128])
    for half in range(2):
        ps, pk2 = ps_next(K)
        for jj in range(4):
            j = half * 4 + jj
            p.op("dve", lambda j=j: nc.vector.tensor_scalar(out=diag[:], in0=K.identf[:], scalar1=K.modT[l][:, g0 + j:g0 + j + 1], scalar2=None, op0=ALU.mult),
                 reads=[mk, "identf"], writes=["diag"])
            p.op("pe", lambda jj=jj, ps=ps: nc.tensor.matmul(ps[:, jj * 128:(jj + 1) * 128], lhsT=K.onesf[:], rhs=diag[:], start=True, stop=True),
                 reads=["diag", "onesf"], writes=[pk2])
        p.op("act", lambda half=half, ps=ps: nc.scalar.copy(out=dst[:, half * 512:(half + 1) * 512], in_=ps[:]), reads=[pk2], writes=[name])
    return dst


def phase_norm(K, l, xsrc, gsc, gsck, sh0):
    p, nc = K.p, K.nc
    mk = "modT%d" % l
    with p.scope():
        xt = [p.sb("xt", [128, 1024]) for _ in range(2)]
        junk2 = [p.sb("junk", [128, 1024], BF16) for _ in range(2)]
        xs = [p.sb("xs", [128, 1024], BF16) for _ in range(2)]
        ss2 = [p.sb("ss", [128, 4]) for _ in range(2)]
        tmp2 = [p.sb("tmp", [128, 8, 128]) for _ in range(2)]
        hto = [p.sb("hto", [128, 8, TT], BF16) for _ in range(2)]
        for i in range(K.S // 128):
            x_ = xt[i % 2]
            ss = ss2[i % 2]; junk = junk2[i % 2]; tmp = tmp2[i % 2]
            ssk = "ss%d" % (i % 2); jk = "junk%d" % (i % 2); tk_ = "tmp%d" % (i % 2)
            xk = "xt%d" % (i % 2)
            if i == 0:
                p.dma("sp", [(x_[:], xsrc[0:128, :])], reads=["xsrc"], writes=[xk])
            if i + 1 < K.S // 128:
                p.dma("sp", [(xt[(i + 1) % 2][:], xsrc[(i + 1) * 128:(i + 2) * 128, :])], reads=["xsrc"], writes=["xt%d" % ((i + 1) % 2)])
            p.op("act", lambda x_=x_, junk=junk, ss=ss: nc.scalar.activation(out=junk[:], in_=x_[:], func=AF.Square, accum_out=ss[:, 0:1]), reads=[xk], writes=[jk, ssk])
            p.op("act", lambda ss=ss: nc.scalar.activation(out=ss[:, 1:2], in_=ss[:, 0:1], func=AF.Sqrt, scale=1.0 / D, bias=1e-6), reads=[ssk], writes=[ssk])
            p.op("dve", lambda ss=ss: nc.vector.reciprocal(out=ss[:, 2:3], in_=ss[:, 1:2]), reads=[ssk], writes=[ssk])
            xs_ = xs[i % 2]
            xsk = "xs%d" % (i % 2)
            p.op("act", lambda x_=x_, xs_=xs_, ss=ss: nc.scalar.activation(out=xs_[:], in_=x_[:], func=AF.Copy, scale=ss[:, 2:3]), reads=[xk, ssk], writes=[xsk])

            pstv = K.pst[:] if i % 2 == 0 else K.psb[6][:].bitcast(BF16)
            pstk = "pst" if i % 2 == 0 else "ps6"

            def emit(xs_=xs_, pstv=pstv):
                for j in range(8):
                    inst = nc.tensor.transpose(pstv[:, j * 128:(j + 1) * 128], xs_[:, j * 128:(j + 1) * 128], K.identb[:])
                return inst
            p.op("pe", emit, reads=[xsk, "identb"], writes=[pstk])
            ho = hto[(i // 4) % 2]
            hk = "hto%d" % ((i // 4) % 2)
            p.op("dve", lambda tmp=tmp, pstv=pstv: nc.vector.tensor_tensor(out=tmp[:], in0=pstv.rearrange("p (j t) -> p j t", j=8), in1=bc(gsc[:], [128, 8, 128], 2), op=ALU.mult),
                 reads=[pstk, gsck], writes=[tk_])
            p.op("pool", lambda ho=ho, i=i, tmp=tmp: nc.gpsimd.tensor_tensor(out=ho[:, :, (i % 4) * 128:(i % 4 + 1) * 128], in0=tmp[:], in1=bc(K.modT[l][:, sh0:sh0 + 8], [128, 8, 128], 2), op=ALU.add),
                 reads=[tk_, mk], writes=[hk])
            if i % 4 == 3:
                t = i // 4
                p.dma("sp", [(K.hT[:, :, t * TT:(t + 1) * TT].rearrange("kc p t -> p kc t"), ho[:])], reads=[hk], writes=["hT"])


def phase_lru(K, l):
    p, nc = K.p, K.nc
    A, V, G = nc.scalar, nc.vector, nc.gpsimd
    with p.scope():
        W = p.sb("W", [128, 8, 1024], BF16)
        load_w(K, W[:, :, 0:512], "W", K.inp["w_in"][l], C_XL, C_XL + 512)
        load_w(K, W[:, :, 512:1024], "W", K.inp["w_in"][l], C_GL, C_GL + 512)
        wbd = {}
        for nm in ("wr", "wi"):
            t = p.sb(nm, [128, 4, 128], BF16)
            p.op("pool", lambda t=t: G.memset(t[:], 0.0), writes=[nm])
            pairs = []
            for c in range(4):
                for j in range(2):
                    pairs.append((t[j * 64:(j + 1) * 64, c, j * 64:(j + 1) * 64], K.inp["lru_" + nm][l, 2 * c + j]))
            p.dma("pool", pairs, writes=[nm])
            wbd[nm] = t
        sm = p.sb("sm", [128, 3, 4])
        p.dma("sp", [(sm[:, 0, :], K.inp["lru_br"][l]), (sm[:, 1, :], K.inp["lru_bi"][l]), (sm[:, 2, :], K.inp["lru_lambda"][l])], writes=["sm"])
        cA = p.sb("cA", [128, 3, 4])
        p.op("act", lambda: A.activation(out=cA[:, 0, :], in_=sm[:, 2, :], func=AF.Exp, scale=-1.0), reads=["sm"], writes=["cA"])
        p.op("act", lambda: A.activation(out=cA[:, 0, :], in_=cA[:, 0, :], func=AF.Ln, bias=1.0), reads=["cA"], writes=["cA"])
        p.op("dve", lambda: V.tensor_scalar(out=cA[:, 1, :], in0=cA[:, 0, :], scalar1=-8.0, scalar2=None, op0=ALU.mult), reads=["cA"], writes=["cA"])
        p.op("dve", lambda: V.tensor_scalar(out=cA[:, 2, :], in0=cA[:, 0, :], scalar1=-16.0, scalar2=None, op0=ALU.mult), reads=["cA"], writes=["cA"])
        cw = p.sb("cw", [128, 4, 4])
        cb = p.sb("cb", [128, 4])
        p.dma("sp", [(cw[:], K.inp["conv_w4"][l, :, 18:22, :]), (cb[:], K.inp["conv_b"][l, :, 18:22])], writes=["cw"])
        Pb = [p.sb("Pb", [128, 515]) for _ in range(4)]
        for c in range(4):
            p.op("pool", lambda c=c: G.memset(Pb[c][:], 0.0), writes=["Pb%d" % c])
        carry = p.sb("carry", [128, 4])
        p.op("pool", lambda: G.memset(carry[:], 0.0), writes=["carry"])
        B4 = lambda n, dt=F32: p.sb(n, [128, 4, TT], dt)
        xl, xlb, r, ig, a, a2, bb, hh, gate = B4("xl"), B4("xlb", BF16), B4("r"), B4("ig"), B4("a"), B4("a2"), B4("bb"), B4("hh"), B4("gate")
        yT = [B4("yT", BF16) for _ in range(2)]
        K.hbuf = [p.sb("hbuf", [128, 8, TT], BF16) for _ in range(2)]
        for t in range(K.S // TT):
            hb, hk = load_hT(K, t)
            for c in range(4):
                ps, pk = proj(K, W, "W", c * 128, 128, hb, hk)
                conv_chunk(K, ps, pk, Pb[c], "Pb%d" % c, cw, cb, c, xl[:, c, :], "xl%d" % c, AF.Identity)
                p.op("pool", lambda c=c: G.tensor_copy(out=xlb[:, c, :], in_=xl[:, c, :]), reads=["xl%d" % c], writes=["xlb%d" % c])
            for (nm, dst, bi_) in (("wr", r, 0), ("wi", ig, 1)):
                for c in range(4):
                    ps, pk = ps_next(K)
                    p.op("pe", lambda c=c, ps=ps, nm=nm: nc.tensor.matmul(ps[:], lhsT=wbd[nm][:, c, :], rhs=xlb[:, c, :], start=True, stop=True), reads=[nm, "xlb%d" % c], writes=[pk])
                    p.op("act", lambda c=c, ps=ps, dst=dst, bi_=bi_: A.activation(out=dst[:, c, :], in_=ps[:], func=AF.Sigmoid, bias=sm[:, bi_, c:c + 1]), reads=[pk, "sm"], writes=["%s_%d" % (nm, c)])
            for c in range(4):
                p.op("act", lambda c=c: A.activation(out=a[:, c, :], in_=r[:, c, :], func=AF.Exp, scale=cA[:, 1, c:c + 1]), reads=["wr_%d" % c, "cA"], writes=["a%d" % c])
                p.op("act", lambda c=c: A.activation(out=a2[:, c, :], in_=r[:, c, :], func=AF.Exp, scale=cA[:, 2, c:c + 1]), reads=["wr_%d" % c, "cA"], writes=["a2%d" % c])
            for c in range(4):
                p.op("act", lambda c=c: A.activation(out=a2[:, c, :], in_=a2[:, c, :], func=AF.Relu, scale=-1.0, bias=1.0), reads=["a2%d" % c], writes=["a2%d" % c])
                p.op("act", lambda c=c: A.activation(out=a2[:, c, :], in_=a2[:, c, :], func=AF.Sqrt), reads=["a2%d" % c], writes=["a2%d" % c])
            for c in range(4):
                bk = "bb%d" % c
                p.op("dve", lambda c=c: V.tensor_tensor(out=bb[:, c, :], in0=ig[:, c, :], in1=xl[:, c, :], op=ALU.mult), reads=["wi_%d" % c, "xl%d" % c], writes=[bk])
                if t == 0:
                    p.op("dve", lambda c=c: V.tensor_tensor(out=bb[:, c, 1:TT], in0=bb[:, c, 1:TT], in1=a2[:, c, 1:TT], op=ALU.mult), reads=[bk, "a2%d" % c], writes=[bk])
                else:
                    p.op("dve", lambda c=c: V.tensor_tensor(out=bb[:, c, :], in0=bb[:, c, :], in1=a2[:, c, :], op=ALU.mult), reads=[bk, "a2%d" % c], writes=[bk])
                p.op("dve", lambda c=c: V.tensor_tensor_scan(out=hh[:, c, :], data0=a[:, c, :], data1=bb[:, c, :], initial=carry[:, c:c + 1], op0=ALU.mult, op1=ALU.add),
                     reads=["a%d" % c, bk, "carry"], writes=["hh%d" % c])
                p.op("act", lambda c=c: A.copy(out=carry[:, c:c + 1], in_=hh[:, c, TT - 1:TT]), reads=["hh%d" % c], writes=["carry"])
            y = yT[t % 2]
            yk = "yT%d" % (t % 2)
            for c in range(4):
                ps, pk = proj(K, W, "W", 512 + c * 128, 128, hb, hk)
                p.op("act", lambda c=c, ps=ps: A.activation(out=gate[:, c, :], in_=ps[:], func=AF.Gelu_apprx_tanh), reads=[pk], writes=["gate%d" % c])
                p.op("dve", lambda c=c, y=y: V.tensor_tensor(out=y[:, c, :], in0=hh[:, c, :], in1=gate[:, c, :], op=ALU.mult), reads=["hh%d" % c, "gate%d" % c], writes=[yk])
            p.dma("sp", [(K.Y[YCH["d"]:YCH["d"] + 4, :, t * TT:(t + 1) * TT].rearrange("c p t -> p c t"), y[:])], reads=[yk], writes=["Y"])


def phase_zero_branch(K, br, n):
    p, nc = K.p, K.nc
    with p.scope():
        z = p.sb("z", [128, n, TT], BF16)
        p.op("pool", lambda: nc.gpsimd.memset(z[:], 0.0), writes=["z"])
        for t in range(K.S // TT):
            p.dma("sp", [(K.Y[YCH[br]:YCH[br] + n, :, t * TT:(t + 1) * TT].rearrange("c p t -> p c t"), z[:])], reads=["z"], writes=["Y"])


def phase_merge(K, l, xsrc, xdst):
    p, nc = K.p, K.nc
    A, V, G = nc.scalar, nc.vector, nc.gpsimd
    TT = 512
    with p.scope():
        gtm = make_gt(K, l, 16, "gtm")
        Wg = p.sb("Wg", [128, 8, 4096], BF16)
        for m in range(4):
            load_w(K, Wg[:, :, m * 1024:(m + 1) * 1024], "Wg%d" % m, K.inp["w_in"][l], C_GM + m * 1024, C_GM + (m + 1) * 1024)
        Wb = p.sb("Wb", [128, 15, 1024], BF16)
        p.dma("pool", [(Wb[:], K.inp["w_branch"][l].rearrange("(c p) d -> p c d", p=128))], writes=["Wb"])
        Wo = p.sb("Wo", [128, 8, 1024], BF16)
        p.dma("pool", [(Wo[:], K.inp["w_out"][l].rearrange("(c p) d -> p c d", p=128))], writes=["Wo"])
        K.hbuf = [p.sb("hbuf", [128, 8, TT], BF16) for _ in range(2)]
        ybuf = [p.sb("ybuf", [128, 15, TT], BF16) for _ in range(2)]
        mg = p.sb("mg", [128, 8, TT], BF16)
        sg = [p.sb("sg", [128, TT]) for _ in range(2)]
        acc = p.sb("acc", [128, TT])
        xt = [p.sb("xt", [128, 1024]) for _ in range(2)]
        tm = p.sb("tm", [128, 512])
        brs = ((0, 0, 3), (1, 3, 4), (2, 7, 4), (3, 11, 4))
        for t in range(K.S // TT):
            hb, hk = load_hT(K, t, TT)
            yb = ybuf[t % 2]
            ybk = "ybuf%d" % (t % 2)
            p.dma("sp", [(yb[:], K.Y[:, :, t * TT:(t + 1) * TT].rearrange("c p t -> p c t"))], reads=["Y"], writes=[ybk])
            for j in range(8):
                for (m, c0, ncn) in brs:
                    psg, pkg = proj(K, Wg, "Wg%d" % m, m * 1024 + j * 128, 128, hb, hk, N=TT)
                    s_ = sg[m % 2]
                    sk = "sg%d" % (m % 2)
                    p.op("act", lambda psg=psg, s_=s_: A.activation(out=s_[:], in_=psg[:, 0:TT], func=AF.Sigmoid), reads=[pkg], writes=[sk])
                    psy, pky = ps_next(K)

                    def emit(psy=psy, c0=c0, ncn=ncn, j=j, yb=yb):
                        for cc in range(ncn):
                            inst = nc.tensor.matmul(psy[:, 0:TT], lhsT=Wb[:, c0 + cc, j * 128:(j + 1) * 128], rhs=yb[:, c0 + cc, :], start=(cc == 0), stop=(cc == ncn - 1))
                        return inst
                    p.op("pe", emit, reads=["Wb", ybk], writes=[pky])
                    if m == 0:
                        p.op("dve", lambda psy=psy, s_=s_: V.tensor_tensor(out=acc[:], in0=psy[:, 0:TT], in1=s_[:], op=ALU.mult), reads=[pky, sk], writes=["acc"])
                    else:
                        p.op("dve", lambda psy=psy, s_=s_: V.tensor_tensor(out=s_[:], in0=psy[:, 0:TT], in1=s_[:], op=ALU.mult), reads=[pky, sk], writes=[sk])
                        if m < 3:
                            p.op("pool", lambda s_=s_: G.tensor_tensor(out=acc[:], in0=acc[:], in1=s_[:], op=ALU.add), reads=["acc", sk], writes=["acc"])
                        else:
                            p.op("pool", lambda s_=s_, j=j: G.tensor_tensor(out=mg[:, j, :], in0=acc[:], in1=s_[:], op=ALU.add), reads=["acc", sk], writes=["mg%d" % j])
            for s4 in range(TT // 128):
                i = t * (TT // 128) + s4
                x_ = xt[i % 2]
                xk = "xt%d" % (i % 2)
                p.dma("sp", [(x_[:], xsrc[i * 128:(i + 1) * 128, :])], reads=["xsrc"], writes=[xk])
                for half in range(2):
                    ps, pk = ps_next(K)

                    def emit(ps=ps, s4=s4, half=half):
                        for kc in range(8):
                            inst = nc.tensor.matmul(ps[:], lhsT=mg[:, kc, s4 * 128:(s4 + 1) * 128], rhs=Wo[:, kc, half * 512:(half + 1) * 512], start=(kc == 0), stop=(kc == 7))
                        return inst
                    p.op("pe", emit, reads=["Wo"] + ["mg%d" % j for j in range(8)], writes=[pk])
                    p.op("dve", lambda ps=ps, half=half: V.tensor_tensor(out=tm[:], in0=ps[:], in1=gtm[:, half * 512:(half + 1) * 512], op=ALU.mult), reads=[pk, "gtm"], writes=["tm"])
                    p.op("pool", lambda x_=x_, half=half: G.tensor_tensor(out=x_[:, half * 512:(half + 1) * 512], in0=x_[:, half * 512:(half + 1) * 512], in1=tm[:], op=ALU.add), reads=["tm", xk], writes=[xk])
                p.dma("sp", [(xdst[i * 128:(i + 1) * 128, :], x_[:])], reads=[xk], writes=["xdst"])


def phase_ffn(K, l, xsrc, xdst, final):
    p, nc = K.p, K.nc
    A, V, G = nc.scalar, nc.vector, nc.gpsimd
    TF = min(2048, K.S)
    NS = TF // TT
    with p.scope():
        K.hbuf = None
        gtf = make_gt(K, l, 40, "gtf")
        hb = p.sb("hbF", [128, 8, TF], BF16)
        actT = p.sb("actT", [128, 22, TF], BF16)
        w13 = [p.sb("w13", [128, 8, 256], BF16) for _ in range(3)]
        w2 = [p.sb("w2", [128, 22, 512], BF16) for _ in range(2)]
        sa = [p.sb("sa", [128, TT]) for _ in range(2)]
        xt = [p.sb("xt", [128, 1024]) for _ in range(2)]
        tm = p.sb("tm", [128, 512])
        junk = p.sb("junk", [128, 1024], BF16)
        ss = p.sb("ss", [128, 4])
        lnf = p.sb("lnf", [128, 1024])
        if final:
            p.dma("sp", [(lnf[:], K.inp["lnfin"])], writes=["lnf"])
        w13v = K.inp["ffn_w13"][l].rearrange("(kc p) c -> p kc c", p=128)
        w2v = K.inp["ffn_w2"][l].rearrange("(c p) d -> p c d", p=128)
        wi = 0
        for tf in range(K.S // TF):
            p.dma("sp", [(hb[:], K.hT[:, :, tf * TF:(tf + 1) * TF].rearrange("kc p t -> p kc t"))], reads=["hT"], writes=["hbF"])
            for hc in range(22):
                wa = w13[wi % 3]
                wk = "w13_%d" % (wi % 3)
                wi += 1
                p.dma("pool", [(wa[:, :, 0:128], w13v[:, :, hc * 128:(hc + 1) * 128]), (wa[:, :, 128:256], w13v[:, :, 2816 + hc * 128:2816 + (hc + 1) * 128])], writes=[wk])
                for s in range(NS):
                    psa, pka = ps_next(K)
                    psb_, pkb = ps_next(K)
                    for (ps, pk, off) in ((psa, pka, 0), (psb_, pkb, 128)):
                        def emit(ps=ps, off=off, wa=wa, s=s):
                            for kc in range(8):
                                inst = nc.tensor.matmul(ps[:], lhsT=wa[:, kc, off:off + 128], rhs=hb[:, kc, s * TT:(s + 1) * TT], start=(kc == 0), stop=(kc == 7))
                            return inst
                        p.op("pe", emit, reads=[wk, "hbF"], writes=[pk])
                    s_ = sa[s % 2]
                    sk = "sa%d" % (s % 2)
                    p.op("act", lambda psa=psa, s_=s_: A.activation(out=s_[:], in_=psa[:], func=AF.Silu), reads=[pka], writes=[sk])
                    p.op("dve", lambda psb_=psb_, s_=s_, hc=hc, s=s: V.tensor_tensor(out=actT[:, hc, s * TT:(s + 1) * TT], in0=psb_[:], in1=s_[:], op=ALU.mult), reads=[pkb, sk], writes=["actT%d" % hc])
            for half in range(2):
                p.dma("pool", [(w2[half][:], w2v[:, :, half * 512:(half + 1) * 512])], writes=["w2_%d" % half])
            for s4 in range(TF // 128):
                i = tf * (TF // 128) + s4
                x_ = xt[i % 2]
                xk = "xt%d" % (i % 2)
                p.dma("sp", [(x_[:], xsrc[i * 128:(i + 1) * 128, :])], reads=["xsrc"], writes=[xk])
                for half in range(2):
                    ps, pk = ps_next(K)

                    def emit(ps=ps, s4=s4, half=half):
                        for hc in range(22):
                            inst = nc.tensor.matmul(ps[:], lhsT=actT[:, hc, s4 * 128:(s4 + 1) * 128], rhs=w2[half][:, hc, :], start=(hc == 0), stop=(hc == 21))
                        return inst
                    p.op("pe", emit, reads=["w2_%d" % half] + ["actT%d" % hc for hc in range(22)], writes=[pk])
                    p.op("dve", lambda ps=ps, half=half: V.tensor_tensor(out=tm[:], in0=ps[:], in1=gtf[:, half * 512:(half + 1) * 512], op=ALU.mult), reads=[pk, "gtf"], writes=["tm"])
                    p.op("pool", lambda x_=x_, half=half: G.tensor_tensor(out=x_[:, half * 512:(half + 1) * 512], in0=x_[:, half * 512:(half + 1) * 512], in1=tm[:], op=ALU.add), reads=["tm", xk], writes=[xk])
                if final:
                    p.op("act", lambda x_=x_: A.activation(out=junk[:], in_=x_[:], func=AF.Square, accum_out=ss[:, 0:1]), reads=[xk], writes=["junk", "ss"])
                    p.op("act", lambda: A.activation(out=ss[:, 1:2], in_=ss[:, 0:1], func=AF.Sqrt, scale=1.0 / D, bias=1e-6), reads=["ss"], writes=["ss"])
                    p.op("dve", lambda: V.reciprocal(out=ss[:, 2:3], in_=ss[:, 1:2]), reads=["ss"], writes=["ss"])
                    p.op("dve", lambda x_=x_: V.scalar_tensor_tensor(out=x_[:], in0=x_[:], scalar=ss[:, 2:3], in1=lnf[:], op0=ALU.mult, op1=ALU.mult), reads=[xk, "ss", "lnf"], writes=[xk])
                p.dma("sp", [(xdst[i * 128:(i + 1) * 128, :], x_[:])], reads=[xk], writes=["xdst"])


INPUT_SHAPES = None


def input_shapes(S, L):
    return {
        "x": ([S, D], F32), "cT": ([128, 8], F32),
        "lnmix": ([L, 128, 8], F32), "lnffn": ([L, 128, 8], F32), "lnfin": ([128, D], F32),
        "ada_w": ([L, D, 6 * D], F32), "ada_bT": ([L, 128, 48], F32),
        "w_in": ([L, D, NIN], F32), "conv_w4": ([L, 128, 22, 4], F32), "conv_b": ([L, 128, 22], F32),
        "lru_wr": ([L, 8, 64, 64], F32), "lru_wi": ([L, 8, 64, 64], F32),
        "lru_br": ([L, 128, 4], F32), "lru_bi": ([L, 128, 4], F32), "lru_lambda": ([L, 128, 4], F32),
        "s5_lamr": ([L, 128, 24], F32), "s5_lami": ([L, 128, 24], F32), "s5_ldt": ([L, 128, 24], F32),
        "s5_br": ([L, 128, 24, 16], F32), "s5_bi": ([L, 128, 24, 16], F32), "s5_cr": ([L, 128, 24, 16], F32), "s5_ci": ([L, 128, 24, 16], F32),
        "s5_dB": ([L, 128, 384], F32), "s5_glub": ([L, 128, 3], F32), "s5_glu_w": ([L, 384, 384], F32),
        "gdn_dt_bias": ([L, 4, 1], F32), "gdn_a_log": ([L, 4, 1], F32), "gdn_norm_w": ([L, 128, 1], F32),
        "ssd_dt_bias": ([L, 8, 1], F32), "ssd_a_log": ([L, 8, 1], F32), "ssd_dB": ([L, 64, 512], F32), "ssd_nwB": ([L, 64, 512], F32),
        "w_branch": ([L, 1920, D], F32), "w_out": ([L, D, D], F32),
        "ffn_w13": ([L, D, 5632], F32), "ffn_w2": ([L, 2816, D], F32),
    }


def prep_inputs(inputs, b, S, L):
    f = lambda a: np.ascontiguousarray(np.asarray(a, dtype=np.float32))
    inputs = {k: (np.asarray(v)[:L] if k not in ('x', 'c', 'ln_final_g') else np.asarray(v)) for k, v in inputs.items()}
    fm = lambda v, n: f(v.reshape(L, n, 128).transpose(0, 2, 1))
    m = {}
    m["x"] = f(inputs["x"][b, :S])
    m["cT"] = f(inputs["c"][b].reshape(8, 128).T)
    m["lnmix"] = fm(inputs["ln_mix_g"], 8)
    m["lnffn"] = fm(inputs["ln_ffn_g"], 8)
    m["lnfin"] = f(np.broadcast_to(inputs["ln_final_g"][None, :], (128, D)))
    m["ada_w"] = f(inputs["ada_w"])
    m["ada_bT"] = fm(inputs["ada_b"], 48)
    m["w_in"] = f(inputs["w_in"])
    m["conv_w4"] = f(inputs["conv_w"].reshape(L, 4, 22, 128).transpose(0, 3, 2, 1))
    m["conv_b"] = fm(inputs["conv_b"], 22)
    m["lru_wr"] = f(inputs["lru_wr"])
    m["lru_wi"] = f(inputs["lru_wi"])
    m["lru_br"] = fm(inputs["lru_br"], 4)
    m["lru_bi"] = fm(inputs["lru_bi"], 4)
    m["lru_lambda"] = fm(inputs["lru_lambda"], 4)
    two = lambda a: np.concatenate([a, a], axis=1)
    m["s5_lamr"] = f(two(inputs["s5_lambda_re"].transpose(0, 2, 1)))
    m["s5_lami"] = f(two(inputs["s5_lambda_im"].transpose(0, 2, 1)))
    m["s5_ldt"] = f(np.broadcast_to(inputs["s5_log_dt"][:, None, :], (L, 128, 24)))
    m["s5_br"] = f(two(inputs["s5_b_re"].transpose(0, 2, 1, 3)))
    m["s5_bi"] = f(two(inputs["s5_b_im"].transpose(0, 2, 1, 3)))
    m["s5_cr"] = f(two(inputs["s5_c_re"].transpose(0, 3, 1, 2)))
    m["s5_ci"] = f(two(inputs["s5_c_im"].transpose(0, 3, 1, 2)))
    m["s5_dB"] = f(np.broadcast_to(inputs["s5_d"][:, None, :], (L, 128, 384)))
    m["s5_glub"] = fm(inputs["s5_glu_b"], 3)
    m["s5_glu_w"] = f(inputs["s5_glu_w"])
    m["gdn_dt_bias"] = f(inputs["gdn_dt_bias"].reshape(L, 4, 1))
    m["gdn_a_log"] = f(inputs["gdn_a_log"].reshape(L, 4, 1))
    m["gdn_norm_w"] = f(inputs["gdn_norm_w"].reshape(L, 128, 1))
    m["ssd_dt_bias"] = f(inputs["ssd_dt_bias"].reshape(L, 8, 1))
    m["ssd_a_log"] = f(inputs["ssd_a_log"].reshape(L, 8, 1))
    m["ssd_dB"] = f(np.broadcast_to(np.repeat(inputs["ssd_d"], 64, axis=1)[:, None, :], (L, 64, 512)))
    m["ssd_nwB"] = f(np.broadcast_to(inputs["ssd_norm_w"][:, None, :], (L, 64, 512)))
    m["w_branch"] = f(inputs["w_branch"])
    m["w_out"] = f(inputs["w_out"])
    m["ffn_w13"] = f(inputs["ffn_w13"])
    m["ffn_w2"] = f(inputs["ffn_w2"])
    return m


def build(S=4096, L=2, dbg=False, branches="abcd"):
    nc = bass.Bass("TRN2", target_bir_lowering=False)
    K = Ctx()
    K.nc, K.S, K.depth = nc, S, L
    import os as _os
    K.stop = int(_os.environ['SSD_STOP']) if 'SSD_STOP' in _os.environ else None
    K.gdn_c = int(_os.environ.get('GDN_C', '128'))
    K.gdn_ng = int(_os.environ.get('GDN_NG', '4' if K.gdn_c == 64 else '2'))
    K.inp = {n: nc.dram_tensor(n, sh, dt, kind="ExternalInput").ap() for n, (sh, dt) in input_shapes(S, L).items()}
    K.out = nc.dram_tensor("out", [S, D], F32, kind="ExternalOutput").ap()
    kind = "ExternalOutput" if dbg else "Internal"
    K.hT = nc.dram_tensor("hT", [8, 128, S], BF16, kind=kind).ap()
    K.Y = nc.dram_tensor("Y", [15, 128, S], BF16, kind=kind).ap()
    K.X1 = nc.dram_tensor("X1", [S, D], F32, kind=kind).ap()
    K.Us = nc.dram_tensor("Us", [S, 384], BF16, kind=kind).ap()
    K.p = Prog(nc)
    p = K.p
    phase_setup(K)
    phase_mods(K)
    make_masks(K)
    for l in range(L):
        xsrc = K.inp["x"] if l == 0 else K.X1
        phase_norm(K, l, xsrc, K.gscm[l], "gscm%d" % l, 0)
        for br, n in (("a", 3), ("b", 4), ("c", 4)):
            if br not in branches:
                phase_zero_branch(K, br, n)
        if "a" in branches:
            phase_s5(K, l)
        if "b" in branches:
            phase_gdn(K, l)
        if "c" in branches:
            phase_ssd(K, l)
        if "d" in branches:
            phase_lru(K, l)
        else:
            phase_zero_branch(K, "d", 4)
        phase_merge(K, l, xsrc, K.X1)
        phase_norm(K, l, K.X1, K.gscf[l], "gscf%d" % l, 24)
        last = (l == L - 1)
        phase_ffn(K, l, K.X1, K.out if last else K.X1, last)
    p.barrier()
    p.es.close()
    return nc


def kernel(**inputs):
    S, L = 4096, 2
    nc = build(S, L)
    in_maps = [prep_inputs(inputs, b, S, L) for b in range(8)]
    res = run_bass_kernel_spmd(nc, in_maps, core_ids=list(range(8)))
    return np.stack([np.asarray(r["out"]) for r in res.results], axis=0).astype(np.float32)


def make_masks(K):
    p, nc = K.p, K.nc
    G = nc.gpsimd
    K.rm = p.sb("rm", [128, 2])
    p.op("pool", lambda: G.memset(K.rm[:], 0.0), writes=["rm"])
    p.op("pool", lambda: G.memset(K.rm[0:64, 0:1], 1.0), reads=["rm"], writes=["rm"])
    p.op("pool", lambda: G.memset(K.rm[64:128, 1:2], 1.0), reads=["rm"], writes=["rm"])


def make_ssd_masks(K):
    p, nc = K.p, K.nc
    G = nc.gpsimd
    K.IND8 = p.sb("IND8", [8, 8, 64])
    p.op("pool", lambda: G.memset(K.IND8[:], 1.0), writes=["IND8"])
    p.op("pool", lambda: G.affine_select(out=K.IND8[:], in_=K.IND8[:], pattern=[[-1, 8], [0, 64]], compare_op=ALU.is_equal, fill=0.0, base=0, channel_multiplier=1), reads=["IND8"], writes=["IND8"])
    K.M_le = p.sb("M_le", [64, 8, 64], BF16)
    p.op("pool", lambda: G.memset(K.M_le[:], 0.0), writes=["M_le"])
    p.op("pool", lambda: G.affine_select(out=K.M_le[:], in_=K.M_le[:], pattern=[[0, 8], [1, 64]], compare_op=ALU.is_ge, fill=NEG, base=0, channel_multiplier=-1), reads=["M_le"], writes=["M_le"])
    K.rmask = p.sb("rmask", [8, 8, 64])
    p.op("pool", lambda: G.memset(K.rmask[:], 1.0), writes=["rmask"])
    p.op("pool", lambda: G.memset(K.rmask[:, :, 0:1], 0.0), reads=["rmask"], writes=["rmask"])


def decay_exp(K, lhs_pos, lhs_neg_bd, lhs_neg, rhs_bd_neg, mask, mkey, nh, out_ap, okey, reads):
    raise NotImplementedError


def phase_ssd(K, l):
    p, nc = K.p, K.nc
    A, V, G, T = nc.scalar, nc.vector, nc.gpsimd, nc.tensor
    with p.scope():
        make_ssd_masks(K)
        NW = 1288
        W = p.sb("W", [128, 8, NW], BF16)
        wv = K.inp["w_in"][l]
        load_w(K, W[:, :, 0:768], "W", wv, C_XS, C_XS + 768)
        load_w(K, W[:, :, 768:1280], "W", wv, C_ZS, C_ZS + 512)
        load_w(K, W[:, :, 1280:1288], "W", wv, C_DT, C_DT + 8)
        cw = p.sb("cw", [128, 6, 4]); cb = p.sb("cb", [128, 6])
        sm8 = p.sb("sm8", [8, 4])
        dB = p.sb("dB", [64, 512]); nwB = p.sb("nwB", [64, 512])
        p.dma("sp", [(cw[:], K.inp["conv_w4"][l, :, 12:18, :]), (cb[:], K.inp["conv_b"][l, :, 12:18]),
                     (sm8[:, 0:1], K.inp["ssd_dt_bias"][l]), (sm8[:, 1:2], K.inp["ssd_a_log"][l]),
                     (dB[:], K.inp["ssd_dB"][l]), (nwB[:], K.inp["ssd_nwB"][l])], writes=["cw"])
        p.op("act", lambda: A.activation(out=sm8[:, 2:3], in_=sm8[:, 1:2], func=AF.Exp), reads=["cw"], writes=["sm8"])
        p.op("dve", lambda: V.tensor_scalar(out=sm8[:, 2:3], in0=sm8[:, 2:3], scalar1=-1.0, scalar2=None, op0=ALU.mult), reads=["sm8"], writes=["sm8"])
        Pbw = [p.sb("Pbw", [128, 515]) for _ in range(2)]
        tails = p.sb("tails", [128, 6, 3])
        p.op("pool", lambda: G.memset(tails[:], 0.0), writes=["tails"])
        import os as _os
        NG = int(_os.environ.get("SSD_NG", "4"))
        St = p.sb("St", [128, 4, 64]); Stb = [p.sb("Stb", [128, 4, 64], BF16) for _ in range(8)]; tmpS = p.sb("tmpS", [128, 4, 64])
        p.op("pool", lambda: G.memset(St[:], 0.0), writes=["St"])
        p.op("pool", lambda: G.memset(Stb[0][:], 0.0), writes=["Stb0"])
        K.hbuf = [p.sb("hbuf", [128, 8, TT], BF16) for _ in range(2)]
        xsT = p.sb("xsT", [128, 4, TT], BF16); BT = p.sb("BT", [128, TT], BF16); CT = p.sb("CT", [128, TT], BF16)
        BTz = [p.sb("BTz", [128, TT], BF16) for _ in range(2)]; CTz = [p.sb("CTz", [128, TT], BF16) for _ in range(2)]
        dtT = p.sb("dtT", [8, TT]); csT = p.sb("csT", [8, TT]); ncsT = p.sb("ncsT", [8, TT]); erevT = p.sb("erevT", [8, TT]); laT = p.sb("laT", [8, TT])
        csbd = [p.sb("csbd", [8, 8, 64]) for _ in range(NG)]
        LT = [p.sb("LT", [64, 512], BF16) for _ in range(NG)]
        ECS = [p.sb("ECS", [128, 512]) for _ in range(NG)]
        tok = [p.sb("tok", [64, 24]) for _ in range(NG)]
        MT = [p.sb("MT", [64, 512], BF16) for _ in range(NG)]
        Ssb = [p.sb("Ssb", [64, 128]) for _ in range(NG)]
        xtok = [p.sb("xtok", [64, 512], BF16) for _ in range(NG)]
        xdt = [p.sb("xdt", [64, 512], BF16) for _ in range(NG)]
        xdd = [p.sb("xdd", [64, 512], BF16) for _ in range(NG)]
        Cdec = [p.sb("Cdec", [128, 2, 256], BF16) for _ in range(NG)]
        sz = [p.sb("sz", [64, 512]) for _ in range(NG)]
        y1 = [p.sb("y1", [64, 512]) for _ in range(NG)]
        y3 = [p.sb("y3", [64, 512], BF16) for _ in range(NG)]
        ssq = [p.sb("ssq", [64, 4]) for _ in range(NG)]; junk = [p.sb("junk", [64, 512], BF16) for _ in range(NG)]
        Btok = [p.sb("Btok", [64, 128], BF16) for _ in range(NG)]
        yT = [p.sb("yT", [128, 4, TT], BF16) for _ in range(2)]
        for t in range(K.S // TT):
            hb, hk = load_hT(K, t)
            for c in range(6):
                ps, pk = proj(K, W, "W", c * 128, 128, hb, hk)
                dst = xsT[:, c, :] if c < 4 else (BT[:] if c == 4 else CT[:])
                conv_chunk(K, ps, pk, Pbw[c % 2], "Pbw%d" % (c % 2), cw, cb, c, dst, "cv%d" % c, AF.Silu, tails=tails, tkey="tails")
            for g in range(2):
                p.op("pool", lambda g=g: G.tensor_scalar(out=BTz[g][:], in0=BT[:], scalar1=K.rm[:, g:g + 1], scalar2=None, op0=ALU.mult), reads=["cv4", "rm"], writes=["BTz%d" % g])
                p.op("pool", lambda g=g: G.tensor_scalar(out=CTz[g][:], in0=CT[:], scalar1=K.rm[:, g:g + 1], scalar2=None, op0=ALU.mult), reads=["cv5", "rm"], writes=["CTz%d" % g])
            ps, pk = proj(K, W, "W", 1280, 8, hb, hk)
            p.op("act", lambda ps=ps: A.activation(out=dtT[:], in_=ps[0:8, :], func=AF.Exp, bias=sm8[:, 0:1]), reads=[pk, "cw"], writes=["dtT"])
            p.op("act", lambda: A.activation(out=dtT[:], in_=dtT[:], func=AF.Ln, bias=1.0), reads=["dtT"], writes=["dtT"])
            p.op("dve", lambda: V.tensor_scalar(out=laT[:], in0=dtT[:], scalar1=sm8[:, 2:3], scalar2=None, op0=ALU.mult), reads=["dtT", "sm8"], writes=["laT"])
            p.op("dve", lambda: V.tensor_tensor_scan(out=csT[:], data0=K.rmask[:].rearrange("h c l -> h (c l)"), data1=laT[:], initial=0.0, op0=ALU.mult, op1=ALU.add), reads=["laT", "rmask"], writes=["csT"])
            p.op("dve", lambda: V.tensor_scalar(out=ncsT[:], in0=csT[:], scalar1=-1.0, scalar2=None, op0=ALU.mult), reads=["csT"], writes=["ncsT"])
            cs3 = csT[:].rearrange("h (c l) -> h c l", c=8)
            p.op("dve", lambda: V.tensor_tensor(out=erevT[:].rearrange("h (c l) -> h c l", c=8), in0=cs3[:, :, 63:64].to_broadcast([8, 8, 64]), in1=cs3, op=ALU.subtract), reads=["csT"], writes=["erevT"])
            p.op("act", lambda: A.activation(out=erevT[:], in_=erevT[:], func=AF.Exp), reads=["erevT"], writes=["erevT"])
            y = yT[t % 2]; yk = "yT%d" % (t % 2)
            def chunk_gen(cx, b2, cglob):
                sl = slice(cx * 64, (cx + 1) * 64)
                k_ = lambda n: "%s%d" % (n, b2)
                Sin, sink = Stb[cglob % 8], "Stb%d" % (cglob % 8)
                Sout, soutk = Stb[(cglob + 1) % 8], "Stb%d" % ((cglob + 1) % 8)
                p.op("pool", lambda: G.tensor_tensor(out=csbd[b2][:], in0=csT[:, sl].unsqueeze(1).to_broadcast([8, 8, 64]), in1=K.IND8[:], op=ALU.mult), reads=["csT", "IND8"], writes=[k_("csbd")])
                yield
                if K.stop is not None and K.stop <= 0:
                    return
                Tp, tk = ps_next(K)

                def emit_t(Tp=Tp, sl=sl):
                    T.matmul(Tp[0:64, 0:8], lhsT=dtT[:, sl], rhs=K.identf[0:8, 0:8], start=True, stop=True)
                    T.matmul(Tp[0:64, 8:16], lhsT=erevT[:, sl], rhs=K.identf[0:8, 0:8], start=True, stop=True)
                    return T.matmul(Tp[0:64, 16:24], lhsT=ncsT[:, sl], rhs=K.identf[0:8, 0:8], start=True, stop=True)
                p.op("pe", emit_t, reads=["dtT", "erevT", "ncsT", "identf"], writes=[tk])
                p.op("act", lambda Tp=Tp, b2=b2: A.copy(out=tok[b2][:], in_=Tp[0:64, 0:24]), reads=[tk], writes=[k_("tok")])
                Dp, dk = ps_next(K)

                def emit_d(Dp=Dp, b2=b2, sl=sl):
                    T.matmul(Dp[0:64, :], lhsT=K.onesf[0:8, 0:64], rhs=csbd[b2][:].rearrange("k h l -> k (h l)"), start=True, stop=False)
                    return T.matmul(Dp[0:64, :], lhsT=K.identb[0:64, 0:64], rhs=K.M_le[:].rearrange("k h l -> k (h l)"), start=False, stop=True)
                p.op("pe", emit_d, reads=[k_("csbd"), "M_le", "onesf", "identb"], writes=[dk])
                for h in range(8):
                    p.op("act", lambda Dp=Dp, b2=b2, h=h: A.activation(out=LT[b2][:, h * 64:(h + 1) * 64], in_=Dp[0:64, h * 64:(h + 1) * 64], func=AF.Exp, bias=tok[b2][:, 16 + h:17 + h]),
                         reads=[dk, k_("tok")], writes=[k_("LT")])
                Ep, ek = ps_next(K)
                p.op("pe", lambda Ep=Ep, b2=b2: T.matmul(Ep[:], lhsT=K.onesf[0:8, :], rhs=csbd[b2][:].rearrange("k h l -> k (h l)"), start=True, stop=True), reads=[k_("csbd"), "onesf"], writes=[ek])
                p.op("act", lambda Ep=Ep, b2=b2: A.activation(out=ECS[b2][:], in_=Ep[:], func=AF.Exp), reads=[ek], writes=[k_("ECS")])
                yield
                if K.stop is not None and K.stop <= 1:
                    return
                Sp, sk = ps_next(K)

                def emit_s(Sp=Sp, sl=sl):
                    T.matmul(Sp[0:64, 0:64], lhsT=BTz[0][:, sl], rhs=CT[:, sl], start=True, stop=True)
                    return T.matmul(Sp[0:64, 64:128], lhsT=BTz[1][:, sl], rhs=CT[:, sl], start=True, stop=True)
                p.op("pe", emit_s, reads=["BTz0", "BTz1", "cv5"], writes=[sk])
                p.op("act", lambda Sp=Sp, b2=b2: A.copy(out=Ssb[b2][:], in_=Sp[0:64, 0:128]), reads=[sk], writes=[k_("Ssb")])
                for g in range(2):
                    p.op("dve", lambda b2=b2, g=g: V.tensor_tensor(out=MT[b2][:, g * 256:(g + 1) * 256].rearrange("m (j l) -> m j l", j=4),
                                                              in0=LT[b2][:, g * 256:(g + 1) * 256].rearrange("m (j l) -> m j l", j=4),
                                                              in1=Ssb[b2][:, g * 64:(g + 1) * 64].unsqueeze(1).to_broadcast([64, 4, 64]), op=ALU.mult), reads=[k_("Ssb"), k_("LT")], writes=[k_("MT")])
                yield
                if K.stop is not None and K.stop <= 2:
                    return
                Xp_, xpk = ps_next(K)

                def emit_x(sl=sl, Xp_=Xp_):
                    for c4 in range(4):
                        inst = T.matmul(Xp_[0:64, c4 * 128:(c4 + 1) * 128], lhsT=xsT[:, c4, sl], rhs=K.identb[:], start=True, stop=True)
                    return inst
                p.op("pe", emit_x, reads=["cv0", "cv1", "cv2", "cv3", "identb"], writes=[xpk])
                p.op("act", lambda b2=b2, Xp_=Xp_: A.copy(out=xtok[b2][:], in_=Xp_[0:64, 0:512]), reads=[xpk], writes=[k_("xtok")])
                p.op("dve", lambda b2=b2, Xp_=Xp_: V.tensor_tensor(out=xdt[b2][:].rearrange("m (h q) -> m h q", h=8), in0=Xp_[0:64, 0:512].rearrange("m (h q) -> m h q", h=8),
                                                          in1=tok[b2][:, 0:8].unsqueeze(2).to_broadcast([64, 8, 64]), op=ALU.mult), reads=[xpk, k_("tok")], writes=[k_("xdt")])
                p.op("pool", lambda b2=b2: G.tensor_tensor(out=xdd[b2][:].rearrange("m (h q) -> m h q", h=8), in0=xdt[b2][:].rearrange("m (h q) -> m h q", h=8),
                                                           in1=tok[b2][:, 8:16].unsqueeze(2).to_broadcast([64, 8, 64]), op=ALU.mult), reads=[k_("xdt"), k_("tok")], writes=[k_("xdd")])
                yield
                if K.stop is not None and K.stop <= 3:
                    return
                for g in range(2):
                    p.op("pool", lambda g=g, b2=b2: G.tensor_tensor(out=Cdec[b2][:, g, :].rearrange("n (j l) -> n j l", j=4), in0=CTz[g][:, sl].unsqueeze(1).to_broadcast([128, 4, 64]),
                                                                    in1=ECS[b2][:, g * 256:(g + 1) * 256].rearrange("n (j l) -> n j l", j=4), op=ALU.mult), reads=["CTz%d" % g, k_("ECS")], writes=[k_("Cdec")])
                Zp, zk = ps_next(K)

                def emit_z(Zp=Zp, sl=sl):
                    for kc in range(8):
                        inst = T.matmul(Zp[0:64, :], lhsT=hb[:, kc, sl], rhs=W[:, kc, 768:1280], start=(kc == 0), stop=(kc == 7))
                    return inst
                p.op("pe", emit_z, reads=["W", hk], writes=[zk])
                p.op("act", lambda Zp=Zp, b2=b2: A.activation(out=sz[b2][:], in_=Zp[0:64, :], func=AF.Silu), reads=[zk], writes=[k_("sz")])
                yield
                if K.stop is not None and K.stop <= 4:
                    return
                Bp, bk = ps_next(K)
                p.op("pe", lambda sl=sl, Bp=Bp: T.matmul(Bp[0:64, 0:128], lhsT=BT[:, sl], rhs=K.identb[:], start=True, stop=True), reads=["cv4", "identb"], writes=[bk])
                p.op("act", lambda b2=b2, Bp=Bp: A.copy(out=Btok[b2][:], in_=Bp[0:64, 0:128]), reads=[bk], writes=[k_("Btok")])
                Ip, ik = ps_next(K)
                p.op("pe", lambda Ip=Ip, b2=b2: T.matmul(Ip[:], lhsT=Btok[b2][:], rhs=xdd[b2][:], start=True, stop=True), reads=[k_("Btok"), k_("xdd")], writes=[ik])
                for g in range(2):
                    pr = slice(g * 64, (g + 1) * 64)
                    p.op("pool", lambda g=g, pr=pr, b2=b2: G.tensor_tensor(out=tmpS[pr], in0=St[pr],
                                                                           in1=ECS[b2][pr, :].rearrange("n (h l) -> n h l", h=8)[:, 4 * g:4 * g + 4, 63:64].to_broadcast([64, 4, 64]), op=ALU.mult),
                         reads=["St", k_("ECS")], writes=["tmpS%d" % g])
                    p.op("dve", lambda g=g, pr=pr, Ip=Ip: V.tensor_tensor(out=St[pr], in0=Ip[pr, g * 256:(g + 1) * 256].rearrange("n (j q) -> n j q", j=4), in1=tmpS[pr], op=ALU.add),
                         reads=[ik, "tmpS%d" % g], writes=["St"])
                p.op("act", lambda: A.copy(out=Sout[:], in_=St[:]), reads=["St"], writes=[soutk])
                yield
                if K.stop is not None and K.stop <= 5:
                    return
                Yp, yk_ = ps_next(K)

                def emit_y(Yp=Yp, b2=b2):
                    for h in range(8):
                        g, j = h // 4, h % 4
                        T.matmul(Yp[0:64, h * 64:(h + 1) * 64], lhsT=MT[b2][:, h * 64:(h + 1) * 64], rhs=xdt[b2][:, h * 64:(h + 1) * 64], start=True, stop=False)
                        inst = T.matmul(Yp[0:64, h * 64:(h + 1) * 64], lhsT=Cdec[b2][:, g, j * 64:(j + 1) * 64], rhs=Sin[:, j, :], start=False, stop=True)
                    return inst
                p.op("pe", emit_y, reads=[k_("MT"), k_("xdt"), k_("Cdec"), sink], writes=[yk_])
                yield
                if K.stop is not None and K.stop <= 6:
                    return
                p.op("pool", lambda b2=b2: G.tensor_tensor(out=y1[b2][:], in0=xtok[b2][:], in1=dB[:], op=ALU.mult), reads=[k_("xtok"), "cw"], writes=[k_("y1")])
                p.op("dve", lambda b2=b2, Yp=Yp: V.tensor_tensor(out=y1[b2][:], in0=Yp[0:64, :], in1=y1[b2][:], op=ALU.add), reads=[yk_, k_("y1")], writes=[k_("y1")])
                p.op("pool", lambda b2=b2: G.tensor_tensor(out=y1[b2][:], in0=y1[b2][:], in1=sz[b2][:], op=ALU.mult), reads=[k_("y1"), k_("sz")], writes=[k_("y1")])
                p.op("act", lambda b2=b2: A.activation(out=junk[b2][:], in_=y1[b2][:], func=AF.Square, accum_out=ssq[b2][:, 0:1]), reads=[k_("y1")], writes=[k_("ssq")])
                p.op("act", lambda: A.activation(out=ssq[b2][:, 1:2], in_=ssq[b2][:, 0:1], func=AF.Sqrt, scale=1.0 / 512, bias=1e-6), reads=[k_("ssq")], writes=[k_("ssq")])
                p.op("dve", lambda: V.reciprocal(out=ssq[b2][:, 2:3], in_=ssq[b2][:, 1:2]), reads=[k_("ssq")], writes=[k_("ssq")])
                p.op("dve", lambda b2=b2: V.scalar_tensor_tensor(out=y3[b2][:], in0=y1[b2][:], scalar=ssq[b2][:, 2:3], in1=nwB[:], op0=ALU.mult, op1=ALU.mult), reads=[k_("y1"), k_("ssq"), "cw"], writes=[k_("y3")])

                yield
                if K.stop is not None and K.stop <= 7:
                    return
                Op_, opk = ps_next(K)

                def emit_o(b2=b2, Op_=Op_):
                    for c4 in range(4):
                        inst = T.matmul(Op_[:, c4 * 64:(c4 + 1) * 64], lhsT=y3[b2][:, c4 * 128:(c4 + 1) * 128], rhs=K.identb[0:64, 0:64], start=True, stop=True)
                    return inst
                p.op("pe", emit_o, reads=[k_("y3"), "identb"], writes=[opk])
                p.op("act", lambda y=y, sl=sl, Op_=Op_: A.copy(out=y[:, :, sl], in_=Op_[:, 0:256].rearrange("q (c l) -> q c l", c=4)), reads=[opk], writes=[yk])
            for b0 in range(0, 8, NG):
                gens = [chunk_gen(cx, cx - b0, t * 8 + cx) for cx in range(b0, min(8, b0 + NG))]
                while gens:
                    for g_ in list(gens):
                        try:
                            next(g_)
                        except StopIteration:
                            gens.remove(g_)
            p.dma("sp", [(K.Y[YCH["c"]:YCH["c"] + 4, :, t * TT:(t + 1) * TT].rearrange("c p t -> p c t"), y[:])], reads=[yk], writes=["Y"])


def phase_gdn(K, l):
    p, nc = K.p, K.nc
    A, V, G, T = nc.scalar, nc.vector, nc.gpsimd, nc.tensor
    fl = lambda ap: ap.rearrange("k h l -> k (h l)")
    with p.scope():
        W = p.sb("W", [128, 8, 2056], BF16)
        wv = K.inp["w_in"][l]
        load_w(K, W[:, :, 0:1536], "W", wv, C_Q, C_Q + 1536)
        load_w(K, W[:, :, 1536:2048], "W", wv, C_ZG, C_ZG + 512)
        load_w(K, W[:, :, 2048:2056], "W", wv, C_BG, C_BG + 8)
        cw = p.sb("cw", [128, 12, 4]); cb = p.sb("cb", [128, 12])
        sm4 = p.sb("sm4", [4, 4]); nw = p.sb("nw", [128, 1])
        p.dma("sp", [(cw[:], K.inp["conv_w4"][l, :, 0:12, :]), (cb[:], K.inp["conv_b"][l, :, 0:12]),
                     (sm4[:, 0:1], K.inp["gdn_dt_bias"][l]), (sm4[:, 1:2], K.inp["gdn_a_log"][l]), (nw[:], K.inp["gdn_norm_w"][l])], writes=["cw"])
        p.op("act", lambda: A.activation(out=sm4[:, 2:3], in_=sm4[:, 1:2], func=AF.Exp), reads=["cw"], writes=["sm4"])
        p.op("dve", lambda: V.tensor_scalar(out=sm4[:, 2:3], in0=sm4[:, 2:3], scalar1=-1.0, scalar2=None, op0=ALU.mult), reads=["sm4"], writes=["sm4"])
        C = K.gdn_c
        IND4t = p.sb("IND4t", [4, 4, C])
        p.op("pool", lambda: G.memset(IND4t[:], 1.0), writes=["gmask"])
        p.op("pool", lambda: G.affine_select(out=IND4t[:], in_=IND4t[:], pattern=[[-1, 4], [0, C]], compare_op=ALU.is_equal, fill=0.0, base=0, channel_multiplier=1), reads=["gmask"], writes=["gmask"])
        rmaskC = p.sb("rmaskC", [4, TT // C, C])
        p.op("pool", lambda: G.memset(rmaskC[:], 1.0), reads=["gmask"], writes=["gmask"])
        p.op("pool", lambda: G.memset(rmaskC[:, :, 0:1], 0.0), reads=["gmask"], writes=["gmask"])
        M_le = p.sb("M_le", [C, 4, C], BF16); M_lt = p.sb("M_lt", [C, 4, C], BF16); M_gt = p.sb("M_gt", [C, 4, C], BF16)
        for (t_, cm, coef, cmp) in ((M_le, -1, 1, ALU.is_ge), (M_lt, -1, 1, ALU.is_gt), (M_gt, 1, -1, ALU.is_gt)):
            p.op("pool", lambda t_=t_: G.memset(t_[:], 0.0), reads=["gmask"], writes=["gmask"])
            p.op("pool", lambda t_=t_, cm=cm, coef=coef, cmp=cmp: G.affine_select(out=t_[:], in_=t_[:], pattern=[[0, 4], [coef, C]], compare_op=cmp, fill=NEG, base=0, channel_multiplier=cm), reads=["gmask"], writes=["gmask"])
        bmf = {}
        kb = 8
        while kb < C:
            nb = C // kb
            ind = p.sb("ind%d" % kb, [16, C])
            p.op("pool", lambda ind=ind: G.memset(ind[:], 1.0), reads=["gmask"], writes=["gmask"])
            p.op("pool", lambda ind=ind, kb=kb: G.affine_select(out=ind[:], in_=ind[:], pattern=[[1, C]], compare_op=ALU.is_ge, fill=0.0, base=0, channel_multiplier=-kb), reads=["gmask"], writes=["gmask"])
            p.op("pool", lambda ind=ind, kb=kb: G.affine_select(out=ind[:], in_=ind[:], pattern=[[-1, C]], compare_op=ALU.is_ge, fill=0.0, base=kb - 1, channel_multiplier=kb), reads=["gmask"], writes=["gmask"])
            ps, pk = ps_next(K)
            p.op("pe", lambda ind=ind, nb=nb, ps=ps: T.matmul(ps[0:C, 0:C], lhsT=ind[0:nb, :], rhs=ind[0:nb, :], start=True, stop=True), reads=["gmask"], writes=[pk])
            bm = p.sb("bm%d" % kb, [C, C])
            p.op("act", lambda bm=bm, ps=ps: A.copy(out=bm[:], in_=ps[0:C, 0:C]), reads=[pk], writes=["gmask"])
            bmf[kb] = bm
            kb *= 2
        b4 = lambda m_: m_[:].unsqueeze(1).to_broadcast([C, 4, C])
        BMb = p.sb("BMb", [C, 4, C], BF16)
        p.op("dve", lambda: V.tensor_copy(out=BMb[:], in_=b4(bmf[8])), reads=["gmask"], writes=["gmask"])
        Dlist = []
        kb = 8
        while kb < C:
            Dm = p.sb("Dm%d" % kb, [C, 4, C], BF16)
            if 2 * kb < C:
                p.op("dve", lambda Dm=Dm, kb=kb: V.tensor_tensor(out=Dm[:], in0=b4(bmf[2 * kb]), in1=b4(bmf[kb]), op=ALU.subtract), reads=["gmask"], writes=["gmask"])
            else:
                p.op("dve", lambda Dm=Dm, kb=kb: V.tensor_scalar(out=Dm[:], in0=b4(bmf[kb]), scalar1=-1.0, scalar2=1.0, op0=ALU.mult, op1=ALU.add), reads=["gmask"], writes=["gmask"])
            Dlist.append(Dm)
            kb *= 2
        SEL4 = p.sb("SEL4", [4, 4, 128])
        p.op("pool", lambda: G.memset(SEL4[:], 1.0), writes=["SEL4"])
        p.op("pool", lambda: G.affine_select(out=SEL4[:], in_=SEL4[:], pattern=[[-1, 4], [0, 128]], compare_op=ALU.is_equal, fill=0.0, base=0, channel_multiplier=1), reads=["SEL4"], writes=["SEL4"])
        Pbw = [p.sb("Pbw", [128, 515]) for _ in range(2)]
        tails = p.sb("tails", [128, 12, 3])
        p.op("pool", lambda: G.memset(tails[:], 0.0), writes=["tails"])
        Sf = p.sb("Sf", [128, 4, 128]); Sb = p.sb("Sb", [128, 4, 128], BF16)
        p.op("pool", lambda: G.memset(Sf[:], 0.0), writes=["Sf"])
        p.op("pool", lambda: G.memset(Sb[:], 0.0), writes=["Sb"])
        hb1 = p.sb("hbuf", [128, 8, TT], BF16)
        K.hbuf = [hb1, hb1]
        cf = p.sb("cf", [128, TT]); sqb = p.sb("sqb", [128, TT], BF16); rinv = p.sb("rinv", [128, TT])
        qT = p.sb("qT", [128, 4, TT], BF16); kT = p.sb("kT", [128, 4, TT], BF16); vT = p.sb("vT", [128, 4, TT], BF16); qdT = p.sb("qdT", [128, 4, TT], BF16)
        EG = p.sb("EG", [128, 4, TT])
        oT = p.sb("oT", [128, 4, TT], BF16); zs = p.sb("zs", [128, TT]); yT1 = p.sb("yT", [128, 4, TT], BF16)
        yT = [yT1, yT1]
        S4 = lambda n: p.sb(n, [4, TT])
        beT, lnbT, gT, gcT, ngcT, cbT, ecbT, erevT = S4("beT"), S4("lnbT"), S4("gT"), S4("gcT"), S4("ngcT"), S4("cbT"), S4("ecbT"), S4("erevT")
        NG = K.gdn_ng

        def alloc_group():
            d = {}
            d["gcbd"] = p.sb("gcbd", [4, 4, C]); d["ngcbd"] = p.sb("ngcbd", [4, 4, C]); d["cbbd"] = p.sb("cbbd", [4, 4, C])
            d["DA"] = p.sb("DA", [C, 4 * C], BF16); d["DAT"] = p.sb("DAT", [C, 4 * C], BF16); d["DQT"] = p.sb("DQT", [C, 4 * C], BF16)
            d["X"] = [p.sb("X", [C, 4 * C], BF16) for _ in range(2)]; d["Y_"] = [p.sb("Yi", [C, 4 * C], BF16) for _ in range(2)]
            d["Q"] = p.sb("Q", [C, 4 * C], BF16); d["Qt"] = p.sb("Qt", [C, 4 * C], BF16)
            d["Xs"] = [p.sb("Xs", [C, 4 * C], BF16) for _ in range(2)]; d["Ys"] = [p.sb("Ys", [C, 4 * C], BF16) for _ in range(2)]
            d["attnT"] = p.sb("attnT", [C, 4 * C], BF16); d["tok"] = p.sb("tok", [C, 12]); d["tokc"] = p.sb("tokc", [C, 8])
            d["RHSw"] = p.sb("RHSw", [C, 512], BF16); d["RHSu"] = p.sb("RHSu", [C, 512], BF16); d["kdec"] = p.sb("kdec", [C, 512], BF16)
            d["nwT"] = p.sb("nwT", [128, 4 * C], BF16); d["vnb"] = p.sb("vnb", [C, 4, 128], BF16)
            return d
        GB = [alloc_group() for _ in range(NG)]
        I64b = bc(K.identb[0:C, 0:C], [C, 4, C], 1)
        IND4 = IND4t[:]
        v3 = lambda t: t[:].rearrange("m (h l) -> m h l", h=4)
        for t in range(K.S // TT):
            hb, hk = load_hT(K, t)
            for c in range(12):
                ps, pk = proj(K, W, "W", c * 128, 128, hb, hk)
                if c >= 8:
                    conv_chunk(K, ps, pk, Pbw[c % 2], "Pbw%d" % (c % 2), cw, cb, c, vT[:, c - 8, :], "vT", AF.Silu, tails=tails, tkey="tails")
                    continue
                conv_chunk(K, ps, pk, Pbw[c % 2], "Pbw%d" % (c % 2), cw, cb, c, cf[:], "cf", AF.Silu, tails=tails, tkey="tails")
                p.op("act", lambda: A.activation(out=sqb[:], in_=cf[:], func=AF.Square), reads=["cf"], writes=["sqb"])
                ps2, pk2 = ps_next(K)
                p.op("pe", lambda ps2=ps2: T.matmul(ps2[:], lhsT=K.onesb[:], rhs=sqb[:], start=True, stop=True), reads=["sqb", "onesb"], writes=[pk2])
                p.op("act", lambda ps2=ps2: A.activation(out=rinv[:], in_=ps2[:], func=AF.Sqrt, bias=1e-6), reads=[pk2], writes=["rinv"])
                p.op("dve", lambda: V.reciprocal(out=rinv[:], in_=rinv[:]), reads=["rinv"], writes=["rinv"])
                dst, dk, sc = (qT[:, c, :], "qT", 128 ** -0.5) if c < 4 else (kT[:, c - 4, :], "kT", 1.0)
                p.op("dve", lambda dst=dst, sc=sc: V.scalar_tensor_tensor(out=dst, in0=cf[:], scalar=sc, in1=rinv[:], op0=ALU.mult, op1=ALU.mult), reads=["cf", "rinv"], writes=[dk])
            ps, pk = proj(K, W, "W", 2048, 4, hb, hk)
            p.op("act", lambda ps=ps: A.activation(out=beT[:], in_=ps[0:4, :], func=AF.Sigmoid), reads=[pk], writes=["beT"])
            p.op("act", lambda: A.activation(out=lnbT[:], in_=beT[:], func=AF.Ln), reads=["beT"], writes=["lnbT"])
            ps, pk = proj(K, W, "W", 2052, 4, hb, hk)
            p.op("act", lambda ps=ps: A.activation(out=gT[:], in_=ps[0:4, :], func=AF.Exp, bias=sm4[:, 0:1]), reads=[pk, "cw"], writes=["gT"])
            p.op("act", lambda: A.activation(out=gT[:], in_=gT[:], func=AF.Ln, bias=1.0), reads=["gT"], writes=["gT"])
            p.op("dve", lambda: V.tensor_scalar(out=gT[:], in0=gT[:], scalar1=sm4[:, 2:3], scalar2=None, op0=ALU.mult), reads=["gT", "sm4"], writes=["gT"])
            p.op("dve", lambda: V.tensor_tensor_scan(out=gcT[:], data0=fl(rmaskC[:]), data1=gT[:], initial=0.0, op0=ALU.mult, op1=ALU.add), reads=["gT", "gmask"], writes=["gcT"])
            p.op("dve", lambda: V.tensor_scalar(out=ngcT[:], in0=gcT[:], scalar1=-1.0, scalar2=None, op0=ALU.mult), reads=["gcT"], writes=["ngcT"])
            p.op("dve", lambda: V.tensor_tensor(out=cbT[:], in0=gcT[:], in1=lnbT[:], op=ALU.add), reads=["gcT", "lnbT"], writes=["cbT"])
            p.op("act", lambda: A.activation(out=ecbT[:], in_=cbT[:], func=AF.Exp), reads=["cbT"], writes=["ecbT"])
            g3 = gcT[:].rearrange("h (c l) -> h c l", c=TT // C)
            p.op("dve", lambda: V.tensor_tensor(out=erevT[:].rearrange("h (c l) -> h c l", c=TT // C), in0=g3[:, :, C - 1:C].to_broadcast([4, TT // C, C]), in1=g3, op=ALU.subtract), reads=["gcT"], writes=["erevT"])
            p.op("act", lambda: A.activation(out=erevT[:], in_=erevT[:], func=AF.Exp), reads=["erevT"], writes=["erevT"])
            for h in range(4):
                ps, pk = ps_next(K)
                p.op("pe", lambda ps=ps, h=h: T.matmul(ps[:], lhsT=SEL4[:, h, :], rhs=gcT[:], start=True, stop=True), reads=["SEL4", "gcT"], writes=[pk])
                p.op("act", lambda ps=ps, h=h: A.activation(out=EG[:, h, :], in_=ps[:], func=AF.Exp), reads=[pk], writes=["EG"])
            p.op("dve", lambda: V.tensor_tensor(out=qdT[:], in0=qT[:], in1=EG[:], op=ALU.mult), reads=["qT", "EG"], writes=["qdT"])
            y = yT[0]; yk = "yT0"
            def chunk_gen(cx, gi):
                d = GB[gi]
                gcbd, ngcbd, cbbd, DA, DAT, DQT, X, Y_, Q, Qt, Xs, Ys = (d[n] for n in ("gcbd", "ngcbd", "cbbd", "DA", "DAT", "DQT", "X", "Y_", "Q", "Qt", "Xs", "Ys"))
                attnT, tok, RHSw, RHSu, kdec, nwT, vnb = (d[n] for n in ("attnT", "tok", "RHSw", "RHSu", "kdec", "nwT", "vnb"))
                tokc = d["tokc"]
                Em, Emt, M1, M1t = DA, DAT, Xs[0], Ys[0]
                kq = lambda n: "%s_g%d" % (n, gi)
                sl = slice(cx * C, (cx + 1) * C)
                p.op("pool", lambda: G.tensor_tensor(out=gcbd[:], in0=gcT[:, sl].unsqueeze(1).to_broadcast([4, 4, C]), in1=IND4, op=ALU.mult), reads=["gcT", "gmask"], writes=[kq("gcbd")])
                p.op("pool", lambda: G.tensor_tensor(out=ngcbd[:], in0=ngcT[:, sl].unsqueeze(1).to_broadcast([4, 4, C]), in1=IND4, op=ALU.mult), reads=["ngcT", "gmask"], writes=[kq("ngcbd")])
                p.op("pool", lambda: G.tensor_tensor(out=cbbd[:], in0=cbT[:, sl].unsqueeze(1).to_broadcast([4, 4, C]), in1=IND4, op=ALU.mult), reads=["cbT", "gmask"], writes=[kq("cbbd")])
                yield
                Tc, tck = ps_next(K)

                def emit_tc(Tc=Tc):
                    T.matmul(Tc[0:C, 0:4], lhsT=cbT[:, sl], rhs=K.identf[0:4, 0:4], start=True, stop=True)
                    return T.matmul(Tc[0:C, 4:8], lhsT=ngcT[:, sl], rhs=K.identf[0:4, 0:4], start=True, stop=True)
                p.op("pe", emit_tc, reads=["cbT", "ngcT", "identf"], writes=[tck])
                p.op("act", lambda Tc=Tc: A.copy(out=tokc[:], in_=Tc[0:C, 0:8]), reads=[tck], writes=[kq("tokc")])
                for (dst, dk, c0_, rowbd, rk, msk) in ((DA, kq("DA"), 0, ngcbd, kq("ngcbd"), M_gt),
                                                       (DAT, kq("DAT"), 4, cbbd, kq("cbbd"), M_lt),
                                                       (DQT, kq("DQT"), 4, gcbd, kq("gcbd"), M_le)):
                    Dp, dpk = ps_next(K)

                    def emit_d(Dp=Dp, rowbd=rowbd, msk=msk):
                        T.matmul(Dp[0:C, 0:4 * C], lhsT=K.onesf[0:4, 0:C], rhs=fl(rowbd[:]), start=True, stop=False)
                        return T.matmul(Dp[0:C, 0:4 * C], lhsT=K.identb[0:C, 0:C], rhs=fl(msk[:]), start=False, stop=True)
                    p.op("pe", emit_d, reads=[rk, "gmask", "onesf", "identb"], writes=[dpk])
                    for h in range(4):
                        p.op("act", lambda Dp=Dp, dst=dst, h=h, c0_=c0_: A.activation(out=dst[:, h * C:(h + 1) * C], in_=Dp[0:C, h * C:(h + 1) * C], func=AF.Exp, bias=tokc[:, c0_ + h:c0_ + h + 1]),
                             reads=[dpk, kq("tokc")], writes=[dk])
                yield
                Gp, gk = ps_next(K)
                Qp, qk = ps_next(K)

                def emit_g(Gp=Gp, Qp=Qp):
                    for h in range(4):
                        T.matmul(Gp[0:C, h * C:(h + 1) * C], lhsT=kT[:, h, sl], rhs=kT[:, h, sl], start=True, stop=True)
                    for h in range(4):
                        inst = T.matmul(Qp[0:C, h * C:(h + 1) * C], lhsT=kT[:, h, sl], rhs=qT[:, h, sl], start=True, stop=True)
                    return inst
                p.op("pe", emit_g, reads=["kT", "qT"], writes=[gk, qk])
                p.op("dve", lambda Gp=Gp: V.scalar_tensor_tensor(out=X[0][:], in0=Gp[0:C, 0:4 * C], scalar=-1.0, in1=DA[:], op0=ALU.mult, op1=ALU.mult), reads=[gk, kq("DA")], writes=[kq("X0")])
                p.op("dve", lambda Gp=Gp: V.scalar_tensor_tensor(out=Y_[0][:], in0=Gp[0:C, 0:4 * C], scalar=-1.0, in1=DAT[:], op0=ALU.mult, op1=ALU.mult), reads=[gk, kq("DAT")], writes=[kq("Y0")])
                p.op("dve", lambda Qp=Qp: V.tensor_tensor(out=attnT[:], in0=Qp[0:C, 0:4 * C], in1=DQT[:], op=ALU.mult), reads=[qk, kq("DQT")], writes=[kq("attnT")])
                yield
                fl4 = lambda m: m[:].rearrange("k h l -> k (h l)")

                def mm4(out_ps, lhs, rhs):
                    def emit():
                        for h in range(4):
                            inst = T.matmul(out_ps[0:C, h * C:(h + 1) * C], lhsT=lhs[:, h * C:(h + 1) * C], rhs=rhs[:, h * C:(h + 1) * C], start=True, stop=True)
                        return inst
                    return emit
                p.op("dve", lambda: V.tensor_tensor(out=X[1][:], in0=X[0][:], in1=fl4(BMb), op=ALU.mult), reads=[kq("X0"), "gmask"], writes=[kq("X1")])
                p.op("pool", lambda: G.tensor_tensor(out=Y_[1][:], in0=Y_[0][:], in1=fl4(BMb), op=ALU.mult), reads=[kq("Y0"), "gmask"], writes=[kq("Y1")])
                p.op("pool", lambda: G.tensor_tensor(out=v3(Q), in0=v3(Y_[1]), in1=I64b, op=ALU.add), reads=[kq("Y1"), "identf"], writes=[kq("Q")])
                p.op("dve", lambda: V.tensor_tensor(out=v3(Qt), in0=v3(X[1]), in1=I64b, op=ALU.add), reads=[kq("X1"), "identf"], writes=[kq("Qt")])
                xb, yb = X[1], Y_[1]; xbk, ybk = kq("X1"), kq("Y1")
                for i in range(2):
                    xn, yn = (Xs[i % 2], Ys[i % 2]); xnk, ynk = kq("Xs%d" % (i % 2)), kq("Ys%d" % (i % 2))
                    Xp, xk = ps_next(K); Yp, ypk = ps_next(K)
                    p.op("pe", mm4(Xp, yb, xb), reads=[xbk, ybk], writes=[xk])
                    p.op("pe", mm4(Yp, xb, yb), reads=[xbk, ybk], writes=[ypk])
                    p.op("act", lambda Xp=Xp, xn=xn: A.copy(out=xn[:], in_=Xp[0:C, 0:4 * C]), reads=[xk], writes=[xnk])
                    p.op("dve", lambda Yp=Yp, yn=yn: V.tensor_copy(out=yn[:], in_=Yp[0:C, 0:4 * C]), reads=[ypk], writes=[ynk])
                    yield
                    Dq, dqk = ps_next(K); Dt, dtk = ps_next(K)
                    p.op("pe", mm4(Dq, xn, Q), reads=[xnk, kq("Q")], writes=[dqk])
                    p.op("pe", mm4(Dt, yn, Qt), reads=[ynk, kq("Qt")], writes=[dtk])
                    p.op("dve", lambda Dq=Dq: V.tensor_tensor(out=Q[:], in0=Dq[0:C, 0:4 * C], in1=Q[:], op=ALU.add), reads=[dqk, kq("Q")], writes=[kq("Q")])
                    p.op("dve", lambda Dt=Dt: V.tensor_tensor(out=Qt[:], in0=Dt[0:C, 0:4 * C], in1=Qt[:], op=ALU.add), reads=[dtk, kq("Qt")], writes=[kq("Qt")])
                    yield
                    xb, yb, xbk, ybk = xn, yn, xnk, ynk
                for li, Dm in enumerate(Dlist):
                    last = (li == len(Dlist) - 1)
                    p.op("pool", lambda Dm=Dm: G.tensor_tensor(out=Em[:], in0=Y_[0][:], in1=fl4(Dm), op=ALU.mult), reads=[kq("Y0"), "gmask"], writes=[kq("DA")])
                    p.op("pool", lambda Dm=Dm: G.tensor_tensor(out=Emt[:], in0=X[0][:], in1=fl4(Dm), op=ALU.mult), reads=[kq("X0"), "gmask"], writes=[kq("DAT")])
                    yield
                    P1, p1k = ps_next(K)
                    p.op("pe", mm4(P1, Emt, Q), reads=[kq("DAT"), kq("Q")], writes=[p1k])
                    p.op("act", lambda P1=P1: A.copy(out=M1[:], in_=P1[0:C, 0:4 * C]), reads=[p1k], writes=[kq("Xs0")])
                    if not last:
                        P1t, p1tk = ps_next(K)
                        p.op("pe", mm4(P1t, Em, Qt), reads=[kq("DA"), kq("Qt")], writes=[p1tk])
                        p.op("dve", lambda P1t=P1t: V.tensor_copy(out=M1t[:], in_=P1t[0:C, 0:4 * C]), reads=[p1tk], writes=[kq("Ys0")])
                    yield
                    P2, p2k = ps_next(K)
                    p.op("pe", mm4(P2, Qt, M1), reads=[kq("Qt"), kq("Xs0")], writes=[p2k])
                    if not last:
                        P2t, p2tk = ps_next(K)
                        p.op("pe", mm4(P2t, Q, M1t), reads=[kq("Q"), kq("Ys0")], writes=[p2tk])
                    p.op("dve", lambda P2=P2: V.tensor_tensor(out=Q[:], in0=P2[0:C, 0:4 * C], in1=Q[:], op=ALU.add), reads=[p2k, kq("Q")], writes=[kq("Q")])
                    if not last:
                        p.op("dve", lambda P2t=P2t: V.tensor_tensor(out=Qt[:], in0=P2t[0:C, 0:4 * C], in1=Qt[:], op=ALU.add), reads=[p2tk, kq("Qt")], writes=[kq("Qt")])
                    yield
                yield
                Tp, tk = ps_next(K)

                def emit_t(Tp=Tp):
                    T.matmul(Tp[0:C, 0:4], lhsT=beT[:, sl], rhs=K.identf[0:4, 0:4], start=True, stop=True)
                    T.matmul(Tp[0:C, 4:8], lhsT=ecbT[:, sl], rhs=K.identf[0:4, 0:4], start=True, stop=True)
                    return T.matmul(Tp[0:C, 8:12], lhsT=erevT[:, sl], rhs=K.identf[0:4, 0:4], start=True, stop=True)
                p.op("pe", emit_t, reads=["beT", "ecbT", "erevT", "identf"], writes=[tk])
                p.op("act", lambda Tp=Tp: A.copy(out=tok[:], in_=Tp[0:C, 0:12]), reads=[tk], writes=[kq("tok")])
                Kp, kk = ps_next(K)

                def emit_k(Kp=Kp):
                    for h in range(4):
                        inst = T.matmul(Kp[0:C, h * 128:(h + 1) * 128], lhsT=kT[:, h, sl], rhs=K.identb[:], start=True, stop=True)
                    return inst
                p.op("pe", emit_k, reads=["kT", "identb"], writes=[kk])
                k3 = Kp[0:C, :].rearrange("m (h d) -> m h d", h=4)
                p.op("dve", lambda k3=k3: V.tensor_tensor(out=RHSw[:].rearrange("m (h d) -> m h d", h=4), in0=k3, in1=bc(tok[:, 4:8], [C, 4, 128], 2), op=ALU.mult), reads=[kk, kq("tok")], writes=[kq("RHSw")])
                p.op("dve", lambda k3=k3: V.tensor_tensor(out=kdec[:].rearrange("m (h d) -> m h d", h=4), in0=k3, in1=bc(tok[:, 8:12], [C, 4, 128], 2), op=ALU.mult), reads=[kk, kq("tok")], writes=[kq("kdec")])
                Vp, vk = ps_next(K)

                def emit_v(Vp=Vp):
                    for h in range(4):
                        inst = T.matmul(Vp[0:C, h * 128:(h + 1) * 128], lhsT=vT[:, h, sl], rhs=K.identb[:], start=True, stop=True)
                    return inst
                p.op("pe", emit_v, reads=["vT", "identb"], writes=[vk])
                p.op("dve", lambda Vp=Vp: V.tensor_tensor(out=RHSu[:].rearrange("m (h d) -> m h d", h=4), in0=Vp[0:C, :].rearrange("m (h d) -> m h d", h=4), in1=bc(tok[:, 0:4], [C, 4, 128], 2), op=ALU.mult),
                     reads=[vk, kq("tok")], writes=[kq("RHSu")])
                yield
                Wp, wk = ps_next(K)

                def emit_w(Wp=Wp):
                    for h in range(4):
                        inst = T.matmul(Wp[:, h * C:(h + 1) * C], lhsT=RHSw[:, h * 128:(h + 1) * 128], rhs=Q[:, h * C:(h + 1) * C], start=True, stop=True)
                    return inst
                p.op("pe", emit_w, reads=[kq("RHSw"), kq("Q")], writes=[wk])
                p.op("act", lambda Wp=Wp: A.mul(out=nwT[:], in_=Wp[:, 0:4 * C], mul=-1.0), reads=[wk], writes=[kq("nwT")])
                yield
                Np, nk = ps_next(K)

                def emit_n(Np=Np):
                    for h in range(4):
                        T.matmul(Np[0:C, h * 128:(h + 1) * 128], lhsT=Q[:, h * C:(h + 1) * C], rhs=RHSu[:, h * 128:(h + 1) * 128], start=True, stop=False)
                        inst = T.matmul(Np[0:C, h * 128:(h + 1) * 128], lhsT=nwT[:, h * C:(h + 1) * C], rhs=Sb[:, h, :], start=False, stop=True)
                    return inst
                p.op("pe", emit_n, reads=[kq("Q"), kq("RHSu"), kq("nwT"), "Sb"], writes=[nk])
                p.op("act", lambda Np=Np: A.copy(out=vnb[:].rearrange("m h e -> m (h e)"), in_=Np[0:C, :]), reads=[nk], writes=[kq("vnb")])
                Op, ok_ = ps_next(K)

                def emit_o(Op=Op):
                    for h in range(4):
                        T.matmul(Op[:, h * C:(h + 1) * C], lhsT=Sb[:, h, :], rhs=qdT[:, h, sl], start=True, stop=False)
                        inst = T.matmul(Op[:, h * C:(h + 1) * C], lhsT=vnb[:, h, :], rhs=attnT[:, h * C:(h + 1) * C], start=False, stop=True)
                    return inst
                p.op("pe", emit_o, reads=["Sb", "qdT", kq("vnb"), kq("attnT")], writes=[ok_])
                p.op("act", lambda Op=Op: A.copy(out=oT[:, :, sl], in_=Op[:, 0:4 * C].rearrange("e (h c) -> e h c", h=4)), reads=[ok_], writes=["oT"])
                Ip, ik = ps_next(K)

                def emit_i(Ip=Ip):
                    for h in range(4):
                        inst = T.matmul(Ip[:, h * 128:(h + 1) * 128], lhsT=kdec[:, h * 128:(h + 1) * 128], rhs=vnb[:, h, :], start=True, stop=True)
                    return inst
                p.op("pe", emit_i, reads=[kq("kdec"), kq("vnb")], writes=[ik])
                for h in range(4):
                    p.op("dve", lambda h=h, Ip=Ip: V.scalar_tensor_tensor(out=Sf[:, h, :], in0=Sf[:, h, :], scalar=EG[:, h, cx * C + C - 1:cx * C + C], in1=Ip[:, h * 128:(h + 1) * 128], op0=ALU.mult, op1=ALU.add),
                         reads=["Sf", "EG", ik], writes=["Sf"])
                p.op("act", lambda: A.copy(out=Sb[:], in_=Sf[:]), reads=["Sf"], writes=["Sb"])
            for b0 in range(0, TT // C, NG):
                gens = [chunk_gen(cx, cx - b0) for cx in range(b0, min(TT // C, b0 + NG))]
                while gens:
                    for g_ in list(gens):
                        try:
                            next(g_)
                        except StopIteration:
                            gens.remove(g_)
            for h in range(4):
                p.op("act", lambda h=h: A.activation(out=sqb[:], in_=oT[:, h, :], func=AF.Square), reads=["oT"], writes=["sqb"])
                ps2, pk2 = ps_next(K)
                p.op("pe", lambda ps2=ps2: T.matmul(ps2[:], lhsT=K.onesb[:], rhs=sqb[:], start=True, stop=True), reads=["sqb", "onesb"], writes=[pk2])
                p.op("act", lambda ps2=ps2: A.activation(out=rinv[:], in_=ps2[:], func=AF.Sqrt, scale=1.0 / 128, bias=1e-6), reads=[pk2], writes=["rinv"])
                p.op("dve", lambda: V.reciprocal(out=rinv[:], in_=rinv[:]), reads=["rinv"], writes=["rinv"])
                ps, pk = proj(K, W, "W", 1536 + h * 128, 128, hb, hk)
                p.op("act", lambda ps=ps: A.activation(out=zs[:], in_=ps[:], func=AF.Silu), reads=[pk], writes=["zs"])
                p.op("dve", lambda h=h: V.scalar_tensor_tensor(out=cf[:], in0=oT[:, h, :], scalar=nw[:, 0:1], in1=rinv[:], op0=ALU.mult, op1=ALU.mult), reads=["oT", "rinv", "cw"], writes=["cf"])
                p.op("dve", lambda h=h, y=y: V.tensor_tensor(out=y[:, h, :], in0=cf[:], in1=zs[:], op=ALU.mult), reads=["cf", "zs"], writes=[yk])
            p.dma("sp", [(K.Y[YCH["b"]:YCH["b"] + 4, :, t * TT:(t + 1) * TT].rearrange("c p t -> p c t"), y[:])], reads=[yk], writes=["Y"])


def phase_s5(K, l):
    import math
    p, nc = K.p, K.nc
    A, V, G, T = nc.scalar, nc.vector, nc.gpsimd, nc.tensor
    S = K.S
    NCH = S // 32
    PI = math.pi
    b2 = lambda ap, shape: ap.unsqueeze(2).to_broadcast(list(shape))
    b1 = lambda ap, shape: ap.unsqueeze(1).to_broadcast(list(shape))
    with p.scope():
        Un = p.sb("Un", [NCH, 32, 384], BF16)
        with p.scope():
            Wu = p.sb("Wu", [128, 8, 384], BF16)
            load_w(K, Wu[:], "Wu", K.inp["w_in"][l], C_U, C_U + 384)
            K.hbuf = [p.sb("hbuf", [128, 8, TT], BF16) for _ in range(2)]
            ub = [p.sb("ub", [128, 384], BF16) for _ in range(2)]
            for t in range(S // TT):
                hb, hk = load_hT(K, t)
                for s4 in range(4):
                    i = t * 4 + s4
                    ps, pk = ps_next(K)

                    def emit(ps=ps, s4=s4, hb=hb):
                        for kc in range(8):
                            inst = T.matmul(ps[:, 0:384], lhsT=hb[:, kc, s4 * 128:(s4 + 1) * 128], rhs=Wu[:, kc, :], start=(kc == 0), stop=(kc == 7))
                        return inst
                    p.op("pe", emit, reads=["Wu", hk], writes=[pk])
                    u_ = ub[i % 2]; uk = "ub%d" % (i % 2)
                    p.op("act", lambda ps=ps, u_=u_: A.copy(out=u_[:], in_=ps[:, 0:384]), reads=[pk], writes=[uk])
                    p.dma("sp", [(K.Us[i * 128:(i + 1) * 128, :], u_[:])], reads=[uk], writes=["Us"])
        p.dma("sp", [(Un[:], K.Us.rearrange("(n s) c -> n s c", s=32))], reads=["Us"], writes=["Un"])
        with p.scope():
            sml = p.sb("sml", [128, 3, 24])
            Br = p.sb("Br", [128, 24, 16]); Bi = p.sb("Bi", [128, 24, 16]); Cr = p.sb("Cr", [128, 24, 16]); Ci = p.sb("Ci", [128, 24, 16])
            dB = p.sb("dB", [128, 384])
            p.dma("sp", [(sml[:, 0, :], K.inp["s5_lamr"][l]), (sml[:, 1, :], K.inp["s5_lami"][l]), (sml[:, 2, :], K.inp["s5_ldt"][l]),
                         (Br[:], K.inp["s5_br"][l]), (Bi[:], K.inp["s5_bi"][l]), (Cr[:], K.inp["s5_cr"][l]), (Ci[:], K.inp["s5_ci"][l]),
                         (dB[:], K.inp["s5_dB"][l])], writes=["tab"])
            tb = "tab"
            lamr, lami = sml[:, 0, :], sml[:, 1, :]
            sc = p.sb("sc", [128, 12, 24])
            dt, ar, ai, lam2, nr, ni, cr, ci, tA, tB = [sc[:, j, :] for j in range(10)]
            p.op("act", lambda: A.activation(out=dt, in_=sml[:, 2, :], func=AF.Exp), reads=[tb], writes=[tb])
            p.op("dve", lambda: V.tensor_tensor(out=ar, in0=lamr, in1=dt, op=ALU.mult), reads=[tb], writes=[tb])
            p.op("dve", lambda: V.tensor_tensor(out=ai, in0=lami, in1=dt, op=ALU.mult), reads=[tb], writes=[tb])
            ones33 = p.sb("ones33", [128, 33]); tg = p.sb("tg", [128, 33])
            p.op("pool", lambda: G.memset(ones33[:], 1.0), writes=[tb])
            p.op("dve", lambda: V.tensor_tensor_scan(out=tg[:], data0=ones33[:], data1=ones33[:], initial=-1.0, op0=ALU.mult, op1=ALU.add), reads=[tb], writes=[tb])
            T3 = lambda n: p.sb(n, [128, 24, 33])
            arg, mag, sn, cs, Pr, Pi, Nr, Ni = T3("arg"), T3("mag"), T3("sn"), T3("cs"), T3("Pr"), T3("Pi"), T3("Nr"), T3("Ni")
            sh3 = [128, 24, 33]
            p.op("dve", lambda: V.tensor_tensor(out=arg[:], in0=b2(ar, sh3), in1=b1(tg[:], sh3), op=ALU.mult), reads=[tb], writes=[tb])
            p.op("act", lambda: A.activation(out=mag[:], in_=arg[:], func=AF.Exp), reads=[tb], writes=[tb])
            p.op("act", lambda: A.activation(out=Nr[:], in_=arg[:], func=AF.Exp, scale=-1.0), reads=[tb], writes=[tb])
            p.op("dve", lambda: V.tensor_tensor(out=arg[:], in0=b2(ai, sh3), in1=b1(tg[:], sh3), op=ALU.mult), reads=[tb], writes=[tb])
            MAGIC = 12582912.0
            for (dst, off) in ((sn, 0.0), (cs, 0.5 * PI)):
                p.op("dve", lambda dst=dst, off=off: V.tensor_scalar(out=dst[:], in0=arg[:], scalar1=off, scalar2=1.0 / (2 * PI), op0=ALU.add, op1=ALU.mult), reads=[tb], writes=[tb])
                p.op("dve", lambda dst=dst: V.tensor_scalar(out=dst[:], in0=dst[:], scalar1=MAGIC, scalar2=None, op0=ALU.add), reads=[tb], writes=[tb])
                p.op("dve", lambda dst=dst: V.tensor_scalar(out=dst[:], in0=dst[:], scalar1=-MAGIC, scalar2=None, op0=ALU.add), reads=[tb], writes=[tb])
                p.op("dve", lambda dst=dst: V.scalar_tensor_tensor(out=dst[:], in0=dst[:], scalar=-2 * PI, in1=arg[:], op0=ALU.mult, op1=ALU.add), reads=[tb], writes=[tb])
                if off != 0.0:
                    p.op("dve", lambda dst=dst, off=off: V.tensor_scalar(out=dst[:], in0=dst[:], scalar1=off, scalar2=None, op0=ALU.add), reads=[tb], writes=[tb])
            p.op("act", lambda: A.activation(out=sn[:], in_=sn[:], func=AF.Sin), reads=[tb], writes=[tb])
            p.op("act", lambda: A.activation(out=cs[:], in_=cs[:], func=AF.Sin), reads=[tb], writes=[tb])
            p.op("dve", lambda: V.tensor_tensor(out=Pr[:], in0=mag[:], in1=cs[:], op=ALU.mult), reads=[tb], writes=[tb])
            p.op("dve", lambda: V.tensor_tensor(out=Pi[:], in0=mag[:], in1=sn[:], op=ALU.mult), reads=[tb], writes=[tb])
            p.op("dve", lambda: V.scalar_tensor_tensor(out=Ni[:], in0=Nr[:], scalar=-1.0, in1=sn[:], op0=ALU.mult, op1=ALU.mult), reads=[tb], writes=[tb])
            p.op("dve", lambda: V.tensor_tensor(out=Nr[:], in0=Nr[:], in1=cs[:], op=ALU.mult), reads=[tb], writes=[tb])
            P1r, P1i = Pr[:, :, 1], Pi[:, :, 1]
            vt = lambda o, a, b, op: p.op("dve", lambda: V.tensor_tensor(out=o, in0=a, in1=b, op=op), reads=[tb], writes=[tb])
            vt(lam2, lamr, lamr, ALU.mult); vt(tA, lami, lami, ALU.mult); vt(lam2, lam2, tA, ALU.add)
            p.op("dve", lambda: V.reciprocal(out=lam2, in_=lam2), reads=[tb], writes=[tb])
            p.op("dve", lambda: V.tensor_scalar(out=tB, in0=P1r, scalar1=-1.0, scalar2=None, op0=ALU.add), reads=[tb], writes=[tb])
            vt(nr, tB, lamr, ALU.mult); vt(tA, P1i, lami, ALU.mult); vt(nr, nr, tA, ALU.add)
            vt(ni, P1i, lamr, ALU.mult); vt(tA, tB, lami, ALU.mult); vt(ni, ni, tA, ALU.subtract)
            vt(cr, nr, lam2, ALU.mult); vt(ci, ni, lam2, ALU.mult)
            Bbr = p.sb("Bbr", [128, 24, 16]); Bbi = p.sb("Bbi", [128, 24, 16]); tq = p.sb("tq", [128, 24, 16])
            Ba = p.sb("Ba", [128, 24, 16]); Bb = p.sb("Bb", [128, 24, 16]); Ca = p.sb("Ca", [128, 24, 16]); Cb = p.sb("Cb", [128, 24, 16])
            sh16 = [128, 24, 16]
            vt(Bbr[:], Br[:], b2(cr, sh16), ALU.mult); vt(tq[:], Bi[:], b2(ci, sh16), ALU.mult); vt(Bbr[:], Bbr[:], tq[:], ALU.subtract)
            vt(Bbi[:], Bi[:], b2(cr, sh16), ALU.mult); vt(tq[:], Br[:], b2(ci, sh16), ALU.mult); vt(Bbi[:], Bbi[:], tq[:], ALU.add)
            top, bot = slice(0, 64), slice(64, 128)
            neg = lambda o, a: p.op("dve", lambda: V.tensor_scalar(out=o, in0=a, scalar1=-1.0, scalar2=None, op0=ALU.mult), reads=[tb], writes=[tb])
            cp = lambda o, a: p.op("dve", lambda: V.tensor_copy(out=o, in_=a), reads=[tb], writes=[tb])
            cp(Ba[top], Bbr[top]); cp(Ba[bot], Bbi[bot]); neg(Bb[top], Bbi[top]); cp(Bb[bot], Bbr[bot])
            cp(Ca[top], Cr[top]); neg(Ca[bot], Ci[bot]); neg(Cb[top], Ci[top]); neg(Cb[bot], Cr[bot])
            JT = p.sb("JT", [128, 128])
            p.op("pool", lambda: G.memset(JT[:], 0.0), writes=[tb])
            p.op("pool", lambda: G.affine_select(out=JT[:], in_=JT[:], pattern=[[-1, 128]], compare_op=ALU.not_equal, fill=-1.0, base=-64, channel_multiplier=1), reads=[tb], writes=[tb])
            p.op("pool", lambda: G.affine_select(out=JT[:], in_=JT[:], pattern=[[1, 128]], compare_op=ALU.not_equal, fill=1.0, base=-64, channel_multiplier=-1), reads=[tb], writes=[tb])
            MK = p.sb("MK", [128, 4, 32, 16], BF16)
            p.op("pool", lambda: G.memset(MK[:], 1.0), writes=[tb])
            p.op("pool", lambda: G.affine_select(out=MK[:], in_=MK[:], pattern=[[-128, 4], [16, 32], [0, 16]], compare_op=ALU.is_ge, fill=0.0, base=15, channel_multiplier=-1), reads=[tb], writes=[tb])
            UT = p.sb("UT", [128, 24, 4, NCH], BF16)
            Ug = [p.sb("Ug", [NCH, 512], BF16) for _ in range(2)]
            for g in range(24):
                u_ = Ug[g % 2]; uk = "Ug%d" % (g % 2)
                p.op("pool", lambda g=g, u_=u_: G.tensor_copy(out=u_[:].rearrange("n (s h) -> n s h", h=16), in_=Un[:, :, g * 16:(g + 1) * 16]), reads=["Un"], writes=[uk])
                ps, pk = ps_next(K)

                def emit(ps=ps, u_=u_):
                    for kt in range(4):
                        inst = T.matmul(ps[:, kt * NCH:(kt + 1) * NCH], lhsT=u_[:, kt * 128:(kt + 1) * 128], rhs=K.identb[0:NCH, 0:NCH], start=True, stop=True)
                    return inst
                p.op("pe", emit, reads=[uk, "identb"], writes=[pk])
                p.op("act", lambda ps=ps, g=g: A.copy(out=UT[:, g, :, :], in_=ps[:, 0:4 * NCH].rearrange("q (k n) -> q k n", k=4)), reads=[pk], writes=["UT"])
            NGS = 3
            BsR = [p.sb("BsR", [128, 32, 16]) for _ in range(NGS)]; t1s = [p.sb("t1", [128, 33, 16]) for _ in range(NGS)]
            CLr = [p.sb("CLr", [128, 33, 16]) for _ in range(NGS)]
            BsRT = [p.sb("BsRT", [128, 4, 128], BF16) for _ in range(2)]
            Sa = p.sb("Sa", [128, 24, NCH]); Sbb = p.sb("Sbb", [128, 24, NCH])
            sh_b = [128, 32, 16]; sh_c = [128, 33, 16]

            def make_bsr(g, sl_=None):
                sl_ = g % 2 if sl_ is None else sl_
                t1 = t1s[sl_]; t1k = "t1_%d" % sl_
                o = BsR[sl_]; ok = "BsR%d" % sl_
                p.op("dve", lambda: V.tensor_tensor(out=o[:], in0=b2(Nr[:, g, 0:32], sh_b), in1=b1(Ba[:, g, :], sh_b), op=ALU.mult), reads=[tb], writes=[ok])
                p.op("pool", lambda: G.tensor_tensor(out=t1[:, 0:32, :], in0=b2(Ni[:, g, 0:32], sh_b), in1=b1(Bb[:, g, :], sh_b), op=ALU.mult), reads=[tb], writes=[t1k])
                p.op("dve", lambda: V.tensor_tensor(out=o[:], in0=o[:], in1=t1[:, 0:32, :], op=ALU.add), reads=[ok, t1k], writes=[ok])
                return o, ok

            def make_clr(g, sl_=None):
                sl_ = g % 2 if sl_ is None else sl_
                t1 = t1s[sl_]; t1k = "t1_%d" % sl_
                o = CLr[sl_]; ok = "CLr%d" % sl_
                p.op("dve", lambda: V.tensor_tensor(out=o[:], in0=b2(Pr[:, g, :], sh_c), in1=b1(Ca[:, g, :], sh_c), op=ALU.mult), reads=[tb], writes=[ok])
                p.op("pool", lambda: G.tensor_tensor(out=t1[:], in0=b2(Pi[:, g, :], sh_c), in1=b1(Cb[:, g, :], sh_c), op=ALU.mult), reads=[tb], writes=[t1k])
                p.op("dve", lambda: V.tensor_tensor(out=o[:], in0=o[:], in1=t1[:], op=ALU.add), reads=[ok, t1k], writes=[ok])
                return o, ok
            for g in range(24):
                o, ok = make_bsr(g)
                of = o[:].rearrange("q s h -> q (s h)")
                ps, pk = ps_next(K)

                def emit(ps=ps, of=of):
                    for kt in range(4):
                        inst = T.matmul(ps[:, kt * 128:(kt + 1) * 128], lhsT=of[:, kt * 128:(kt + 1) * 128], rhs=K.identf[:], start=True, stop=True)
                    return inst
                p.op("pe", emit, reads=[ok, "identf"], writes=[pk])
                bt = BsRT[g % 2]; btk = "BsRT%d" % (g % 2)
                p.op("act", lambda ps=ps, bt=bt: A.copy(out=bt[:].rearrange("q k n -> q (k n)"), in_=ps[:]), reads=[pk], writes=[btk])
                ps2, pk2 = ps_next(K)

                def emit2(ps2=ps2, bt=bt, g=g):
                    for kt in range(4):
                        inst = T.matmul(ps2[:, 0:NCH], lhsT=bt[:, kt, :], rhs=UT[:, g, kt, :], start=(kt == 0), stop=(kt == 3))
                    return inst
                p.op("pe", emit2, reads=[btk, "UT"], writes=[pk2])
                p.op("act", lambda ps2=ps2, g=g: A.copy(out=Sa[:, g, :], in_=ps2[:, 0:NCH]), reads=[pk2], writes=["Sa"])
            gb = max(d for d in (1, 2, 3, 4, 6, 8, 12, 24) if d * NCH <= 512)
            for g0 in range(0, 24, gb):
                ps, pk = ps_next(K)
                p.op("pe", lambda ps=ps, g0=g0: T.matmul(ps[:, 0:gb * NCH], lhsT=JT[:], rhs=Sa[:, g0:g0 + gb, :].rearrange("q g n -> q (g n)"), start=True, stop=True), reads=["Sa", tb], writes=[pk])
                p.op("act", lambda ps=ps, g0=g0: A.copy(out=Sbb[:, g0:g0 + gb, :].rearrange("q g n -> q (g n)"), in_=ps[:, 0:gb * NCH]), reads=[pk], writes=["Sbb"])
            shn = [128, 24, NCH]
            with p.scope():
                tS = p.sb("tS", [128, 24, NCH]); tS2 = p.sb("tS2", [128, 24, NCH])
                a31r, a31i, a32r, a32i = Pr[:, :, 31], Pi[:, :, 31], Pr[:, :, 32], Pi[:, :, 32]
                p.op("dve", lambda: V.tensor_tensor(out=tS[:], in0=Sa[:], in1=b2(a31i, shn), op=ALU.mult), reads=["Sa", tb], writes=["tS"])
                p.op("pool", lambda: G.tensor_tensor(out=tS2[:], in0=Sbb[:], in1=b2(a31i, shn), op=ALU.mult), reads=["Sbb", tb], writes=["tS2"])
                p.op("dve", lambda: V.tensor_tensor(out=Sa[:], in0=Sa[:], in1=b2(a31r, shn), op=ALU.mult), reads=["Sa", "tS"], writes=["Sa"])
                p.op("pool", lambda: G.tensor_tensor(out=Sbb[:], in0=Sbb[:], in1=b2(a31r, shn), op=ALU.mult), reads=["Sbb", "tS2"], writes=["Sbb"])
                p.op("dve", lambda: V.tensor_tensor(out=Sa[:], in0=Sa[:], in1=tS2[:], op=ALU.add), reads=["Sa", "tS2"], writes=["Sa"])
                p.op("pool", lambda: G.tensor_tensor(out=Sbb[:], in0=Sbb[:], in1=tS[:], op=ALU.subtract), reads=["Sbb", "tS"], writes=["Sbb"])
            Hab = p.sb("Hab", [128, 24, NCH], BF16)
            ha = [p.sb("ha", [128, 24]) for _ in range(2)]; hbb = [p.sb("hbb", [128, 24]) for _ in range(2)]
            u1 = p.sb("u1", [128, 24]); u2 = p.sb("u2", [128, 24]); u3 = p.sb("u3", [128, 24]); u4 = p.sb("u4", [128, 24])
            p.op("pool", lambda: G.memset(Hab[:], 0.0), writes=["Hab"])
            p.op("pool", lambda: G.memset(ha[0][:], 0.0), writes=["ha0"])
            p.op("pool", lambda: G.memset(hbb[0][:], 0.0), writes=["hb0"])
            for n in range(NCH - 1):
                c_, n_ = n % 2, (n + 1) % 2
                hak, hbk, hak2, hbk2 = "ha%d" % c_, "hb%d" % c_, "ha%d" % n_, "hb%d" % n_
                p.op("dve", lambda c_=c_: V.tensor_tensor(out=u1[:], in0=ha[c_][:], in1=a32r, op=ALU.mult), reads=[hak, tb], writes=["u1"])
                p.op("dve", lambda c_=c_: V.tensor_tensor(out=u2[:], in0=hbb[c_][:], in1=a32i, op=ALU.mult), reads=[hbk, tb], writes=["u2"])
                p.op("pool", lambda c_=c_: G.tensor_tensor(out=u3[:], in0=hbb[c_][:], in1=a32r, op=ALU.mult), reads=[hbk, tb], writes=["u3"])
                p.op("pool", lambda c_=c_: G.tensor_tensor(out=u4[:], in0=ha[c_][:], in1=a32i, op=ALU.mult), reads=[hak, tb], writes=["u4"])
                p.op("dve", lambda: V.tensor_tensor(out=u1[:], in0=u1[:], in1=u2[:], op=ALU.add), reads=["u1", "u2"], writes=["u1"])
                p.op("pool", lambda: G.tensor_tensor(out=u3[:], in0=u3[:], in1=u4[:], op=ALU.subtract), reads=["u3", "u4"], writes=["u3"])
                p.op("dve", lambda n=n, n_=n_: V.tensor_tensor(out=ha[n_][:], in0=u1[:], in1=Sa[:, :, n], op=ALU.add), reads=["u1", "Sa"], writes=[hak2])
                p.op("pool", lambda n=n, n_=n_: G.tensor_tensor(out=hbb[n_][:], in0=u3[:], in1=Sbb[:, :, n], op=ALU.add), reads=["u3", "Sbb"], writes=[hbk2])
                p.op("act", lambda n=n, n_=n_: A.copy(out=Hab[:, :, n + 1], in_=ha[n_][:]), reads=[hak2], writes=["Hab"])
            Tb = [p.sb("Tb", [128, 4, 512], BF16) for _ in range(NGS)]
            CL1 = [p.sb("CL1", [128, 512], BF16) for _ in range(NGS)]
            tmpus = [p.sb("tmpu", [NCH, 32, 16]) for _ in range(NGS)]
            def grp_gen(g, sl_):
                tmpu = tmpus[sl_]; tuk = "tmpu%d" % sl_
                o, ok = make_bsr(g, sl_)
                c, ck = make_clr(g, sl_)
                yield
                of = o[:].rearrange("q s h -> q (s h)")
                cfl = c[:].rearrange("q s h -> q (s h)")
                tb_ = Tb[sl_]; tbk = "Tb%d" % sl_
                for kt in range(4):
                    ps, pk = ps_next(K)
                    p.op("pe", lambda ps=ps, kt=kt, of=of, cfl=cfl: T.matmul(ps[:], lhsT=of[:, kt * 128:(kt + 1) * 128], rhs=cfl[:, 0:512], start=True, stop=True), reads=[ok, ck], writes=[pk])
                    p.op("dve", lambda ps=ps, kt=kt, tb_=tb_: V.tensor_tensor(out=tb_[:, kt, :], in0=ps[:], in1=MK[:, kt].rearrange("q t h -> q (t h)"), op=ALU.mult), reads=[pk, tb], writes=[tbk])
                c1 = CL1[sl_]; c1k = "CL1%d" % sl_
                p.op("act", lambda c1=c1, cfl=cfl: A.copy(out=c1[:], in_=cfl[:, 16:528]), reads=[ck], writes=[c1k])
                yield
                ps, pk = ps_next(K)

                def emit(ps=ps, tb_=tb_, c1=c1, g=g):
                    for kt in range(4):
                        T.matmul(ps[0:NCH, :], lhsT=UT[:, g, kt, :], rhs=tb_[:, kt, :], start=(kt == 0), stop=False)
                    return T.matmul(ps[0:NCH, :], lhsT=Hab[:, g, :], rhs=c1[:], start=False, stop=True)
                p.op("pe", emit, reads=["UT", tbk, c1k, "Hab"], writes=[pk])
                ug = Un[:, :, g * 16:(g + 1) * 16]
                p.op("pool", lambda ug=ug, g=g: G.tensor_tensor(out=tmpu[:], in0=ug, in1=b1(dB[0:NCH, g * 16:(g + 1) * 16], [NCH, 32, 16]), op=ALU.mult), reads=["Un", tb], writes=[tuk])
                p.op("dve", lambda ug=ug, ps=ps: V.tensor_tensor(out=ug, in0=ps[0:NCH, :].rearrange("n (t h) -> n t h", h=16), in1=tmpu[:], op=ALU.add), reads=[pk, tuk], writes=["Un"])
            for g0 in range(0, 24, NGS):
                gens = [grp_gen(g, g - g0) for g in range(g0, min(24, g0 + NGS))]
                while gens:
                    for g_ in list(gens):
                        try:
                            next(g_)
                        except StopIteration:
                            gens.remove(g_)
        for j in range(4):
            p.op("act", lambda j=j: A.activation(out=Un[:, j * 8:(j + 1) * 8, :], in_=Un[:, j * 8:(j + 1) * 8, :], func=AF.Gelu_apprx_tanh), reads=["Un"], writes=["Un"])
        y1T = p.sb("y1T", [128, 3, S], BF16)
        for t in range(32):
            ps, pk = ps_next(K)

            def emit(ps=ps, t=t):
                for c3 in range(3):
                    inst = T.matmul(ps[:, c3 * NCH:(c3 + 1) * NCH], lhsT=Un[:, t, c3 * 128:(c3 + 1) * 128], rhs=K.identb[0:NCH, 0:NCH], start=True, stop=True)
                return inst
            p.op("pe", emit, reads=["Un", "identb"], writes=[pk])
            dst = y1T[:].rearrange("q c (n t) -> q c n t", t=32)[:, :, :, t]
            eng = "act" if t % 2 == 0 else "dve"
            if eng == "act":
                p.op("act", lambda ps=ps, dst=dst: A.copy(out=dst, in_=ps[:, 0:3 * NCH].rearrange("q (c n) -> q c n", c=3)), reads=[pk], writes=["y1T"])
            else:
                p.op("dve", lambda ps=ps, dst=dst: V.tensor_copy(out=dst, in_=ps[:, 0:3 * NCH].rearrange("q (c n) -> q c n", c=3)), reads=[pk], writes=["y1T"])
        Wg = p.sb("Wglu", [128, 3, 384], BF16)
        p.dma("pool", [(Wg[:], K.inp["s5_glu_w"][l].rearrange("(c q) j -> q c j", q=128))], writes=["Wglu"])
        gbv = p.sb("gbv", [128, 3])
        p.dma("sp", [(gbv[:], K.inp["s5_glub"][l])], writes=["gbv"])
        sg = [p.sb("sg", [128, TT]) for _ in range(2)]
        ya = [p.sb("ya", [128, 3, TT], BF16) for _ in range(2)]
        for t in range(S // TT):
            y = ya[t % 2]; yk = "ya%d" % (t % 2)
            for jc in range(3):
                ps, pk = ps_next(K)

                def emit(ps=ps, jc=jc, t=t):
                    for kc in range(3):
                        inst = T.matmul(ps[:], lhsT=Wg[:, kc, jc * 128:(jc + 1) * 128], rhs=y1T[:, kc, t * TT:(t + 1) * TT], start=(kc == 0), stop=(kc == 2))
                    return inst
                p.op("pe", emit, reads=["Wglu", "y1T"], writes=[pk])
                s_ = sg[jc % 2]; sk = "sg%d" % (jc % 2)
                p.op("act", lambda ps=ps, s_=s_, jc=jc: A.activation(out=s_[:], in_=ps[:], func=AF.Sigmoid, bias=gbv[:, jc:jc + 1]), reads=[pk, "gbv"], writes=[sk])
                p.op("dve", lambda s_=s_, jc=jc, t=t, y=y: V.tensor_tensor(out=y[:, jc, :], in0=y1T[:, jc, t * TT:(t + 1) * TT], in1=s_[:], op=ALU.mult), reads=[sk, "y1T"], writes=[yk])
            p.dma("sp", [(K.Y[YCH["a"]:YCH["a"] + 3, :, t * TT:(t + 1) * TT].rearrange("c p t -> p c t"), y[:])], reads=[yk], writes=["Y"])
```

```python
import contextlib
import numpy as np
import concourse.bass as bass
import concourse.mybir as mybir
from concourse.bass_utils import run_bass_kernel_spmd

AF = mybir.ActivationFunctionType
ALU = mybir.AluOpType
F32 = mybir.dt.float32
BF16 = mybir.dt.bfloat16

D = 1024
NIN = 8848
NEG = -30000.0
TT = 512
C_Q, C_K, C_V, C_XS, C_BS, C_CS, C_XL = 0, 512, 1024, 1536, 2048, 2176, 2304
C_U, C_BG, C_AG, C_ZG, C_ZS, C_DT, C_GL, C_GM = 2816, 3200, 3204, 3208, 3720, 4232, 4240, 4752
YCH = {"a": 0, "b": 3, "c": 7, "d": 11}


class Prog:
    N_DMA_SEMS = 40

    def __init__(self, nc):
        self.nc = nc
        self.es = contextlib.ExitStack()
        self.eng = {"pe": nc.tensor, "act": nc.scalar, "dve": nc.vector, "pool": nc.gpsimd, "sp": nc.sync}
        self.sem = {e: self.es.enter_context(nc.semaphore("s_" + e)) for e in self.eng}
        self.cnt = {e: 0 for e in self.eng}
        self.waited = {e: {} for e in self.eng}
        self.dsem = [self.es.enter_context(nc.semaphore("d%d" % i)) for i in range(self.N_DMA_SEMS)]
        self.dtot = [0] * self.N_DMA_SEMS
        self.dnext = 0
        self.res = {}
        self.stack = [self.es]
        self.uid = 0

    def sb(self, name, shape, dt=F32):
        self.uid += 1
        return self.stack[-1].enter_context(self.nc.sbuf_tensor("%s_%d" % (name, self.uid), list(shape), dt))

    def ps(self, name, shape, dt=F32):
        return self.es.enter_context(self.nc.psum_tensor(name, list(shape), dt))

    @contextlib.contextmanager
    def scope(self):
        st = contextlib.ExitStack()
        self.stack.append(st)
        try:
            yield
        finally:
            self.barrier()
            self.stack.pop()
            st.close()

    def _wait(self, e, ev):
        sem, val, src = ev
        if src == e and e == "pe":
            return
        w = self.waited[e]
        if w.get(sem.name, 0) >= val:
            return
        self.eng[e].wait_ge(sem, val)
        w[sem.name] = val

    def _deps(self, e, reads, writes):
        for k in reads:
            r = self.res.get(k)
            if r and r[0] is not None:
                self._wait(e, r[0])
        for k in writes:
            r = self.res.get(k)
            if r:
                if r[0] is not None:
                    self._wait(e, r[0])
                for ev in r[1]:
                    self._wait(e, ev)

    def _commit(self, ev, reads, writes):
        for k in reads:
            r = self.res.setdefault(k, [None, []])
            r[1].append(ev)
            if len(r[1]) > 12:
                r[1] = r[1][-12:] if False else r[1]
        for k in writes:
            self.res[k] = [ev, []]

    def op(self, e, emit, reads=(), writes=()):
        self._deps(e, reads, writes)
        inst = emit()
        self.cnt[e] += 1
        inst.then_inc(self.sem[e], 1)
        ev = (self.sem[e], self.cnt[e], e)
        self._commit(ev, reads, writes)
        return ev

    def dma(self, q, pairs, reads=(), writes=()):
        self._deps(q, reads, writes)
        k = self.dnext
        self.dnext = (self.dnext + 1) % self.N_DMA_SEMS
        sem = self.dsem[k]
        if self.dtot[k] > 0:
            self._wait(q, (sem, self.dtot[k], None))
        for (o, i) in pairs:
            self.eng[q].dma_start(out=o, in_=i).then_inc(sem, 16)
            self.dtot[k] += 16
        ev = (sem, self.dtot[k], None)
        self._commit(ev, reads, writes)
        return ev

    def barrier(self):
        for e in self.eng:
            for e2 in self.eng:
                if e2 != e and self.cnt[e2] > 0:
                    self._wait(e, (self.sem[e2], self.cnt[e2], e2))
            for k in range(self.N_DMA_SEMS):
                if self.dtot[k] > 0:
                    self._wait(e, (self.dsem[k], self.dtot[k], None))
        self.res = {}


class Ctx:
    pass


def bc(ap, shape, axis):
    return ap.unsqueeze(axis).to_broadcast(list(shape))


def ps_next(K):
    K.ps_i = (K.ps_i + 1) % len(K.psb)
    return K.psb[K.ps_i], "ps%d" % K.ps_i


def load_w(K, dst, dkey, src2d, c0, c1, q="pool", first=False):
    v = src2d.rearrange("(kc p) c -> p kc c", p=128)
    K.p.dma(q, [(dst, v[:, :, c0:c1])], writes=[dkey])


def load_hT(K, t, TT=TT):
    b = K.hbuf[t % 2]
    key = "hbuf%d" % (t % 2)
    K.p.dma("sp", [(b[:], K.hT[:, :, t * TT:(t + 1) * TT].rearrange("kc p t -> p kc t"))], reads=["hT"], writes=[key])
    return b, key


def proj(K, W, wkey, c0, n, hb, hkey, N=TT):
    ps, pk = ps_next(K)
    nc = K.nc

    def emit():
        for kc in range(8):
            inst = nc.tensor.matmul(ps[0:n, 0:N], lhsT=W[:, kc, c0:c0 + n], rhs=hb[:, kc, :], start=(kc == 0), stop=(kc == 7))
        return inst
    K.p.op("pe", emit, reads=[wkey, hkey], writes=[pk])
    return ps, pk


def conv_chunk(K, ps, pk, Pb, pbkey, cw, cb, ci, out_ap, okey, func, n=128, cwk="cw", tails=None, tkey=None):
    nc = K.nc
    p = K.p
    p.op("act", lambda: nc.scalar.copy(out=Pb[0:n, 3:515], in_=ps[0:n, :]), reads=[pk], writes=[pbkey])
    if tails is not None:
        p.op("pool", lambda: nc.gpsimd.tensor_copy(out=Pb[0:n, 0:3], in_=tails[0:n, ci, :]), reads=[tkey, pbkey], writes=[pbkey])
    acc = K.cacc[K.cacc_i % 2]
    akey = "cacc%d" % (K.cacc_i % 2)
    K.cacc_i += 1
    p.op("dve", lambda: nc.vector.tensor_scalar(out=acc[0:n, :], in0=Pb[0:n, 3:515], scalar1=cw[0:n, ci, 3:4], scalar2=cb[0:n, ci:ci + 1],
                                                op0=ALU.mult, op1=ALU.add), reads=[pbkey, cwk], writes=[akey])
    for k in (2, 1, 0):
        p.op("dve", lambda k=k: nc.vector.scalar_tensor_tensor(out=acc[0:n, :], in0=Pb[0:n, k:k + 512], scalar=cw[0:n, ci, k:k + 1], in1=acc[0:n, :],
                                                               op0=ALU.mult, op1=ALU.add), reads=[pbkey, akey, cwk], writes=[akey])
    if tails is not None:
        p.op("pool", lambda: nc.gpsimd.tensor_copy(out=tails[0:n, ci, :], in_=Pb[0:n, 512:515]), reads=[pbkey], writes=[tkey])
    else:
        p.op("pool", lambda: nc.gpsimd.tensor_copy(out=Pb[0:n, 0:3], in_=Pb[0:n, 512:515]), reads=[pbkey], writes=[pbkey])
    p.op("act", lambda: nc.scalar.activation(out=out_ap, in_=acc[0:n, :], func=func), reads=[akey], writes=[okey])


def phase_setup(K):
    p, nc = K.p, K.nc
    G = nc.gpsimd
    K.identb = p.sb("identb", [128, 128], BF16)
    K.identf = p.sb("identf", [128, 128], F32)
    K.onesb = p.sb("onesb", [128, 128], BF16)
    K.onesf = p.sb("onesf", [128, 128], F32)
    for t, k in ((K.identb, "identb"), (K.identf, "identf")):
        p.op("pool", lambda t=t: G.memset(t[:], 1.0), writes=[k])
        p.op("pool", lambda t=t: G.affine_select(out=t[:], in_=t[:], pattern=[[-1, 128]], compare_op=ALU.is_equal, fill=0.0, base=0, channel_multiplier=1), reads=[k], writes=[k])
    p.op("pool", lambda: G.memset(K.onesb[:], 1.0), writes=["onesb"])
    p.op("pool", lambda: G.memset(K.onesf[:], 1.0), writes=["onesf"])
    K.psb = [p.ps("psb%d" % i, [128, 512], F32) for i in range(7)]
    K.ps_i = 0
    K.pst = p.ps("pst", [128, 1024], BF16)
    K.cacc = [p.sb("cacc", [128, 512]) for _ in range(2)]
    K.cacc_i = 0
    K.modT = [p.sb("modT", [128, 48]) for _ in range(K.depth)]
    K.gscm = [p.sb("gscm", [128, 8]) for _ in range(K.depth)]
    K.gscf = [p.sb("gscf", [128, 8]) for _ in range(K.depth)]


def phase_mods(K):
    p, nc = K.p, K.nc
    with p.scope():
        cT = p.sb("cT", [128, 8])
        condT = p.sb("condT", [128, 8])
        p.dma("sp", [(cT[:], K.inp["cT"])], writes=["cT"])
        p.op("act", lambda: nc.scalar.activation(out=condT[:], in_=cT[:], func=AF.Silu), reads=["cT"], writes=["condT"])
        aw = [p.sb("aw", [128, 8, 512]) for _ in range(2)]
        adab = p.sb("adab", [128, 48])
        lng = p.sb("lng", [128, 8])
        for l in range(K.depth):
            pm, pk = ps_next(K)
            for fc in range(12):
                a = aw[fc % 2]
                ak = "aw%d" % (fc % 2)
                p.dma("sp", [(a[:], K.inp["ada_w"][l].rearrange("(kc p) f -> p kc f", p=128)[:, :, fc * 512:(fc + 1) * 512])], writes=[ak])
                for sub in range(4):
                    f = fc * 4 + sub

                    def emit(a=a, sub=sub, f=f):
                        for kc in range(8):
                            inst = nc.tensor.matmul(pm[:, f:f + 1], lhsT=a[:, kc, sub * 128:(sub + 1) * 128], rhs=condT[:, kc:kc + 1], start=(kc == 0), stop=(kc == 7))
                        return inst
                    p.op("pe", emit, reads=[ak, "condT"], writes=[pk])
            p.dma("sp", [(adab[:], K.inp["ada_bT"][l])], writes=["adab"])
            mk = "modT%d" % l
            p.op("dve", lambda l=l: nc.vector.tensor_tensor(out=K.modT[l][:], in0=pm[:, 0:48], in1=adab[:], op=ALU.add), reads=[pk, "adab"], writes=[mk])
            for (dst, dk, gname, sc0) in ((K.gscm[l], "gscm%d" % l, "lnmix", 8), (K.gscf[l], "gscf%d" % l, "lnffn", 32)):
                p.dma("sp", [(lng[:], K.inp[gname][l])], writes=["lng"])
                p.op("dve", lambda dst=dst, sc0=sc0, l=l: nc.vector.scalar_tensor_tensor(out=dst[:], in0=K.modT[l][:, sc0:sc0 + 8], scalar=1.0, in1=lng[:], op0=ALU.add, op1=ALU.mult),
                     reads=[mk, "lng"], writes=[dk])


def make_gt(K, l, g0, name):
    p, nc = K.p, K.nc
    mk = "modT%d" % l
    dst = p.sb(name, [128, 1024])
    diag = p.sb("diag", [128, 128])
    for half in range(2):
        ps, pk2 = ps_next(K)
        for jj in range(4):
            j = half * 4 + jj
            p.op("dve", lambda j=j: nc.vector.tensor_scalar(out=diag[:], in0=K.identf[:], scalar1=K.modT[l][:, g0 + j:g0 + j + 1], scalar2=None, op0=ALU.mult),
                 reads=[mk, "identf"], writes=["diag"])
            p.op("pe", lambda jj=jj, ps=ps: nc.tensor.matmul(ps[:, jj * 128:(jj + 1) * 128], lhsT=K.onesf[:], rhs=diag[:], start=True, stop=True),
                 reads=["diag", "onesf"], writes=[pk2])
        p.op("act", lambda half=half, ps=ps: nc.scalar.copy(out=dst[:, half * 512:(half + 1) * 512], in_=ps[:]), reads=[pk2], writes=[name])
    return dst


def phase_norm(K, l, xsrc, gsc, gsck, sh0):
    p, nc = K.p, K.nc
    mk = "modT%d" % l
    with p.scope():
        xt = [p.sb("xt", [128, 1024]) for _ in range(2)]
        junk2 = [p.sb("junk", [128, 1024], BF16) for _ in range(2)]
        xs = [p.sb("xs", [128, 1024], BF16) for _ in range(2)]
        ss2 = [p.sb("ss", [128, 4]) for _ in range(2)]
        tmp2 = [p.sb("tmp", [128, 8, 128]) for _ in range(2)]
        hto = [p.sb("hto", [128, 8, TT], BF16) for _ in range(2)]
        for i in range(K.S // 128):
            x_ = xt[i % 2]
            ss = ss2[i % 2]; junk = junk2[i % 2]; tmp = tmp2[i % 2]
            ssk = "ss%d" % (i % 2); jk = "junk%d" % (i % 2); tk_ = "tmp%d" % (i % 2)
            xk = "xt%d" % (i % 2)
            if i == 0:
                p.dma("sp", [(x_[:], xsrc[0:128, :])], reads=["xsrc"], writes=[xk])
            if i + 1 < K.S // 128:
                p.dma("sp", [(xt[(i + 1) % 2][:], xsrc[(i + 1) * 128:(i + 2) * 128, :])], reads=["xsrc"], writes=["xt%d" % ((i + 1) % 2)])
            p.op("act", lambda x_=x_, junk=junk, ss=ss: nc.scalar.activation(out=junk[:], in_=x_[:], func=AF.Square, accum_out=ss[:, 0:1]), reads=[xk], writes=[jk, ssk])
            p.op("act", lambda ss=ss: nc.scalar.activation(out=ss[:, 1:2], in_=ss[:, 0:1], func=AF.Sqrt, scale=1.0 / D, bias=1e-6), reads=[ssk], writes=[ssk])
            p.op("dve", lambda ss=ss: nc.vector.reciprocal(out=ss[:, 2:3], in_=ss[:, 1:2]), reads=[ssk], writes=[ssk])
            xs_ = xs[i % 2]
            xsk = "xs%d" % (i % 2)
            p.op("act", lambda x_=x_, xs_=xs_, ss=ss: nc.scalar.activation(out=xs_[:], in_=x_[:], func=AF.Copy, scale=ss[:, 2:3]), reads=[xk, ssk], writes=[xsk])

            pstv = K.pst[:] if i % 2 == 0 else K.psb[6][:].bitcast(BF16)
            pstk = "pst" if i % 2 == 0 else "ps6"

            def emit(xs_=xs_, pstv=pstv):
                for j in range(8):
                    inst = nc.tensor.transpose(pstv[:, j * 128:(j + 1) * 128], xs_[:, j * 128:(j + 1) * 128], K.identb[:])
                return inst
            p.op("pe", emit, reads=[xsk, "identb"], writes=[pstk])
            ho = hto[(i // 4) % 2]
            hk = "hto%d" % ((i // 4) % 2)
            p.op("dve", lambda tmp=tmp, pstv=pstv: nc.vector.tensor_tensor(out=tmp[:], in0=pstv.rearrange("p (j t) -> p j t", j=8), in1=bc(gsc[:], [128, 8, 128], 2), op=ALU.mult),
                 reads=[pstk, gsck], writes=[tk_])
            p.op("pool", lambda ho=ho, i=i, tmp=tmp: nc.gpsimd.tensor_tensor(out=ho[:, :, (i % 4) * 128:(i % 4 + 1) * 128], in0=tmp[:], in1=bc(K.modT[l][:, sh0:sh0 + 8], [128, 8, 128], 2), op=ALU.add),
                 reads=[tk_, mk], writes=[hk])
            if i % 4 == 3:
                t = i // 4
                p.dma("sp", [(K.hT[:, :, t * TT:(t + 1) * TT].rearrange("kc p t -> p kc t"), ho[:])], reads=[hk], writes=["hT"])


def phase_lru(K, l):
    p, nc = K.p, K.nc
    A, V, G = nc.scalar, nc.vector, nc.gpsimd
    with p.scope():
        W = p.sb("W", [128, 8, 1024], BF16)
        load_w(K, W[:, :, 0:512], "W", K.inp["w_in"][l], C_XL, C_XL + 512)
        load_w(K, W[:, :, 512:1024], "W", K.inp["w_in"][l], C_GL, C_GL + 512)
        wbd = {}
        for nm in ("wr", "wi"):
            t = p.sb(nm, [128, 4, 128], BF16)
            p.op("pool", lambda t=t: G.memset(t[:], 0.0), writes=[nm])
            pairs = []
            for c in range(4):
                for j in range(2):
                    pairs.append((t[j * 64:(j + 1) * 64, c, j * 64:(j + 1) * 64], K.inp["lru_" + nm][l, 2 * c + j]))
            p.dma("pool", pairs, writes=[nm])
            wbd[nm] = t
        sm = p.sb("sm", [128, 3, 4])
        p.dma("sp", [(sm[:, 0, :], K.inp["lru_br"][l]), (sm[:, 1, :], K.inp["lru_bi"][l]), (sm[:, 2, :], K.inp["lru_lambda"][l])], writes=["sm"])
        cA = p.sb("cA", [128, 3, 4])
        p.op("act", lambda: A.activation(out=cA[:, 0, :], in_=sm[:, 2, :], func=AF.Exp, scale=-1.0), reads=["sm"], writes=["cA"])
        p.op("act", lambda: A.activation(out=cA[:, 0, :], in_=cA[:, 0, :], func=AF.Ln, bias=1.0), reads=["cA"], writes=["cA"])
        p.op("dve", lambda: V.tensor_scalar(out=cA[:, 1, :], in0=cA[:, 0, :], scalar1=-8.0, scalar2=None, op0=ALU.mult), reads=["cA"], writes=["cA"])
        p.op("dve", lambda: V.tensor_scalar(out=cA[:, 2, :], in0=cA[:, 0, :], scalar1=-16.0, scalar2=None, op0=ALU.mult), reads=["cA"], writes=["cA"])
        cw = p.sb("cw", [128, 4, 4])
        cb = p.sb("cb", [128, 4])
        p.dma("sp", [(cw[:], K.inp["conv_w4"][l, :, 18:22, :]), (cb[:], K.inp["conv_b"][l, :, 18:22])], writes=["cw"])
        Pb = [p.sb("Pb", [128, 515]) for _ in range(4)]
        for c in range(4):
            p.op("pool", lambda c=c: G.memset(Pb[c][:], 0.0), writes=["Pb%d" % c])
        carry = p.sb("carry", [128, 4])
        p.op("pool", lambda: G.memset(carry[:], 0.0), writes=["carry"])
        B4 = lambda n, dt=F32: p.sb(n, [128, 4, TT], dt)
        xl, xlb, r, ig, a, a2, bb, hh, gate = B4("xl"), B4("xlb", BF16), B4("r"), B4("ig"), B4("a"), B4("a2"), B4("bb"), B4("hh"), B4("gate")
        yT = [B4("yT", BF16) for _ in range(2)]
        K.hbuf = [p.sb("hbuf", [128, 8, TT], BF16) for _ in range(2)]
        for t in range(K.S // TT):
            hb, hk = load_hT(K, t)
            for c in range(4):
                ps, pk = proj(K, W, "W", c * 128, 128, hb, hk)
                conv_chunk(K, ps, pk, Pb[c], "Pb%d" % c, cw, cb, c, xl[:, c, :], "xl%d" % c, AF.Identity)
                p.op("pool", lambda c=c: G.tensor_copy(out=xlb[:, c, :], in_=xl[:, c, :]), reads=["xl%d" % c], writes=["xlb%d" % c])
            for (nm, dst, bi_) in (("wr", r, 0), ("wi", ig, 1)):
                for c in range(4):
                    ps, pk = ps_next(K)
                    p.op("pe", lambda c=c, ps=ps, nm=nm: nc.tensor.matmul(ps[:], lhsT=wbd[nm][:, c, :], rhs=xlb[:, c, :], start=True, stop=True), reads=[nm, "xlb%d" % c], writes=[pk])
                    p.op("act", lambda c=c, ps=ps, dst=dst, bi_=bi_: A.activation(out=dst[:, c, :], in_=ps[:], func=AF.Sigmoid, bias=sm[:, bi_, c:c + 1]), reads=[pk, "sm"], writes=["%s_%d" % (nm, c)])
            for c in range(4):
                p.op("act", lambda c=c: A.activation(out=a[:, c, :], in_=r[:, c, :], func=AF.Exp, scale=cA[:, 1, c:c + 1]), reads=["wr_%d" % c, "cA"], writes=["a%d" % c])
                p.op("act", lambda c=c: A.activation(out=a2[:, c, :], in_=r[:, c, :], func=AF.Exp, scale=cA[:, 2, c:c + 1]), reads=["wr_%d" % c, "cA"], writes=["a2%d" % c])
            for c in range(4):
                p.op("act", lambda c=c: A.activation(out=a2[:, c, :], in_=a2[:, c, :], func=AF.Relu, scale=-1.0, bias=1.0), reads=["a2%d" % c], writes=["a2%d" % c])
                p.op("act", lambda c=c: A.activation(out=a2[:, c, :], in_=a2[:, c, :], func=AF.Sqrt), reads=["a2%d" % c], writes=["a2%d" % c])
            for c in range(4):
                bk = "bb%d" % c
                p.op("dve", lambda c=c: V.tensor_tensor(out=bb[:, c, :], in0=ig[:, c, :], in1=xl[:, c, :], op=ALU.mult), reads=["wi_%d" % c, "xl%d" % c], writes=[bk])
                if t == 0:
                    p.op("dve", lambda c=c: V.tensor_tensor(out=bb[:, c, 1:TT], in0=bb[:, c, 1:TT], in1=a2[:, c, 1:TT], op=ALU.mult), reads=[bk, "a2%d" % c], writes=[bk])
                else:
                    p.op("dve", lambda c=c: V.tensor_tensor(out=bb[:, c, :], in0=bb[:, c, :], in1=a2[:, c, :], op=ALU.mult), reads=[bk, "a2%d" % c], writes=[bk])
                p.op("dve", lambda c=c: V.tensor_tensor_scan(out=hh[:, c, :], data0=a[:, c, :], data1=bb[:, c, :], initial=carry[:, c:c + 1], op0=ALU.mult, op1=ALU.add),
                     reads=["a%d" % c, bk, "carry"], writes=["hh%d" % c])
                p.op("act", lambda c=c: A.copy(out=carry[:, c:c + 1], in_=hh[:, c, TT - 1:TT]), reads=["hh%d" % c], writes=["carry"])
            y = yT[t % 2]
            yk = "yT%d" % (t % 2)
            for c in range(4):
                ps, pk = proj(K, W, "W", 512 + c * 128, 128, hb, hk)
                p.op("act", lambda c=c, ps=ps: A.activation(out=gate[:, c, :], in_=ps[:], func=AF.Gelu_apprx_tanh), reads=[pk], writes=["gate%d" % c])
                p.op("dve", lambda c=c, y=y: V.tensor_tensor(out=y[:, c, :], in0=hh[:, c, :], in1=gate[:, c, :], op=ALU.mult), reads=["hh%d" % c, "gate%d" % c], writes=[yk])
            p.dma("sp", [(K.Y[YCH["d"]:YCH["d"] + 4, :, t * TT:(t + 1) * TT].rearrange("c p t -> p c t"), y[:])], reads=[yk], writes=["Y"])


def phase_zero_branch(K, br, n):
    p, nc = K.p, K.nc
    with p.scope():
        z = p.sb("z", [128, n, TT], BF16)
        p.op("pool", lambda: nc.gpsimd.memset(z[:], 0.0), writes=["z"])
        for t in range(K.S // TT):
            p.dma("sp", [(K.Y[YCH[br]:YCH[br] + n, :, t * TT:(t + 1) * TT].rearrange("c p t -> p c t"), z[:])], reads=["z"], writes=["Y"])


def phase_merge(K, l, xsrc, xdst):
    p, nc = K.p, K.nc
    A, V, G = nc.scalar, nc.vector, nc.gpsimd
    TT = 512
    with p.scope():
        gtm = make_gt(K, l, 16, "gtm")
        Wg = p.sb("Wg", [128, 8, 4096], BF16)
        for m in range(4):
            load_w(K, Wg[:, :, m * 1024:(m + 1) * 1024], "Wg%d" % m, K.inp["w_in"][l], C_GM + m * 1024, C_GM + (m + 1) * 1024)
        Wb = p.sb("Wb", [128, 15, 1024], BF16)
        p.dma("pool", [(Wb[:], K.inp["w_branch"][l].rearrange("(c p) d -> p c d", p=128))], writes=["Wb"])
        Wo = p.sb("Wo", [128, 8, 1024], BF16)
        p.dma("pool", [(Wo[:], K.inp["w_out"][l].rearrange("(c p) d -> p c d", p=128))], writes=["Wo"])
        K.hbuf = [p.sb("hbuf", [128, 8, TT], BF16) for _ in range(2)]
        ybuf = [p.sb("ybuf", [128, 15, TT], BF16) for _ in range(2)]
        mg = p.sb("mg", [128, 8, TT], BF16)
        sg = [p.sb("sg", [128, TT]) for _ in range(2)]
        acc = p.sb("acc", [128, TT])
        xt = [p.sb("xt", [128, 1024]) for _ in range(2)]
        tm = p.sb("tm", [128, 512])
        brs = ((0, 0, 3), (1, 3, 4), (2, 7, 4), (3, 11, 4))
        for t in range(K.S // TT):
            hb, hk = load_hT(K, t, TT)
            yb = ybuf[t % 2]
            ybk = "ybuf%d" % (t % 2)
            p.dma("sp", [(yb[:], K.Y[:, :, t * TT:(t + 1) * TT].rearrange("c p t -> p c t"))], reads=["Y"], writes=[ybk])
            for j in range(8):
                for (m, c0, ncn) in brs:
                    psg, pkg = proj(K, Wg, "Wg%d" % m, m * 1024 + j * 128, 128, hb, hk, N=TT)
                    s_ = sg[m % 2]
                    sk = "sg%d" % (m % 2)
                    p.op("act", lambda psg=psg, s_=s_: A.activation(out=s_[:], in_=psg[:, 0:TT], func=AF.Sigmoid), reads=[pkg], writes=[sk])
                    psy, pky = ps_next(K)

                    def emit(psy=psy, c0=c0, ncn=ncn, j=j, yb=yb):
                        for cc in range(ncn):
                            inst = nc.tensor.matmul(psy[:, 0:TT], lhsT=Wb[:, c0 + cc, j * 128:(j + 1) * 128], rhs=yb[:, c0 + cc, :], start=(cc == 0), stop=(cc == ncn - 1))
                        return inst
                    p.op("pe", emit, reads=["Wb", ybk], writes=[pky])
                    if m == 0:
                        p.op("dve", lambda psy=psy, s_=s_: V.tensor_tensor(out=acc[:], in0=psy[:, 0:TT], in1=s_[:], op=ALU.mult), reads=[pky, sk], writes=["acc"])
                    else:
                        p.op("dve", lambda psy=psy, s_=s_: V.tensor_tensor(out=s_[:], in0=psy[:, 0:TT], in1=s_[:], op=ALU.mult), reads=[pky, sk], writes=[sk])
                        if m < 3:
                            p.op("pool", lambda s_=s_: G.tensor_tensor(out=acc[:], in0=acc[:], in1=s_[:], op=ALU.add), reads=["acc", sk], writes=["acc"])
                        else:
                            p.op("pool", lambda s_=s_, j=j: G.tensor_tensor(out=mg[:, j, :], in0=acc[:], in1=s_[:], op=ALU.add), reads=["acc", sk], writes=["mg%d" % j])
            for s4 in range(TT // 128):
                i = t * (TT // 128) + s4
                x_ = xt[i % 2]
                xk = "xt%d" % (i % 2)
                if i == 0:
                    p.dma("sp", [(x_[:], xsrc[0:128, :])], reads=["xsrc"], writes=[xk])
                if i + 1 < K.S // 128:
                    p.dma("sp", [(xt[(i + 1) % 2][:], xsrc[(i + 1) * 128:(i + 2) * 128, :])], reads=["xsrc"], writes=["xt%d" % ((i + 1) % 2)])
                for half in range(2):
                    ps, pk = ps_next(K)

                    def emit(ps=ps, s4=s4, half=half):
                        for kc in range(8):
                            inst = nc.tensor.matmul(ps[:], lhsT=mg[:, kc, s4 * 128:(s4 + 1) * 128], rhs=Wo[:, kc, half * 512:(half + 1) * 512], start=(kc == 0), stop=(kc == 7))
                        return inst
                    p.op("pe", emit, reads=["Wo"] + ["mg%d" % j for j in range(8)], writes=[pk])
                    p.op("dve", lambda ps=ps, half=half: V.tensor_tensor(out=tm[:], in0=ps[:], in1=gtm[:, half * 512:(half + 1) * 512], op=ALU.mult), reads=[pk, "gtm"], writes=["tm"])
                    p.op("pool", lambda x_=x_, half=half: G.tensor_tensor(out=x_[:, half * 512:(half + 1) * 512], in0=x_[:, half * 512:(half + 1) * 512], in1=tm[:], op=ALU.add), reads=["tm", xk], writes=[xk])
                p.dma("sp", [(xdst[i * 128:(i + 1) * 128, :], x_[:])], reads=[xk], writes=["xdst"])


def phase_ffn(K, l, xsrc, xdst, final):
    p, nc = K.p, K.nc
    A, V, G = nc.scalar, nc.vector, nc.gpsimd
    TF = min(2048, K.S)
    NS = TF // TT
    with p.scope():
        K.hbuf = None
        gtf = make_gt(K, l, 40, "gtf")
        hb = p.sb("hbF", [128, 8, TF], BF16)
        actT = p.sb("actT", [128, 22, TF], BF16)
        w13 = [p.sb("w13", [128, 8, 256], BF16) for _ in range(3)]
        w2 = [p.sb("w2", [128, 22, 512], BF16) for _ in range(2)]
        sa = [p.sb("sa", [128, TT]) for _ in range(2)]
        xt = [p.sb("xt", [128, 1024]) for _ in range(2)]
        tm = p.sb("tm", [128, 512])
        junk = p.sb("junk", [128, 1024], BF16)
        ss = p.sb("ss", [128, 4])
        lnf = p.sb("lnf", [128, 1024])
        if final:
            p.dma("sp", [(lnf[:], K.inp["lnfin"])], writes=["lnf"])
        w13v = K.inp["ffn_w13"][l].rearrange("(kc p) c -> p kc c", p=128)
        w2v = K.inp["ffn_w2"][l].rearrange("(c p) d -> p c d", p=128)
        wi = 0
        for tf in range(K.S // TF):
            p.dma("sp", [(hb[:], K.hT[:, :, tf * TF:(tf + 1) * TF].rearrange("kc p t -> p kc t"))], reads=["hT"], writes=["hbF"])
            for hc in range(22):
                wa = w13[wi % 3]
                wk = "w13_%d" % (wi % 3)
                wi += 1
                p.dma("pool", [(wa[:, :, 0:128], w13v[:, :, hc * 128:(hc + 1) * 128]), (wa[:, :, 128:256], w13v[:, :, 2816 + hc * 128:2816 + (hc + 1) * 128])], writes=[wk])
                for s in range(NS):
                    psa, pka = ps_next(K)
                    psb_, pkb = ps_next(K)
                    for (ps, pk, off) in ((psa, pka, 0), (psb_, pkb, 128)):
                        def emit(ps=ps, off=off, wa=wa, s=s):
                            for kc in range(8):
                                inst = nc.tensor.matmul(ps[:], lhsT=wa[:, kc, off:off + 128], rhs=hb[:, kc, s * TT:(s + 1) * TT], start=(kc == 0), stop=(kc == 7))
                            return inst
                        p.op("pe", emit, reads=[wk, "hbF"], writes=[pk])
                    s_ = sa[s % 2]
                    sk = "sa%d" % (s % 2)
                    p.op("act", lambda psa=psa, s_=s_: A.activation(out=s_[:], in_=psa[:], func=AF.Silu), reads=[pka], writes=[sk])
                    p.op("dve", lambda psb_=psb_, s_=s_, hc=hc, s=s: V.tensor_tensor(out=actT[:, hc, s * TT:(s + 1) * TT], in0=psb_[:], in1=s_[:], op=ALU.mult), reads=[pkb, sk], writes=["actT%d" % hc])
            for half in range(2):
                p.dma("pool", [(w2[half][:], w2v[:, :, half * 512:(half + 1) * 512])], writes=["w2_%d" % half])
            for s4 in range(TF // 128):
                i = tf * (TF // 128) + s4
                x_ = xt[i % 2]
                xk = "xt%d" % (i % 2)
                if i == 0:
                    p.dma("sp", [(x_[:], xsrc[0:128, :])], reads=["xsrc"], writes=[xk])
                if i + 1 < K.S // 128:
                    p.dma("sp", [(xt[(i + 1) % 2][:], xsrc[(i + 1) * 128:(i + 2) * 128, :])], reads=["xsrc"], writes=["xt%d" % ((i + 1) % 2)])
                for half in range(2):
                    ps, pk = ps_next(K)

                    def emit(ps=ps, s4=s4, half=half):
                        for hc in range(22):
                            inst = nc.tensor.matmul(ps[:], lhsT=actT[:, hc, s4 * 128:(s4 + 1) * 128], rhs=w2[half][:, hc, :], start=(hc == 0), stop=(hc == 21))
                        return inst
                    p.op("pe", emit, reads=["w2_%d" % half] + ["actT%d" % hc for hc in range(22)], writes=[pk])
                    p.op("dve", lambda ps=ps, half=half: V.tensor_tensor(out=tm[:], in0=ps[:], in1=gtf[:, half * 512:(half + 1) * 512], op=ALU.mult), reads=[pk, "gtf"], writes=["tm"])
                    p.op("pool", lambda x_=x_, half=half: G.tensor_tensor(out=x_[:, half * 512:(half + 1) * 512], in0=x_[:, half * 512:(half + 1) * 512], in1=tm[:], op=ALU.add), reads=["tm", xk], writes=[xk])
                if final:
                    p.op("act", lambda x_=x_: A.activation(out=junk[:], in_=x_[:], func=AF.Square, accum_out=ss[:, 0:1]), reads=[xk], writes=["junk", "ss"])
                    p.op("act", lambda: A.activation(out=ss[:, 1:2], in_=ss[:, 0:1], func=AF.Sqrt, scale=1.0 / D, bias=1e-6), reads=["ss"], writes=["ss"])
                    p.op("dve", lambda: V.reciprocal(out=ss[:, 2:3], in_=ss[:, 1:2]), reads=["ss"], writes=["ss"])
                    p.op("dve", lambda x_=x_: V.scalar_tensor_tensor(out=x_[:], in0=x_[:], scalar=ss[:, 2:3], in1=lnf[:], op0=ALU.mult, op1=ALU.mult), reads=[xk, "ss", "lnf"], writes=[xk])
                p.dma("sp", [(xdst[i * 128:(i + 1) * 128, :], x_[:])], reads=[xk], writes=["xdst"])


INPUT_SHAPES = None


def input_shapes(S, L):
    return {
        "x": ([S, D], F32), "cT": ([128, 8], F32),
        "lnmix": ([L, 128, 8], F32), "lnffn": ([L, 128, 8], F32), "lnfin": ([128, D], F32),
        "ada_w": ([L, D, 6 * D], F32), "ada_bT": ([L, 128, 48], F32),
        "w_in": ([L, D, NIN], F32), "conv_w4": ([L, 128, 22, 4], F32), "conv_b": ([L, 128, 22], F32),
        "lru_wr": ([L, 8, 64, 64], F32), "lru_wi": ([L, 8, 64, 64], F32),
        "lru_br": ([L, 128, 4], F32), "lru_bi": ([L, 128, 4], F32), "lru_lambda": ([L, 128, 4], F32),
        "s5_lamr": ([L, 128, 24], F32), "s5_lami": ([L, 128, 24], F32), "s5_ldt": ([L, 128, 24], F32),
        "s5_br": ([L, 128, 24, 16], F32), "s5_bi": ([L, 128, 24, 16], F32), "s5_cr": ([L, 128, 24, 16], F32), "s5_ci": ([L, 128, 24, 16], F32),
        "s5_dB": ([L, 128, 384], F32), "s5_glub": ([L, 128, 3], F32), "s5_glu_w": ([L, 384, 384], F32),
        "gdn_dt_bias": ([L, 4, 1], F32), "gdn_a_log": ([L, 4, 1], F32), "gdn_norm_w": ([L, 128, 1], F32),
        "ssd_dt_bias": ([L, 8, 1], F32), "ssd_a_log": ([L, 8, 1], F32), "ssd_dB": ([L, 64, 512], F32), "ssd_nwB": ([L, 64, 512], F32),
        "w_branch": ([L, 1920, D], F32), "w_out": ([L, D, D], F32),
        "ffn_w13": ([L, D, 5632], F32), "ffn_w2": ([L, 2816, D], F32),
    }


def prep_inputs(inputs, b, S, L):
    f = lambda a: np.ascontiguousarray(np.asarray(a, dtype=np.float32))
    inputs = {k: (np.asarray(v)[:L] if k not in ('x', 'c', 'ln_final_g') else np.asarray(v)) for k, v in inputs.items()}
    fm = lambda v, n: f(v.reshape(L, n, 128).transpose(0, 2, 1))
    m = {}
    m["x"] = f(inputs["x"][b, :S])
    m["cT"] = f(inputs["c"][b].reshape(8, 128).T)
    m["lnmix"] = fm(inputs["ln_mix_g"], 8)
    m["lnffn"] = fm(inputs["ln_ffn_g"], 8)
    m["lnfin"] = f(np.broadcast_to(inputs["ln_final_g"][None, :], (128, D)))
    m["ada_w"] = f(inputs["ada_w"])
    m["ada_bT"] = fm(inputs["ada_b"], 48)
    m["w_in"] = f(inputs["w_in"])
    m["conv_w4"] = f(inputs["conv_w"].reshape(L, 4, 22, 128).transpose(0, 3, 2, 1))
    m["conv_b"] = fm(inputs["conv_b"], 22)
    m["lru_wr"] = f(inputs["lru_wr"])
    m["lru_wi"] = f(inputs["lru_wi"])
    m["lru_br"] = fm(inputs["lru_br"], 4)
    m["lru_bi"] = fm(inputs["lru_bi"], 4)
    m["lru_lambda"] = fm(inputs["lru_lambda"], 4)
    two = lambda a: np.concatenate([a, a], axis=1)
    m["s5_lamr"] = f(two(inputs["s5_lambda_re"].transpose(0, 2, 1)))
    m["s5_lami"] = f(two(inputs["s5_lambda_im"].transpose(0, 2, 1)))
    m["s5_ldt"] = f(np.broadcast_to(inputs["s5_log_dt"][:, None, :], (L, 128, 24)))
    m["s5_br"] = f(two(inputs["s5_b_re"].transpose(0, 2, 1, 3)))
    m["s5_bi"] = f(two(inputs["s5_b_im"].transpose(0, 2, 1, 3)))
    m["s5_cr"] = f(two(inputs["s5_c_re"].transpose(0, 3, 1, 2)))
    m["s5_ci"] = f(two(inputs["s5_c_im"].transpose(0, 3, 1, 2)))
    m["s5_dB"] = f(np.broadcast_to(inputs["s5_d"][:, None, :], (L, 128, 384)))
    m["s5_glub"] = fm(inputs["s5_glu_b"], 3)
    m["s5_glu_w"] = f(inputs["s5_glu_w"])
    m["gdn_dt_bias"] = f(inputs["gdn_dt_bias"].reshape(L, 4, 1))
    m["gdn_a_log"] = f(inputs["gdn_a_log"].reshape(L, 4, 1))
    m["gdn_norm_w"] = f(inputs["gdn_norm_w"].reshape(L, 128, 1))
    m["ssd_dt_bias"] = f(inputs["ssd_dt_bias"].reshape(L, 8, 1))
    m["ssd_a_log"] = f(inputs["ssd_a_log"].reshape(L, 8, 1))
    m["ssd_dB"] = f(np.broadcast_to(np.repeat(inputs["ssd_d"], 64, axis=1)[:, None, :], (L, 64, 512)))
    m["ssd_nwB"] = f(np.broadcast_to(inputs["ssd_norm_w"][:, None, :], (L, 64, 512)))
    m["w_branch"] = f(inputs["w_branch"])
    m["w_out"] = f(inputs["w_out"])
    m["ffn_w13"] = f(inputs["ffn_w13"])
    m["ffn_w2"] = f(inputs["ffn_w2"])
    return m


def build(S=4096, L=2, dbg=False, branches="abcd"):
    nc = bass.Bass("TRN2", target_bir_lowering=False)
    K = Ctx()
    K.nc, K.S, K.depth = nc, S, L
    import os as _os
    K.stop = int(_os.environ['SSD_STOP']) if 'SSD_STOP' in _os.environ else None
    K.gdn_c = int(_os.environ.get('GDN_C', '128'))
    K.gdn_ng = int(_os.environ.get('GDN_NG', '4' if K.gdn_c == 64 else '2'))
    K.inp = {n: nc.dram_tensor(n, sh, dt, kind="ExternalInput").ap() for n, (sh, dt) in input_shapes(S, L).items()}
    K.out = nc.dram_tensor("out", [S, D], F32, kind="ExternalOutput").ap()
    kind = "ExternalOutput" if dbg else "Internal"
    K.hT = nc.dram_tensor("hT", [8, 128, S], BF16, kind=kind).ap()
    K.Y = nc.dram_tensor("Y", [15, 128, S], BF16, kind=kind).ap()
    K.X1 = nc.dram_tensor("X1", [S, D], F32, kind=kind).ap()
    K.Us = nc.dram_tensor("Us", [S, 384], BF16, kind=kind).ap()
    K.p = Prog(nc)
    p = K.p
    phase_setup(K)
    phase_mods(K)
    make_masks(K)
    for l in range(L):
        xsrc = K.inp["x"] if l == 0 else K.X1
        phase_norm(K, l, xsrc, K.gscm[l], "gscm%d" % l, 0)
        for br, n in (("a", 3), ("b", 4), ("c", 4)):
            if br not in branches:
                phase_zero_branch(K, br, n)
        if "a" in branches:
            phase_s5(K, l)
        if "b" in branches:
            phase_gdn(K, l)
        if "c" in branches:
            phase_ssd(K, l)
        if "d" in branches:
            phase_lru(K, l)
        else:
            phase_zero_branch(K, "d", 4)
        phase_merge(K, l, xsrc, K.X1)
        phase_norm(K, l, K.X1, K.gscf[l], "gscf%d" % l, 24)
        last = (l == L - 1)
        phase_ffn(K, l, K.X1, K.out if last else K.X1, last)
    p.barrier()
    p.es.close()
    return nc


def kernel(**inputs):
    S, L = 4096, 2
    nc = build(S, L)
    in_maps = [prep_inputs(inputs, b, S, L) for b in range(8)]
    res = run_bass_kernel_spmd(nc, in_maps, core_ids=list(range(8)))
    return np.stack([np.asarray(r["out"]) for r in res.results], axis=0).astype(np.float32)


def make_masks(K):
    p, nc = K.p, K.nc
    G = nc.gpsimd
    K.rm = p.sb("rm", [128, 2])
    p.op("pool", lambda: G.memset(K.rm[:], 0.0), writes=["rm"])
    p.op("pool", lambda: G.memset(K.rm[0:64, 0:1], 1.0), reads=["rm"], writes=["rm"])
    p.op("pool", lambda: G.memset(K.rm[64:128, 1:2], 1.0), reads=["rm"], writes=["rm"])


def make_ssd_masks(K):
    p, nc = K.p, K.nc
    G = nc.gpsimd
    K.IND8 = p.sb("IND8", [8, 8, 64])
    p.op("pool", lambda: G.memset(K.IND8[:], 1.0), writes=["IND8"])
    p.op("pool", lambda: G.affine_select(out=K.IND8[:], in_=K.IND8[:], pattern=[[-1, 8], [0, 64]], compare_op=ALU.is_equal, fill=0.0, base=0, channel_multiplier=1), reads=["IND8"], writes=["IND8"])
    K.M_le = p.sb("M_le", [64, 8, 64], BF16)
    p.op("pool", lambda: G.memset(K.M_le[:], 0.0), writes=["M_le"])
    p.op("pool", lambda: G.affine_select(out=K.M_le[:], in_=K.M_le[:], pattern=[[0, 8], [1, 64]], compare_op=ALU.is_ge, fill=NEG, base=0, channel_multiplier=-1), reads=["M_le"], writes=["M_le"])
    K.rmask = p.sb("rmask", [8, 8, 64])
    p.op("pool", lambda: G.memset(K.rmask[:], 1.0), writes=["rmask"])
    p.op("pool", lambda: G.memset(K.rmask[:, :, 0:1], 0.0), reads=["rmask"], writes=["rmask"])


def decay_exp(K, lhs_pos, lhs_neg_bd, lhs_neg, rhs_bd_neg, mask, mkey, nh, out_ap, okey, reads):
    raise NotImplementedError


def phase_ssd(K, l):
    p, nc = K.p, K.nc
    A, V, G, T = nc.scalar, nc.vector, nc.gpsimd, nc.tensor
    with p.scope():
        make_ssd_masks(K)
        NW = 1288
        W = p.sb("W", [128, 8, NW], BF16)
        wv = K.inp["w_in"][l]
        load_w(K, W[:, :, 0:768], "W", wv, C_XS, C_XS + 768)
        load_w(K, W[:, :, 768:1280], "W", wv, C_ZS, C_ZS + 512)
        load_w(K, W[:, :, 1280:1288], "W", wv, C_DT, C_DT + 8)
        cw = p.sb("cw", [128, 6, 4]); cb = p.sb("cb", [128, 6])
        sm8 = p.sb("sm8", [8, 4])
        dB = p.sb("dB", [64, 512]); nwB = p.sb("nwB", [64, 512])
        p.dma("sp", [(cw[:], K.inp["conv_w4"][l, :, 12:18, :]), (cb[:], K.inp["conv_b"][l, :, 12:18]),
                     (sm8[:, 0:1], K.inp["ssd_dt_bias"][l]), (sm8[:, 1:2], K.inp["ssd_a_log"][l]),
                     (dB[:], K.inp["ssd_dB"][l]), (nwB[:], K.inp["ssd_nwB"][l])], writes=["cw"])
        p.op("act", lambda: A.activation(out=sm8[:, 2:3], in_=sm8[:, 1:2], func=AF.Exp), reads=["cw"], writes=["sm8"])
        p.op("dve", lambda: V.tensor_scalar(out=sm8[:, 2:3], in0=sm8[:, 2:3], scalar1=-1.0, scalar2=None, op0=ALU.mult), reads=["sm8"], writes=["sm8"])
        Pbw = [p.sb("Pbw", [128, 515]) for _ in range(2)]
        tails = p.sb("tails", [128, 6, 3])
        p.op("pool", lambda: G.memset(tails[:], 0.0), writes=["tails"])
        import os as _os
        NG = int(_os.environ.get("SSD_NG", "4"))
        St = p.sb("St", [128, 4, 64]); Stb = [p.sb("Stb", [128, 4, 64], BF16) for _ in range(8)]; tmpS = p.sb("tmpS", [128, 4, 64])
        p.op("pool", lambda: G.memset(St[:], 0.0), writes=["St"])
        p.op("pool", lambda: G.memset(Stb[0][:], 0.0), writes=["Stb0"])
        K.hbuf = [p.sb("hbuf", [128, 8, TT], BF16) for _ in range(2)]
        xsT = p.sb("xsT", [128, 4, TT], BF16); BT = p.sb("BT", [128, TT], BF16); CT = p.sb("CT", [128, TT], BF16)
        BTz = [p.sb("BTz", [128, TT], BF16) for _ in range(2)]; CTz = [p.sb("CTz", [128, TT], BF16) for _ in range(2)]
        dtT = p.sb("dtT", [8, TT]); csT = p.sb("csT", [8, TT]); ncsT = p.sb("ncsT", [8, TT]); erevT = p.sb("erevT", [8, TT]); laT = p.sb("laT", [8, TT])
        csbd = [p.sb("csbd", [8, 8, 64]) for _ in range(NG)]
        LT = [p.sb("LT", [64, 512], BF16) for _ in range(NG)]
        ECS = [p.sb("ECS", [128, 512]) for _ in range(NG)]
        tok = [p.sb("tok", [64, 24]) for _ in range(NG)]
        MT = [p.sb("MT", [64, 512], BF16) for _ in range(NG)]
        Ssb = [p.sb("Ssb", [64, 128]) for _ in range(NG)]
        xtok = [p.sb("xtok", [64, 512], BF16) for _ in range(NG)]
        xdt = [p.sb("xdt", [64, 512], BF16) for _ in range(NG)]
        xdd = [p.sb("xdd", [64, 512], BF16) for _ in range(NG)]
        Cdec = [p.sb("Cdec", [128, 2, 256], BF16) for _ in range(NG)]
        sz = [p.sb("sz", [64, 512]) for _ in range(NG)]
        y1 = [p.sb("y1", [64, 512]) for _ in range(NG)]
        y3 = [p.sb("y3", [64, 512], BF16) for _ in range(NG)]
        ssq = [p.sb("ssq", [64, 4]) for _ in range(NG)]; junk = [p.sb("junk", [64, 512], BF16) for _ in range(NG)]
        Btok = [p.sb("Btok", [64, 128], BF16) for _ in range(NG)]
        yT = [p.sb("yT", [128, 4, TT], BF16) for _ in range(2)]
        for t in range(K.S // TT):
            hb, hk = load_hT(K, t)
            for c in range(6):
                ps, pk = proj(K, W, "W", c * 128, 128, hb, hk)
                dst = xsT[:, c, :] if c < 4 else (BT[:] if c == 4 else CT[:])
                conv_chunk(K, ps, pk, Pbw[c % 2], "Pbw%d" % (c % 2), cw, cb, c, dst, "cv%d" % c, AF.Silu, tails=tails, tkey="tails")
            for g in range(2):
                p.op("pool", lambda g=g: G.tensor_scalar(out=BTz[g][:], in0=BT[:], scalar1=K.rm[:, g:g + 1], scalar2=None, op0=ALU.mult), reads=["cv4", "rm"], writes=["BTz%d" % g])
                p.op("pool", lambda g=g: G.tensor_scalar(out=CTz[g][:], in0=CT[:], scalar1=K.rm[:, g:g + 1], scalar2=None, op0=ALU.mult), reads=["cv5", "rm"], writes=["CTz%d" % g])
            ps, pk = proj(K, W, "W", 1280, 8, hb, hk)
            p.op("act", lambda ps=ps: A.activation(out=dtT[:], in_=ps[0:8, :], func=AF.Exp, bias=sm8[:, 0:1]), reads=[pk, "cw"], writes=["dtT"])
            p.op("act", lambda: A.activation(out=dtT[:], in_=dtT[:], func=AF.Ln, bias=1.0), reads=["dtT"], writes=["dtT"])
            p.op("dve", lambda: V.tensor_scalar(out=laT[:], in0=dtT[:], scalar1=sm8[:, 2:3], scalar2=None, op0=ALU.mult), reads=["dtT", "sm8"], writes=["laT"])
            p.op("dve", lambda: V.tensor_tensor_scan(out=csT[:], data0=K.rmask[:].rearrange("h c l -> h (c l)"), data1=laT[:], initial=0.0, op0=ALU.mult, op1=ALU.add), reads=["laT", "rmask"], writes=["csT"])
            p.op("dve", lambda: V.tensor_scalar(out=ncsT[:], in0=csT[:], scalar1=-1.0, scalar2=None, op0=ALU.mult), reads=["csT"], writes=["ncsT"])
            cs3 = csT[:].rearrange("h (c l) -> h c l", c=8)
            p.op("dve", lambda: V.tensor_tensor(out=erevT[:].rearrange("h (c l) -> h c l", c=8), in0=cs3[:, :, 63:64].to_broadcast([8, 8, 64]), in1=cs3, op=ALU.subtract), reads=["csT"], writes=["erevT"])
            p.op("act", lambda: A.activation(out=erevT[:], in_=erevT[:], func=AF.Exp), reads=["erevT"], writes=["erevT"])
            y = yT[t % 2]; yk = "yT%d" % (t % 2)
            def chunk_gen(cx, b2, cglob):
                sl = slice(cx * 64, (cx + 1) * 64)
                k_ = lambda n: "%s%d" % (n, b2)
                Sin, sink = Stb[cglob % 8], "Stb%d" % (cglob % 8)
                Sout, soutk = Stb[(cglob + 1) % 8], "Stb%d" % ((cglob + 1) % 8)
                p.op("pool", lambda: G.tensor_tensor(out=csbd[b2][:], in0=csT[:, sl].unsqueeze(1).to_broadcast([8, 8, 64]), in1=K.IND8[:], op=ALU.mult), reads=["csT", "IND8"], writes=[k_("csbd")])
                yield
                if K.stop is not None and K.stop <= 0:
                    return
                Tp, tk = ps_next(K)

                def emit_t(Tp=Tp, sl=sl):
                    T.matmul(Tp[0:64, 0:8], lhsT=dtT[:, sl], rhs=K.identf[0:8, 0:8], start=True, stop=True)
                    T.matmul(Tp[0:64, 8:16], lhsT=erevT[:, sl], rhs=K.identf[0:8, 0:8], start=True, stop=True)
                    return T.matmul(Tp[0:64, 16:24], lhsT=ncsT[:, sl], rhs=K.identf[0:8, 0:8], start=True, stop=True)
                p.op("pe", emit_t, reads=["dtT", "erevT", "ncsT", "identf"], writes=[tk])
                p.op("act", lambda Tp=Tp, b2=b2: A.copy(out=tok[b2][:], in_=Tp[0:64, 0:24]), reads=[tk], writes=[k_("tok")])
                Dp, dk = ps_next(K)

                def emit_d(Dp=Dp, b2=b2, sl=sl):
                    T.matmul(Dp[0:64, :], lhsT=K.onesf[0:8, 0:64], rhs=csbd[b2][:].rearrange("k h l -> k (h l)"), start=True, stop=False)
                    return T.matmul(Dp[0:64, :], lhsT=K.identb[0:64, 0:64], rhs=K.M_le[:].rearrange("k h l -> k (h l)"), start=False, stop=True)
                p.op("pe", emit_d, reads=[k_("csbd"), "M_le", "onesf", "identb"], writes=[dk])
                for h in range(8):
                    p.op("act", lambda Dp=Dp, b2=b2, h=h: A.activation(out=LT[b2][:, h * 64:(h + 1) * 64], in_=Dp[0:64, h * 64:(h + 1) * 64], func=AF.Exp, bias=tok[b2][:, 16 + h:17 + h]),
                         reads=[dk, k_("tok")], writes=[k_("LT")])
                Ep, ek = ps_next(K)
                p.op("pe", lambda Ep=Ep, b2=b2: T.matmul(Ep[:], lhsT=K.onesf[0:8, :], rhs=csbd[b2][:].rearrange("k h l -> k (h l)"), start=True, stop=True), reads=[k_("csbd"), "onesf"], writes=[ek])
                p.op("act", lambda Ep=Ep, b2=b2: A.activation(out=ECS[b2][:], in_=Ep[:], func=AF.Exp), reads=[ek], writes=[k_("ECS")])
                yield
                if K.stop is not None and K.stop <= 1:
                    return
                Sp, sk = ps_next(K)

                def emit_s(Sp=Sp, sl=sl):
                    T.matmul(Sp[0:64, 0:64], lhsT=BTz[0][:, sl], rhs=CT[:, sl], start=True, stop=True)
                    return T.matmul(Sp[0:64, 64:128], lhsT=BTz[1][:, sl], rhs=CT[:, sl], start=True, stop=True)
                p.op("pe", emit_s, reads=["BTz0", "BTz1", "cv5"], writes=[sk])
                p.op("act", lambda Sp=Sp, b2=b2: A.copy(out=Ssb[b2][:], in_=Sp[0:64, 0:128]), reads=[sk], writes=[k_("Ssb")])
                for g in range(2):
                    p.op("dve", lambda b2=b2, g=g: V.tensor_tensor(out=MT[b2][:, g * 256:(g + 1) * 256].rearrange("m (j l) -> m j l", j=4),
                                                              in0=LT[b2][:, g * 256:(g + 1) * 256].rearrange("m (j l) -> m j l", j=4),
                                                              in1=Ssb[b2][:, g * 64:(g + 1) * 64].unsqueeze(1).to_broadcast([64, 4, 64]), op=ALU.mult), reads=[k_("Ssb"), k_("LT")], writes=[k_("MT")])
                yield
                if K.stop is not None and K.stop <= 2:
                    return
                Xp_, xpk = ps_next(K)

                def emit_x(sl=sl, Xp_=Xp_):
                    for c4 in range(4):
                        inst = T.matmul(Xp_[0:64, c4 * 128:(c4 + 1) * 128], lhsT=xsT[:, c4, sl], rhs=K.identb[:], start=True, stop=True)
                    return inst
                p.op("pe", emit_x, reads=["cv0", "cv1", "cv2", "cv3", "identb"], writes=[xpk])
                p.op("act", lambda b2=b2, Xp_=Xp_: A.copy(out=xtok[b2][:], in_=Xp_[0:64, 0:512]), reads=[xpk], writes=[k_("xtok")])
                p.op("dve", lambda b2=b2, Xp_=Xp_: V.tensor_tensor(out=xdt[b2][:].rearrange("m (h q) -> m h q", h=8), in0=Xp_[0:64, 0:512].rearrange("m (h q) -> m h q", h=8),
                                                          in1=tok[b2][:, 0:8].unsqueeze(2).to_broadcast([64, 8, 64]), op=ALU.mult), reads=[xpk, k_("tok")], writes=[k_("xdt")])
                p.op("pool", lambda b2=b2: G.tensor_tensor(out=xdd[b2][:].rearrange("m (h q) -> m h q", h=8), in0=xdt[b2][:].rearrange("m (h q) -> m h q", h=8),
                                                           in1=tok[b2][:, 8:16].unsqueeze(2).to_broadcast([64, 8, 64]), op=ALU.mult), reads=[k_("xdt"), k_("tok")], writes=[k_("xdd")])
                yield
                if K.stop is not None and K.stop <= 3:
                    return
                for g in range(2):
                    p.op("pool", lambda g=g, b2=b2: G.tensor_tensor(out=Cdec[b2][:, g, :].rearrange("n (j l) -> n j l", j=4), in0=CTz[g][:, sl].unsqueeze(1).to_broadcast([128, 4, 64]),
                                                                    in1=ECS[b2][:, g * 256:(g + 1) * 256].rearrange("n (j l) -> n j l", j=4), op=ALU.mult), reads=["CTz%d" % g, k_("ECS")], writes=[k_("Cdec")])
                Zp, zk = ps_next(K)

                def emit_z(Zp=Zp, sl=sl):
                    for kc in range(8):
                        inst = T.matmul(Zp[0:64, :], lhsT=hb[:, kc, sl], rhs=W[:, kc, 768:1280], start=(kc == 0), stop=(kc == 7))
                    return inst
                p.op("pe", emit_z, reads=["W", hk], writes=[zk])
                p.op("act", lambda Zp=Zp, b2=b2: A.activation(out=sz[b2][:], in_=Zp[0:64, :], func=AF.Silu), reads=[zk], writes=[k_("sz")])
                yield
                if K.stop is not None and K.stop <= 4:
                    return
                Bp, bk = ps_next(K)
                p.op("pe", lambda sl=sl, Bp=Bp: T.matmul(Bp[0:64, 0:128], lhsT=BT[:, sl], rhs=K.identb[:], start=True, stop=True), reads=["cv4", "identb"], writes=[bk])
                p.op("act", lambda b2=b2, Bp=Bp: A.copy(out=Btok[b2][:], in_=Bp[0:64, 0:128]), reads=[bk], writes=[k_("Btok")])
                Ip, ik = ps_next(K)
                p.op("pe", lambda Ip=Ip, b2=b2: T.matmul(Ip[:], lhsT=Btok[b2][:], rhs=xdd[b2][:], start=True, stop=True), reads=[k_("Btok"), k_("xdd")], writes=[ik])
                for g in range(2):
                    pr = slice(g * 64, (g + 1) * 64)
                    p.op("pool", lambda g=g, pr=pr, b2=b2: G.tensor_tensor(out=tmpS[pr], in0=St[pr],
                                                                           in1=ECS[b2][pr, :].rearrange("n (h l) -> n h l", h=8)[:, 4 * g:4 * g + 4, 63:64].to_broadcast([64, 4, 64]), op=ALU.mult),
                         reads=["St", k_("ECS")], writes=["tmpS%d" % g])
                    p.op("dve", lambda g=g, pr=pr, Ip=Ip: V.tensor_tensor(out=St[pr], in0=Ip[pr, g * 256:(g + 1) * 256].rearrange("n (j q) -> n j q", j=4), in1=tmpS[pr], op=ALU.add),
                         reads=[ik, "tmpS%d" % g], writes=["St"])
                p.op("act", lambda: A.copy(out=Sout[:], in_=St[:]), reads=["St"], writes=[soutk])
                yield
                if K.stop is not None and K.stop <= 5:
                    return
                Yp, yk_ = ps_next(K)

                def emit_y(Yp=Yp, b2=b2):
                    for h in range(8):
                        g, j = h // 4, h % 4
                        T.matmul(Yp[0:64, h * 64:(h + 1) * 64], lhsT=MT[b2][:, h * 64:(h + 1) * 64], rhs=xdt[b2][:, h * 64:(h + 1) * 64], start=True, stop=False)
                        inst = T.matmul(Yp[0:64, h * 64:(h + 1) * 64], lhsT=Cdec[b2][:, g, j * 64:(j + 1) * 64], rhs=Sin[:, j, :], start=False, stop=True)
                    return inst
                p.op("pe", emit_y, reads=[k_("MT"), k_("xdt"), k_("Cdec"), sink], writes=[yk_])
                yield
                if K.stop is not None and K.stop <= 6:
                    return
                p.op("pool", lambda b2=b2: G.tensor_tensor(out=y1[b2][:], in0=xtok[b2][:], in1=dB[:], op=ALU.mult), reads=[k_("xtok"), "cw"], writes=[k_("y1")])
                p.op("dve", lambda b2=b2, Yp=Yp: V.tensor_tensor(out=y1[b2][:], in0=Yp[0:64, :], in1=y1[b2][:], op=ALU.add), reads=[yk_, k_("y1")], writes=[k_("y1")])
                p.op("pool", lambda b2=b2: G.tensor_tensor(out=y1[b2][:], in0=y1[b2][:], in1=sz[b2][:], op=ALU.mult), reads=[k_("y1"), k_("sz")], writes=[k_("y1")])
                p.op("act", lambda b2=b2: A.activation(out=junk[b2][:], in_=y1[b2][:], func=AF.Square, accum_out=ssq[b2][:, 0:1]), reads=[k_("y1")], writes=[k_("ssq")])
                p.op("act", lambda: A.activation(out=ssq[b2][:, 1:2], in_=ssq[b2][:, 0:1], func=AF.Sqrt, scale=1.0 / 512, bias=1e-6), reads=[k_("ssq")], writes=[k_("ssq")])
                p.op("dve", lambda: V.reciprocal(out=ssq[b2][:, 2:3], in_=ssq[b2][:, 1:2]), reads=[k_("ssq")], writes=[k_("ssq")])
                p.op("dve", lambda b2=b2: V.scalar_tensor_tensor(out=y3[b2][:], in0=y1[b2][:], scalar=ssq[b2][:, 2:3], in1=nwB[:], op0=ALU.mult, op1=ALU.mult), reads=[k_("y1"), k_("ssq"), "cw"], writes=[k_("y3")])

                yield
                if K.stop is not None and K.stop <= 7:
                    return
                Op_, opk = ps_next(K)

                def emit_o(b2=b2, Op_=Op_):
                    for c4 in range(4):
                        inst = T.matmul(Op_[:, c4 * 64:(c4 + 1) * 64], lhsT=y3[b2][:, c4 * 128:(c4 + 1) * 128], rhs=K.identb[0:64, 0:64], start=True, stop=True)
                    return inst
                p.op("pe", emit_o, reads=[k_("y3"), "identb"], writes=[opk])
                p.op("act", lambda y=y, sl=sl, Op_=Op_: A.copy(out=y[:, :, sl], in_=Op_[:, 0:256].rearrange("q (c l) -> q c l", c=4)), reads=[opk], writes=[yk])
            for b0 in range(0, 8, NG):
                gens = [chunk_gen(cx, cx - b0, t * 8 + cx) for cx in range(b0, min(8, b0 + NG))]
                while gens:
                    for g_ in list(gens):
                        try:
                            next(g_)
                        except StopIteration:
                            gens.remove(g_)
            p.dma("sp", [(K.Y[YCH["c"]:YCH["c"] + 4, :, t * TT:(t + 1) * TT].rearrange("c p t -> p c t"), y[:])], reads=[yk], writes=["Y"])


def phase_gdn(K, l):
    p, nc = K.p, K.nc
    A, V, G, T = nc.scalar, nc.vector, nc.gpsimd, nc.tensor
    fl = lambda ap: ap.rearrange("k h l -> k (h l)")
    with p.scope():
        W = p.sb("W", [128, 8, 2056], BF16)
        wv = K.inp["w_in"][l]
        load_w(K, W[:, :, 0:1536], "W", wv, C_Q, C_Q + 1536)
        load_w(K, W[:, :, 1536:2048], "W", wv, C_ZG, C_ZG + 512)
        load_w(K, W[:, :, 2048:2056], "W", wv, C_BG, C_BG + 8)
        cw = p.sb("cw", [128, 12, 4]); cb = p.sb("cb", [128, 12])
        sm4 = p.sb("sm4", [4, 4]); nw = p.sb("nw", [128, 1])
        p.dma("sp", [(cw[:], K.inp["conv_w4"][l, :, 0:12, :]), (cb[:], K.inp["conv_b"][l, :, 0:12]),
                     (sm4[:, 0:1], K.inp["gdn_dt_bias"][l]), (sm4[:, 1:2], K.inp["gdn_a_log"][l]), (nw[:], K.inp["gdn_norm_w"][l])], writes=["cw"])
        p.op("act", lambda: A.activation(out=sm4[:, 2:3], in_=sm4[:, 1:2], func=AF.Exp), reads=["cw"], writes=["sm4"])
        p.op("dve", lambda: V.tensor_scalar(out=sm4[:, 2:3], in0=sm4[:, 2:3], scalar1=-1.0, scalar2=None, op0=ALU.mult), reads=["sm4"], writes=["sm4"])
        C = K.gdn_c
        IND4t = p.sb("IND4t", [4, 4, C])
        p.op("pool", lambda: G.memset(IND4t[:], 1.0), writes=["gmask"])
        p.op("pool", lambda: G.affine_select(out=IND4t[:], in_=IND4t[:], pattern=[[-1, 4], [0, C]], compare_op=ALU.is_equal, fill=0.0, base=0, channel_multiplier=1), reads=["gmask"], writes=["gmask"])
        rmaskC = p.sb("rmaskC", [4, TT // C, C])
        p.op("pool", lambda: G.memset(rmaskC[:], 1.0), reads=["gmask"], writes=["gmask"])
        p.op("pool", lambda: G.memset(rmaskC[:, :, 0:1], 0.0), reads=["gmask"], writes=["gmask"])
        M_le = p.sb("M_le", [C, 4, C], BF16); M_lt = p.sb("M_lt", [C, 4, C], BF16); M_gt = p.sb("M_gt", [C, 4, C], BF16)
        for (t_, cm, coef, cmp) in ((M_le, -1, 1, ALU.is_ge), (M_lt, -1, 1, ALU.is_gt), (M_gt, 1, -1, ALU.is_gt)):
            p.op("pool", lambda t_=t_: G.memset(t_[:], 0.0), reads=["gmask"], writes=["gmask"])
            p.op("pool", lambda t_=t_, cm=cm, coef=coef, cmp=cmp: G.affine_select(out=t_[:], in_=t_[:], pattern=[[0, 4], [coef, C]], compare_op=cmp, fill=NEG, base=0, channel_multiplier=cm), reads=["gmask"], writes=["gmask"])
        bmf = {}
        kb = 8
        while kb < C:
            nb = C // kb
            ind = p.sb("ind%d" % kb, [16, C])
            p.op("pool", lambda ind=ind: G.memset(ind[:], 1.0), reads=["gmask"], writes=["gmask"])
            p.op("pool", lambda ind=ind, kb=kb: G.affine_select(out=ind[:], in_=ind[:], pattern=[[1, C]], compare_op=ALU.is_ge, fill=0.0, base=0, channel_multiplier=-kb), reads=["gmask"], writes=["gmask"])
            p.op("pool", lambda ind=ind, kb=kb: G.affine_select(out=ind[:], in_=ind[:], pattern=[[-1, C]], compare_op=ALU.is_ge, fill=0.0, base=kb - 1, channel_multiplier=kb), reads=["gmask"], writes=["gmask"])
            ps, pk = ps_next(K)
            p.op("pe", lambda ind=ind, nb=nb, ps=ps: T.matmul(ps[0:C, 0:C], lhsT=ind[0:nb, :], rhs=ind[0:nb, :], start=True, stop=True), reads=["gmask"], writes=[pk])
            bm = p.sb("bm%d" % kb, [C, C])
            p.op("act", lambda bm=bm, ps=ps: A.copy(out=bm[:], in_=ps[0:C, 0:C]), reads=[pk], writes=["gmask"])
            bmf[kb] = bm
            kb *= 2
        b4 = lambda m_: m_[:].unsqueeze(1).to_broadcast([C, 4, C])
        BMb = p.sb("BMb", [C, 4, C], BF16)
        p.op("dve", lambda: V.tensor_copy(out=BMb[:], in_=b4(bmf[8])), reads=["gmask"], writes=["gmask"])
        Dlist = []
        kb = 8
        while kb < C:
            Dm = p.sb("Dm%d" % kb, [C, 4, C], BF16)
            if 2 * kb < C:
                p.op("dve", lambda Dm=Dm, kb=kb: V.tensor_tensor(out=Dm[:], in0=b4(bmf[2 * kb]), in1=b4(bmf[kb]), op=ALU.subtract), reads=["gmask"], writes=["gmask"])
            else:
                p.op("dve", lambda Dm=Dm, kb=kb: V.tensor_scalar(out=Dm[:], in0=b4(bmf[kb]), scalar1=-1.0, scalar2=1.0, op0=ALU.mult, op1=ALU.add), reads=["gmask"], writes=["gmask"])
            Dlist.append(Dm)
            kb *= 2
        SEL4 = p.sb("SEL4", [4, 4, 128])
        p.op("pool", lambda: G.memset(SEL4[:], 1.0), writes=["SEL4"])
        p.op("pool", lambda: G.affine_select(out=SEL4[:], in_=SEL4[:], pattern=[[-1, 4], [0, 128]], compare_op=ALU.is_equal, fill=0.0, base=0, channel_multiplier=1), reads=["SEL4"], writes=["SEL4"])
        Pbw = [p.sb("Pbw", [128, 515]) for _ in range(2)]
        tails = p.sb("tails", [128, 12, 3])
        p.op("pool", lambda: G.memset(tails[:], 0.0), writes=["tails"])
        Sf = p.sb("Sf", [128, 4, 128]); Sb = p.sb("Sb", [128, 4, 128], BF16)
        p.op("pool", lambda: G.memset(Sf[:], 0.0), writes=["Sf"])
        p.op("pool", lambda: G.memset(Sb[:], 0.0), writes=["Sb"])
        hb1 = p.sb("hbuf", [128, 8, TT], BF16)
        K.hbuf = [hb1, hb1]
        cf = p.sb("cf", [128, TT]); sqb = p.sb("sqb", [128, TT], BF16); rinv = p.sb("rinv", [128, TT])
        qT = p.sb("qT", [128, 4, TT], BF16); kT = p.sb("kT", [128, 4, TT], BF16); vT = p.sb("vT", [128, 4, TT], BF16); qdT = p.sb("qdT", [128, 4, TT], BF16)
        EG = p.sb("EG", [128, 4, TT])
        oT = p.sb("oT", [128, 4, TT], BF16); zs = p.sb("zs", [128, TT]); yT1 = p.sb("yT", [128, 4, TT], BF16)
        yT = [yT1, yT1]
        S4 = lambda n: p.sb(n, [4, TT])
        beT, lnbT, gT, gcT, ngcT, cbT, ecbT, erevT = S4("beT"), S4("lnbT"), S4("gT"), S4("gcT"), S4("ngcT"), S4("cbT"), S4("ecbT"), S4("erevT")
        NG = K.gdn_ng

        def alloc_group():
            d = {}
            d["gcbd"] = p.sb("gcbd", [4, 4, C]); d["ngcbd"] = p.sb("ngcbd", [4, 4, C]); d["cbbd"] = p.sb("cbbd", [4, 4, C])
            d["DA"] = p.sb("DA", [C, 4 * C], BF16); d["DAT"] = p.sb("DAT", [C, 4 * C], BF16); d["DQT"] = p.sb("DQT", [C, 4 * C], BF16)
            d["X"] = [p.sb("X", [C, 4 * C], BF16) for _ in range(2)]; d["Y_"] = [p.sb("Yi", [C, 4 * C], BF16) for _ in range(2)]
            d["Q"] = p.sb("Q", [C, 4 * C], BF16); d["Qt"] = p.sb("Qt", [C, 4 * C], BF16)
            d["Xs"] = [p.sb("Xs", [C, 4 * C], BF16) for _ in range(2)]; d["Ys"] = [p.sb("Ys", [C, 4 * C], BF16) for _ in range(2)]
            d["attnT"] = p.sb("attnT", [C, 4 * C], BF16); d["tok"] = p.sb("tok", [C, 12]); d["tokc"] = p.sb("tokc", [C, 8])
            d["RHSw"] = p.sb("RHSw", [C, 512], BF16); d["RHSu"] = p.sb("RHSu", [C, 512], BF16); d["kdec"] = p.sb("kdec", [C, 512], BF16)
            d["nwT"] = p.sb("nwT", [128, 4 * C], BF16); d["vnb"] = p.sb("vnb", [C, 4, 128], BF16)
            return d
        GB = [alloc_group() for _ in range(NG)]
        I64b = bc(K.identb[0:C, 0:C], [C, 4, C], 1)
        IND4 = IND4t[:]
        v3 = lambda t: t[:].rearrange("m (h l) -> m h l", h=4)
        for t in range(K.S // TT):
            hb, hk = load_hT(K, t)
            for c in range(12):
                ps, pk = proj(K, W, "W", c * 128, 128, hb, hk)
                if c >= 8:
                    conv_chunk(K, ps, pk, Pbw[c % 2], "Pbw%d" % (c % 2), cw, cb, c, vT[:, c - 8, :], "vT", AF.Silu, tails=tails, tkey="tails")
                    continue
                conv_chunk(K, ps, pk, Pbw[c % 2], "Pbw%d" % (c % 2), cw, cb, c, cf[:], "cf", AF.Silu, tails=tails, tkey="tails")
                p.op("act", lambda: A.activation(out=sqb[:], in_=cf[:], func=AF.Square), reads=["cf"], writes=["sqb"])
                ps2, pk2 = ps_next(K)
                p.op("pe", lambda ps2=ps2: T.matmul(ps2[:], lhsT=K.onesb[:], rhs=sqb[:], start=True, stop=True), reads=["sqb", "onesb"], writes=[pk2])
                p.op("act", lambda ps2=ps2: A.activation(out=rinv[:], in_=ps2[:], func=AF.Sqrt, bias=1e-6), reads=[pk2], writes=["rinv"])
                p.op("dve", lambda: V.reciprocal(out=rinv[:], in_=rinv[:]), reads=["rinv"], writes=["rinv"])
                dst, dk, sc = (qT[:, c, :], "qT", 128 ** -0.5) if c < 4 else (kT[:, c - 4, :], "kT", 1.0)
                p.op("dve", lambda dst=dst, sc=sc: V.scalar_tensor_tensor(out=dst, in0=cf[:], scalar=sc, in1=rinv[:], op0=ALU.mult, op1=ALU.mult), reads=["cf", "rinv"], writes=[dk])
            ps, pk = proj(K, W, "W", 2048, 4, hb, hk)
            p.op("act", lambda ps=ps: A.activation(out=beT[:], in_=ps[0:4, :], func=AF.Sigmoid), reads=[pk], writes=["beT"])
            p.op("act", lambda: A.activation(out=lnbT[:], in_=beT[:], func=AF.Ln), reads=["beT"], writes=["lnbT"])
            ps, pk = proj(K, W, "W", 2052, 4, hb, hk)
            p.op("act", lambda ps=ps: A.activation(out=gT[:], in_=ps[0:4, :], func=AF.Exp, bias=sm4[:, 0:1]), reads=[pk, "cw"], writes=["gT"])
            p.op("act", lambda: A.activation(out=gT[:], in_=gT[:], func=AF.Ln, bias=1.0), reads=["gT"], writes=["gT"])
            p.op("dve", lambda: V.tensor_scalar(out=gT[:], in0=gT[:], scalar1=sm4[:, 2:3], scalar2=None, op0=ALU.mult), reads=["gT", "sm4"], writes=["gT"])
            p.op("dve", lambda: V.tensor_tensor_scan(out=gcT[:], data0=fl(rmaskC[:]), data1=gT[:], initial=0.0, op0=ALU.mult, op1=ALU.add), reads=["gT", "gmask"], writes=["gcT"])
            p.op("dve", lambda: V.tensor_scalar(out=ngcT[:], in0=gcT[:], scalar1=-1.0, scalar2=None, op0=ALU.mult), reads=["gcT"], writes=["ngcT"])
            p.op("dve", lambda: V.tensor_tensor(out=cbT[:], in0=gcT[:], in1=lnbT[:], op=ALU.add), reads=["gcT", "lnbT"], writes=["cbT"])
            p.op("act", lambda: A.activation(out=ecbT[:], in_=cbT[:], func=AF.Exp), reads=["cbT"], writes=["ecbT"])
            g3 = gcT[:].rearrange("h (c l) -> h c l", c=TT // C)
            p.op("dve", lambda: V.tensor_tensor(out=erevT[:].rearrange("h (c l) -> h c l", c=TT // C), in0=g3[:, :, C - 1:C].to_broadcast([4, TT // C, C]), in1=g3, op=ALU.subtract), reads=["gcT"], writes=["erevT"])
            p.op("act", lambda: A.activation(out=erevT[:], in_=erevT[:], func=AF.Exp), reads=["erevT"], writes=["erevT"])
            for h in range(4):
                ps, pk = ps_next(K)
                p.op("pe", lambda ps=ps, h=h: T.matmul(ps[:], lhsT=SEL4[:, h, :], rhs=gcT[:], start=True, stop=True), reads=["SEL4", "gcT"], writes=[pk])
                p.op("act", lambda ps=ps, h=h: A.activation(out=EG[:, h, :], in_=ps[:], func=AF.Exp), reads=[pk], writes=["EG"])
            p.op("dve", lambda: V.tensor_tensor(out=qdT[:], in0=qT[:], in1=EG[:], op=ALU.mult), reads=["qT", "EG"], writes=["qdT"])
            y = yT[0]; yk = "yT0"
            def chunk_gen(cx, gi):
                d = GB[gi]
                gcbd, ngcbd, cbbd, DA, DAT, DQT, X, Y_, Q, Qt, Xs, Ys = (d[n] for n in ("gcbd", "ngcbd", "cbbd", "DA", "DAT", "DQT", "X", "Y_", "Q", "Qt", "Xs", "Ys"))
                attnT, tok, RHSw, RHSu, kdec, nwT, vnb = (d[n] for n in ("attnT", "tok", "RHSw", "RHSu", "kdec", "nwT", "vnb"))
                tokc = d["tokc"]
                Em, Emt, M1, M1t = DA, DAT, Xs[0], Ys[0]
                kq = lambda n: "%s_g%d" % (n, gi)
                sl = slice(cx * C, (cx + 1) * C)
                p.op("pool", lambda: G.tensor_tensor(out=gcbd[:], in0=gcT[:, sl].unsqueeze(1).to_broadcast([4, 4, C]), in1=IND4, op=ALU.mult), reads=["gcT", "gmask"], writes=[kq("gcbd")])
                p.op("pool", lambda: G.tensor_tensor(out=ngcbd[:], in0=ngcT[:, sl].unsqueeze(1).to_broadcast([4, 4, C]), in1=IND4, op=ALU.mult), reads=["ngcT", "gmask"], writes=[kq("ngcbd")])
                p.op("pool", lambda: G.tensor_tensor(out=cbbd[:], in0=cbT[:, sl].unsqueeze(1).to_broadcast([4, 4, C]), in1=IND4, op=ALU.mult), reads=["cbT", "gmask"], writes=[kq("cbbd")])
                yield
                Tc, tck = ps_next(K)

                def emit_tc(Tc=Tc):
                    T.matmul(Tc[0:C, 0:4], lhsT=cbT[:, sl], rhs=K.identf[0:4, 0:4], start=True, stop=True)
                    return T.matmul(Tc[0:C, 4:8], lhsT=ngcT[:, sl], rhs=K.identf[0:4, 0:4], start=True, stop=True)
                p.op("pe", emit_tc, reads=["cbT", "ngcT", "identf"], writes=[tck])
                p.op("act", lambda Tc=Tc: A.copy(out=tokc[:], in_=Tc[0:C, 0:8]), reads=[tck], writes=[kq("tokc")])
                for (dst, dk, c0_, rowbd, rk, msk) in ((DA, kq("DA"), 0, ngcbd, kq("ngcbd"), M_gt),
                                                       (DAT, kq("DAT"), 4, cbbd, kq("cbbd"), M_lt),
                                                       (DQT, kq("DQT"), 4, gcbd, kq("gcbd"), M_le)):
                    Dp, dpk = ps_next(K)

                    def emit_d(Dp=Dp, rowbd=rowbd, msk=msk):
                        T.matmul(Dp[0:C, 0:4 * C], lhsT=K.onesf[0:4, 0:C], rhs=fl(rowbd[:]), start=True, stop=False)
                        return T.matmul(Dp[0:C, 0:4 * C], lhsT=K.identb[0:C, 0:C], rhs=fl(msk[:]), start=False, stop=True)
                    p.op("pe", emit_d, reads=[rk, "gmask", "onesf", "identb"], writes=[dpk])
                    for h in range(4):
                        p.op("act", lambda Dp=Dp, dst=dst, h=h, c0_=c0_: A.activation(out=dst[:, h * C:(h + 1) * C], in_=Dp[0:C, h * C:(h + 1) * C], func=AF.Exp, bias=tokc[:, c0_ + h:c0_ + h + 1]),
                             reads=[dpk, kq("tokc")], writes=[dk])
                yield
                Gp, gk = ps_next(K)
                Qp, qk = ps_next(K)

                def emit_g(Gp=Gp, Qp=Qp):
                    for h in range(4):
                        T.matmul(Gp[0:C, h * C:(h + 1) * C], lhsT=kT[:, h, sl], rhs=kT[:, h, sl], start=True, stop=True)
                    for h in range(4):
                        inst = T.matmul(Qp[0:C, h * C:(h + 1) * C], lhsT=kT[:, h, sl], rhs=qT[:, h, sl], start=True, stop=True)
                    return inst
                p.op("pe", emit_g, reads=["kT", "qT"], writes=[gk, qk])
                p.op("dve", lambda Gp=Gp: V.scalar_tensor_tensor(out=X[0][:], in0=Gp[0:C, 0:4 * C], scalar=-1.0, in1=DA[:], op0=ALU.mult, op1=ALU.mult), reads=[gk, kq("DA")], writes=[kq("X0")])
                p.op("dve", lambda Gp=Gp: V.scalar_tensor_tensor(out=Y_[0][:], in0=Gp[0:C, 0:4 * C], scalar=-1.0, in1=DAT[:], op0=ALU.mult, op1=ALU.mult), reads=[gk, kq("DAT")], writes=[kq("Y0")])
                p.op("dve", lambda Qp=Qp: V.tensor_tensor(out=attnT[:], in0=Qp[0:C, 0:4 * C], in1=DQT[:], op=ALU.mult), reads=[qk, kq("DQT")], writes=[kq("attnT")])
                yield
                fl4 = lambda m: m[:].rearrange("k h l -> k (h l)")

                def mm4(out_ps, lhs, rhs):
                    def emit():
                        for h in range(4):
                            inst = T.matmul(out_ps[0:C, h * C:(h + 1) * C], lhsT=lhs[:, h * C:(h + 1) * C], rhs=rhs[:, h * C:(h + 1) * C], start=True, stop=True)
                        return inst
                    return emit
                p.op("dve", lambda: V.tensor_tensor(out=X[1][:], in0=X[0][:], in1=fl4(BMb), op=ALU.mult), reads=[kq("X0"), "gmask"], writes=[kq("X1")])
                p.op("pool", lambda: G.tensor_tensor(out=Y_[1][:], in0=Y_[0][:], in1=fl4(BMb), op=ALU.mult), reads=[kq("Y0"), "gmask"], writes=[kq("Y1")])
                p.op("pool", lambda: G.tensor_tensor(out=v3(Q), in0=v3(Y_[1]), in1=I64b, op=ALU.add), reads=[kq("Y1"), "identf"], writes=[kq("Q")])
                p.op("dve", lambda: V.tensor_tensor(out=v3(Qt), in0=v3(X[1]), in1=I64b, op=ALU.add), reads=[kq("X1"), "identf"], writes=[kq("Qt")])
                xb, yb = X[1], Y_[1]; xbk, ybk = kq("X1"), kq("Y1")
                for i in range(2):
                    xn, yn = (Xs[i % 2], Ys[i % 2]); xnk, ynk = kq("Xs%d" % (i % 2)), kq("Ys%d" % (i % 2))
                    Xp, xk = ps_next(K); Yp, ypk = ps_next(K)
                    p.op("pe", mm4(Xp, yb, xb), reads=[xbk, ybk], writes=[xk])
                    p.op("pe", mm4(Yp, xb, yb), reads=[xbk, ybk], writes=[ypk])
                    p.op("act", lambda Xp=Xp, xn=xn: A.copy(out=xn[:], in_=Xp[0:C, 0:4 * C]), reads=[xk], writes=[xnk])
                    p.op("dve", lambda Yp=Yp, yn=yn: V.tensor_copy(out=yn[:], in_=Yp[0:C, 0:4 * C]), reads=[ypk], writes=[ynk])
                    yield
                    Dq, dqk = ps_next(K); Dt, dtk = ps_next(K)
                    p.op("pe", mm4(Dq, xn, Q), reads=[xnk, kq("Q")], writes=[dqk])
                    p.op("pe", mm4(Dt, yn, Qt), reads=[ynk, kq("Qt")], writes=[dtk])
                    p.op("dve", lambda Dq=Dq: V.tensor_tensor(out=Q[:], in0=Dq[0:C, 0:4 * C], in1=Q[:], op=ALU.add), reads=[dqk, kq("Q")], writes=[kq("Q")])
                    p.op("dve", lambda Dt=Dt: V.tensor_tensor(out=Qt[:], in0=Dt[0:C, 0:4 * C], in1=Qt[:], op=ALU.add), reads=[dtk, kq("Qt")], writes=[kq("Qt")])
                    yield
                    xb, yb, xbk, ybk = xn, yn, xnk, ynk
                for li, Dm in enumerate(Dlist):
                    last = (li == len(Dlist) - 1)
                    p.op("pool", lambda Dm=Dm: G.tensor_tensor(out=Em[:], in0=Y_[0][:], in1=fl4(Dm), op=ALU.mult), reads=[kq("Y0"), "gmask"], writes=[kq("DA")])
                    p.op("pool", lambda Dm=Dm: G.tensor_tensor(out=Emt[:], in0=X[0][:], in1=fl4(Dm), op=ALU.mult), reads=[kq("X0"), "gmask"], writes=[kq("DAT")])
                    yield
                    P1, p1k = ps_next(K)
                    p.op("pe", mm4(P1, Emt, Q), reads=[kq("DAT"), kq("Q")], writes=[p1k])
                    p.op("act", lambda P1=P1: A.copy(out=M1[:], in_=P1[0:C, 0:4 * C]), reads=[p1k], writes=[kq("Xs0")])
                    if not last:
                        P1t, p1tk = ps_next(K)
                        p.op("pe", mm4(P1t, Em, Qt), reads=[kq("DA"), kq("Qt")], writes=[p1tk])
                        p.op("dve", lambda P1t=P1t: V.tensor_copy(out=M1t[:], in_=P1t[0:C, 0:4 * C]), reads=[p1tk], writes=[kq("Ys0")])
                    yield
                    P2, p2k = ps_next(K)
                    p.op("pe", mm4(P2, Qt, M1), reads=[kq("Qt"), kq("Xs0")], writes=[p2k])
                    if not last:
                        P2t, p2tk = ps_next(K)
                        p.op("pe", mm4(P2t, Q, M1t), reads=[kq("Q"), kq("Ys0")], writes=[p2tk])
                    p.op("dve", lambda P2=P2: V.tensor_tensor(out=Q[:], in0=P2[0:C, 0:4 * C], in1=Q[:], op=ALU.add), reads=[p2k, kq("Q")], writes=[kq("Q")])
                    if not last:
                        p.op("dve", lambda P2t=P2t: V.tensor_tensor(out=Qt[:], in0=P2t[0:C, 0:4 * C], in1=Qt[:], op=ALU.add), reads=[p2tk, kq("Qt")], writes=[kq("Qt")])
                    yield
                yield
                Tp, tk = ps_next(K)

                def emit_t(Tp=Tp):
                    T.matmul(Tp[0:C, 0:4], lhsT=beT[:, sl], rhs=K.identf[0:4, 0:4], start=True, stop=True)
                    T.matmul(Tp[0:C, 4:8], lhsT=ecbT[:, sl], rhs=K.identf[0:4, 0:4], start=True, stop=True)
                    return T.matmul(Tp[0:C, 8:12], lhsT=erevT[:, sl], rhs=K.identf[0:4, 0:4], start=True, stop=True)
                p.op("pe", emit_t, reads=["beT", "ecbT", "erevT", "identf"], writes=[tk])
                p.op("act", lambda Tp=Tp: A.copy(out=tok[:], in_=Tp[0:C, 0:12]), reads=[tk], writes=[kq("tok")])
                Kp, kk = ps_next(K)

                def emit_k(Kp=Kp):
                    for h in range(4):
                        inst = T.matmul(Kp[0:C, h * 128:(h + 1) * 128], lhsT=kT[:, h, sl], rhs=K.identb[:], start=True, stop=True)
                    return inst
                p.op("pe", emit_k, reads=["kT", "identb"], writes=[kk])
                k3 = Kp[0:C, :].rearrange("m (h d) -> m h d", h=4)
                p.op("dve", lambda k3=k3: V.tensor_tensor(out=RHSw[:].rearrange("m (h d) -> m h d", h=4), in0=k3, in1=bc(tok[:, 4:8], [C, 4, 128], 2), op=ALU.mult), reads=[kk, kq("tok")], writes=[kq("RHSw")])
                p.op("dve", lambda k3=k3: V.tensor_tensor(out=kdec[:].rearrange("m (h d) -> m h d", h=4), in0=k3, in1=bc(tok[:, 8:12], [C, 4, 128], 2), op=ALU.mult), reads=[kk, kq("tok")], writes=[kq("kdec")])
                Vp, vk = ps_next(K)

                def emit_v(Vp=Vp):
                    for h in range(4):
                        inst = T.matmul(Vp[0:C, h * 128:(h + 1) * 128], lhsT=vT[:, h, sl], rhs=K.identb[:], start=True, stop=True)
                    return inst
                p.op("pe", emit_v, reads=["vT", "identb"], writes=[vk])
                p.op("dve", lambda Vp=Vp: V.tensor_tensor(out=RHSu[:].rearrange("m (h d) -> m h d", h=4), in0=Vp[0:C, :].rearrange("m (h d) -> m h d", h=4), in1=bc(tok[:, 0:4], [C, 4, 128], 2), op=ALU.mult),
                     reads=[vk, kq("tok")], writes=[kq("RHSu")])
                yield
                Wp, wk = ps_next(K)

                def emit_w(Wp=Wp):
                    for h in range(4):
                        inst = T.matmul(Wp[:, h * C:(h + 1) * C], lhsT=RHSw[:, h * 128:(h + 1) * 128], rhs=Q[:, h * C:(h + 1) * C], start=True, stop=True)
                    return inst
                p.op("pe", emit_w, reads=[kq("RHSw"), kq("Q")], writes=[wk])
                p.op("act", lambda Wp=Wp: A.mul(out=nwT[:], in_=Wp[:, 0:4 * C], mul=-1.0), reads=[wk], writes=[kq("nwT")])
                yield
                Np, nk = ps_next(K)

                def emit_n(Np=Np):
                    for h in range(4):
                        T.matmul(Np[0:C, h * 128:(h + 1) * 128], lhsT=Q[:, h * C:(h + 1) * C], rhs=RHSu[:, h * 128:(h + 1) * 128], start=True, stop=False)
                        inst = T.matmul(Np[0:C, h * 128:(h + 1) * 128], lhsT=nwT[:, h * C:(h + 1) * C], rhs=Sb[:, h, :], start=False, stop=True)
                    return inst
                p.op("pe", emit_n, reads=[kq("Q"), kq("RHSu"), kq("nwT"), "Sb"], writes=[nk])
                p.op("act", lambda Np=Np: A.copy(out=vnb[:].rearrange("m h e -> m (h e)"), in_=Np[0:C, :]), reads=[nk], writes=[kq("vnb")])
                Op, ok_ = ps_next(K)

                def emit_o(Op=Op):
                    for h in range(4):
                        T.matmul(Op[:, h * C:(h + 1) * C], lhsT=Sb[:, h, :], rhs=qdT[:, h, sl], start=True, stop=False)
                        inst = T.matmul(Op[:, h * C:(h + 1) * C], lhsT=vnb[:, h, :], rhs=attnT[:, h * C:(h + 1) * C], start=False, stop=True)
                    return inst
                p.op("pe", emit_o, reads=["Sb", "qdT", kq("vnb"), kq("attnT")], writes=[ok_])
                p.op("act", lambda Op=Op: A.copy(out=oT[:, :, sl], in_=Op[:, 0:4 * C].rearrange("e (h c) -> e h c", h=4)), reads=[ok_], writes=["oT"])
                Ip, ik = ps_next(K)

                def emit_i(Ip=Ip):
                    for h in range(4):
                        inst = T.matmul(Ip[:, h * 128:(h + 1) * 128], lhsT=kdec[:, h * 128:(h + 1) * 128], rhs=vnb[:, h, :], start=True, stop=True)
                    return inst
                p.op("pe", emit_i, reads=[kq("kdec"), kq("vnb")], writes=[ik])
                for h in range(4):
                    p.op("dve", lambda h=h, Ip=Ip: V.scalar_tensor_tensor(out=Sf[:, h, :], in0=Sf[:, h, :], scalar=EG[:, h, cx * C + C - 1:cx * C + C], in1=Ip[:, h * 128:(h + 1) * 128], op0=ALU.mult, op1=ALU.add),
                         reads=["Sf", "EG", ik], writes=["Sf"])
                p.op("act", lambda: A.copy(out=Sb[:], in_=Sf[:]), reads=["Sf"], writes=["Sb"])
            for b0 in range(0, TT // C, NG):
                gens = [chunk_gen(cx, cx - b0) for cx in range(b0, min(TT // C, b0 + NG))]
                while gens:
                    for g_ in list(gens):
                        try:
                            next(g_)
                        except StopIteration:
                            gens.remove(g_)
            for h in range(4):
                p.op("act", lambda h=h: A.activation(out=sqb[:], in_=oT[:, h, :], func=AF.Square), reads=["oT"], writes=["sqb"])
                ps2, pk2 = ps_next(K)
                p.op("pe", lambda ps2=ps2: T.matmul(ps2[:], lhsT=K.onesb[:], rhs=sqb[:], start=True, stop=True), reads=["sqb", "onesb"], writes=[pk2])
                p.op("act", lambda ps2=ps2: A.activation(out=rinv[:], in_=ps2[:], func=AF.Sqrt, scale=1.0 / 128, bias=1e-6), reads=[pk2], writes=["rinv"])
                p.op("dve", lambda: V.reciprocal(out=rinv[:], in_=rinv[:]), reads=["rinv"], writes=["rinv"])
                ps, pk = proj(K, W, "W", 1536 + h * 128, 128, hb, hk)
                p.op("act", lambda ps=ps: A.activation(out=zs[:], in_=ps[:], func=AF.Silu), reads=[pk], writes=["zs"])
                p.op("dve", lambda h=h: V.scalar_tensor_tensor(out=cf[:], in0=oT[:, h, :], scalar=nw[:, 0:1], in1=rinv[:], op0=ALU.mult, op1=ALU.mult), reads=["oT", "rinv", "cw"], writes=["cf"])
                p.op("dve", lambda h=h, y=y: V.tensor_tensor(out=y[:, h, :], in0=cf[:], in1=zs[:], op=ALU.mult), reads=["cf", "zs"], writes=[yk])
            p.dma("sp", [(K.Y[YCH["b"]:YCH["b"] + 4, :, t * TT:(t + 1) * TT].rearrange("c p t -> p c t"), y[:])], reads=[yk], writes=["Y"])


def phase_s5(K, l):
    import math
    p, nc = K.p, K.nc
    A, V, G, T = nc.scalar, nc.vector, nc.gpsimd, nc.tensor
    S = K.S
    NCH = S // 32
    PI = math.pi
    b2 = lambda ap, shape: ap.unsqueeze(2).to_broadcast(list(shape))
    b1 = lambda ap, shape: ap.unsqueeze(1).to_broadcast(list(shape))
    with p.scope():
        Un = p.sb("Un", [NCH, 32, 384], BF16)
        with p.scope():
            Wu = p.sb("Wu", [128, 8, 384], BF16)
            load_w(K, Wu[:], "Wu", K.inp["w_in"][l], C_U, C_U + 384)
            K.hbuf = [p.sb("hbuf", [128, 8, TT], BF16) for _ in range(2)]
            ub = [p.sb("ub", [128, 384], BF16) for _ in range(2)]
            for t in range(S // TT):
                hb, hk = load_hT(K, t)
                for s4 in range(4):
                    i = t * 4 + s4
                    ps, pk = ps_next(K)

                    def emit(ps=ps, s4=s4, hb=hb):
                        for kc in range(8):
                            inst = T.matmul(ps[:, 0:384], lhsT=hb[:, kc, s4 * 128:(s4 + 1) * 128], rhs=Wu[:, kc, :], start=(kc == 0), stop=(kc == 7))
                        return inst
                    p.op("pe", emit, reads=["Wu", hk], writes=[pk])
                    u_ = ub[i % 2]; uk = "ub%d" % (i % 2)
                    p.op("act", lambda ps=ps, u_=u_: A.copy(out=u_[:], in_=ps[:, 0:384]), reads=[pk], writes=[uk])
                    p.dma("sp", [(K.Us[i * 128:(i + 1) * 128, :], u_[:])], reads=[uk], writes=["Us"])
        p.dma("sp", [(Un[:], K.Us.rearrange("(n s) c -> n s c", s=32))], reads=["Us"], writes=["Un"])
        with p.scope():
            sml = p.sb("sml", [128, 3, 24])
            Br = p.sb("Br", [128, 24, 16]); Bi = p.sb("Bi", [128, 24, 16]); Cr = p.sb("Cr", [128, 24, 16]); Ci = p.sb("Ci", [128, 24, 16])
            dB = p.sb("dB", [128, 384])
            p.dma("sp", [(sml[:, 0, :], K.inp["s5_lamr"][l]), (sml[:, 1, :], K.inp["s5_lami"][l]), (sml[:, 2, :], K.inp["s5_ldt"][l]),
                         (Br[:], K.inp["s5_br"][l]), (Bi[:], K.inp["s5_bi"][l]), (Cr[:], K.inp["s5_cr"][l]), (Ci[:], K.inp["s5_ci"][l]),
                         (dB[:], K.inp["s5_dB"][l])], writes=["tab"])
            tb = "tab"
            lamr, lami = sml[:, 0, :], sml[:, 1, :]
            sc = p.sb("sc", [128, 12, 24])
            dt, ar, ai, lam2, nr, ni, cr, ci, tA, tB = [sc[:, j, :] for j in range(10)]
            p.op("act", lambda: A.activation(out=dt, in_=sml[:, 2, :], func=AF.Exp), reads=[tb], writes=[tb])
            p.op("dve", lambda: V.tensor_tensor(out=ar, in0=lamr, in1=dt, op=ALU.mult), reads=[tb], writes=[tb])
            p.op("dve", lambda: V.tensor_tensor(out=ai, in0=lami, in1=dt, op=ALU.mult), reads=[tb], writes=[tb])
            ones33 = p.sb("ones33", [128, 33]); tg = p.sb("tg", [128, 33])
            p.op("pool", lambda: G.memset(ones33[:], 1.0), writes=[tb])
            p.op("dve", lambda: V.tensor_tensor_scan(out=tg[:], data0=ones33[:], data1=ones33[:], initial=-1.0, op0=ALU.mult, op1=ALU.add), reads=[tb], writes=[tb])
            T3 = lambda n: p.sb(n, [128, 24, 33])
            arg, mag, sn, cs, Pr, Pi, Nr, Ni = T3("arg"), T3("mag"), T3("sn"), T3("cs"), T3("Pr"), T3("Pi"), T3("Nr"), T3("Ni")
            sh3 = [128, 24, 33]
            p.op("dve", lambda: V.tensor_tensor(out=arg[:], in0=b2(ar, sh3), in1=b1(tg[:], sh3), op=ALU.mult), reads=[tb], writes=[tb])
            p.op("act", lambda: A.activation(out=mag[:], in_=arg[:], func=AF.Exp), reads=[tb], writes=[tb])
            p.op("act", lambda: A.activation(out=Nr[:], in_=arg[:], func=AF.Exp, scale=-1.0), reads=[tb], writes=[tb])
            p.op("dve", lambda: V.tensor_tensor(out=arg[:], in0=b2(ai, sh3), in1=b1(tg[:], sh3), op=ALU.mult), reads=[tb], writes=[tb])
            MAGIC = 12582912.0
            for (dst, off) in ((sn, 0.0), (cs, 0.5 * PI)):
                p.op("dve", lambda dst=dst, off=off: V.tensor_scalar(out=dst[:], in0=arg[:], scalar1=off, scalar2=1.0 / (2 * PI), op0=ALU.add, op1=ALU.mult), reads=[tb], writes=[tb])
                p.op("dve", lambda dst=dst: V.tensor_scalar(out=dst[:], in0=dst[:], scalar1=MAGIC, scalar2=None, op0=ALU.add), reads=[tb], writes=[tb])
                p.op("dve", lambda dst=dst: V.tensor_scalar(out=dst[:], in0=dst[:], scalar1=-MAGIC, scalar2=None, op0=ALU.add), reads=[tb], writes=[tb])
                p.op("dve", lambda dst=dst: V.scalar_tensor_tensor(out=dst[:], in0=dst[:], scalar=-2 * PI, in1=arg[:], op0=ALU.mult, op1=ALU.add), reads=[tb], writes=[tb])
                if off != 0.0:
                    p.op("dve", lambda dst=dst, off=off: V.tensor_scalar(out=dst[:], in0=dst[:], scalar1=off, scalar2=None, op0=ALU.add), reads=[tb], writes=[tb])
            p.op("act", lambda: A.activation(out=sn[:], in_=sn[:], func=AF.Sin), reads=[tb], writes=[tb])
            p.op("act", lambda: A.activation(out=cs[:], in_=cs[:], func=AF.Sin), reads=[tb], writes=[tb])
            p.op("dve", lambda: V.tensor_tensor(out=Pr[:], in0=mag[:], in1=cs[:], op=ALU.mult), reads=[tb], writes=[tb])
            p.op("dve", lambda: V.tensor_tensor(out=Pi[:], in0=mag[:], in1=sn[:], op=ALU.mult), reads=[tb], writes=[tb])
            p.op("dve", lambda: V.scalar_tensor_tensor(out=Ni[:], in0=Nr[:], scalar=-1.0, in1=sn[:], op0=ALU.mult, op1=ALU.mult), reads=[tb], writes=[tb])
            p.op("dve", lambda: V.tensor_tensor(out=Nr[:], in0=Nr[:], in1=cs[:], op=ALU.mult), reads=[tb], writes=[tb])
            P1r, P1i = Pr[:, :, 1], Pi[:, :, 1]
            vt = lambda o, a, b, op: p.op("dve", lambda: V.tensor_tensor(out=o, in0=a, in1=b, op=op), reads=[tb], writes=[tb])
            vt(lam2, lamr, lamr, ALU.mult); vt(tA, lami, lami, ALU.mult); vt(lam2, lam2, tA, ALU.add)
            p.op("dve", lambda: V.reciprocal(out=lam2, in_=lam2), reads=[tb], writes=[tb])
            p.op("dve", lambda: V.tensor_scalar(out=tB, in0=P1r, scalar1=-1.0, scalar2=None, op0=ALU.add), reads=[tb], writes=[tb])
            vt(nr, tB, lamr, ALU.mult); vt(tA, P1i, lami, ALU.mult); vt(nr, nr, tA, ALU.add)
            vt(ni, P1i, lamr, ALU.mult); vt(tA, tB, lami, ALU.mult); vt(ni, ni, tA, ALU.subtract)
            vt(cr, nr, lam2, ALU.mult); vt(ci, ni, lam2, ALU.mult)
            Bbr = p.sb("Bbr", [128, 24, 16]); Bbi = p.sb("Bbi", [128, 24, 16]); tq = p.sb("tq", [128, 24, 16])
            Ba = p.sb("Ba", [128, 24, 16]); Bb = p.sb("Bb", [128, 24, 16]); Ca = p.sb("Ca", [128, 24, 16]); Cb = p.sb("Cb", [128, 24, 16])
            sh16 = [128, 24, 16]
            vt(Bbr[:], Br[:], b2(cr, sh16), ALU.mult); vt(tq[:], Bi[:], b2(ci, sh16), ALU.mult); vt(Bbr[:], Bbr[:], tq[:], ALU.subtract)
            vt(Bbi[:], Bi[:], b2(cr, sh16), ALU.mult); vt(tq[:], Br[:], b2(ci, sh16), ALU.mult); vt(Bbi[:], Bbi[:], tq[:], ALU.add)
            top, bot = slice(0, 64), slice(64, 128)
            neg = lambda o, a: p.op("dve", lambda: V.tensor_scalar(out=o, in0=a, scalar1=-1.0, scalar2=None, op0=ALU.mult), reads=[tb], writes=[tb])
            cp = lambda o, a: p.op("dve", lambda: V.tensor_copy(out=o, in_=a), reads=[tb], writes=[tb])
            cp(Ba[top], Bbr[top]); cp(Ba[bot], Bbi[bot]); neg(Bb[top], Bbi[top]); cp(Bb[bot], Bbr[bot])
            cp(Ca[top], Cr[top]); neg(Ca[bot], Ci[bot]); neg(Cb[top], Ci[top]); neg(Cb[bot], Cr[bot])
            JT = p.sb("JT", [128, 128])
            p.op("pool", lambda: G.memset(JT[:], 0.0), writes=[tb])
            p.op("pool", lambda: G.affine_select(out=JT[:], in_=JT[:], pattern=[[-1, 128]], compare_op=ALU.not_equal, fill=-1.0, base=-64, channel_multiplier=1), reads=[tb], writes=[tb])
            p.op("pool", lambda: G.affine_select(out=JT[:], in_=JT[:], pattern=[[1, 128]], compare_op=ALU.not_equal, fill=1.0, base=-64, channel_multiplier=-1), reads=[tb], writes=[tb])
            MK = p.sb("MK", [128, 4, 32, 16], BF16)
            p.op("pool", lambda: G.memset(MK[:], 1.0), writes=[tb])
            p.op("pool", lambda: G.affine_select(out=MK[:], in_=MK[:], pattern=[[-128, 4], [16, 32], [0, 16]], compare_op=ALU.is_ge, fill=0.0, base=15, channel_multiplier=-1), reads=[tb], writes=[tb])
            UT = p.sb("UT", [128, 24, 4, NCH], BF16)
            Ug = [p.sb("Ug", [NCH, 512], BF16) for _ in range(2)]
            for g in range(24):
                u_ = Ug[g % 2]; uk = "Ug%d" % (g % 2)
                p.op("pool", lambda g=g, u_=u_: G.tensor_copy(out=u_[:].rearrange("n (s h) -> n s h", h=16), in_=Un[:, :, g * 16:(g + 1) * 16]), reads=["Un"], writes=[uk])
                ps, pk = ps_next(K)

                def emit(ps=ps, u_=u_):
                    for kt in range(4):
                        inst = T.matmul(ps[:, kt * NCH:(kt + 1) * NCH], lhsT=u_[:, kt * 128:(kt + 1) * 128], rhs=K.identb[0:NCH, 0:NCH], start=True, stop=True)
                    return inst
                p.op("pe", emit, reads=[uk, "identb"], writes=[pk])
                p.op("act", lambda ps=ps, g=g: A.copy(out=UT[:, g, :, :], in_=ps[:, 0:4 * NCH].rearrange("q (k n) -> q k n", k=4)), reads=[pk], writes=["UT"])
            NGS = 3
            BsR = [p.sb("BsR", [128, 32, 16]) for _ in range(NGS)]; t1s = [p.sb("t1", [128, 33, 16]) for _ in range(NGS)]
            CLr = [p.sb("CLr", [128, 33, 16]) for _ in range(NGS)]
            BsRT = [p.sb("BsRT", [128, 4, 128], BF16) for _ in range(2)]
            Sa = p.sb("Sa", [128, 24, NCH]); Sbb = p.sb("Sbb", [128, 24, NCH])
            sh_b = [128, 32, 16]; sh_c = [128, 33, 16]

            def make_bsr(g, sl_=None):
                sl_ = g % 2 if sl_ is None else sl_
                t1 = t1s[sl_]; t1k = "t1_%d" % sl_
                o = BsR[sl_]; ok = "BsR%d" % sl_
                p.op("dve", lambda: V.tensor_tensor(out=o[:], in0=b2(Nr[:, g, 0:32], sh_b), in1=b1(Ba[:, g, :], sh_b), op=ALU.mult), reads=[tb], writes=[ok])
                p.op("pool", lambda: G.tensor_tensor(out=t1[:, 0:32, :], in0=b2(Ni[:, g, 0:32], sh_b), in1=b1(Bb[:, g, :], sh_b), op=ALU.mult), reads=[tb], writes=[t1k])
                p.op("dve", lambda: V.tensor_tensor(out=o[:], in0=o[:], in1=t1[:, 0:32, :], op=ALU.add), reads=[ok, t1k], writes=[ok])
                return o, ok

            def make_clr(g, sl_=None):
                sl_ = g % 2 if sl_ is None else sl_
                t1 = t1s[sl_]; t1k = "t1_%d" % sl_
                o = CLr[sl_]; ok = "CLr%d" % sl_
                p.op("dve", lambda: V.tensor_tensor(out=o[:], in0=b2(Pr[:, g, :], sh_c), in1=b1(Ca[:, g, :], sh_c), op=ALU.mult), reads=[tb], writes=[ok])
                p.op("pool", lambda: G.tensor_tensor(out=t1[:], in0=b2(Pi[:, g, :], sh_c), in1=b1(Cb[:, g, :], sh_c), op=ALU.mult), reads=[tb], writes=[t1k])
                p.op("dve", lambda: V.tensor_tensor(out=o[:], in0=o[:], in1=t1[:], op=ALU.add), reads=[ok, t1k], writes=[ok])
                return o, ok
            for g in range(24):
                o, ok = make_bsr(g)
                of = o[:].rearrange("q s h -> q (s h)")
                ps, pk = ps_next(K)

                def emit(ps=ps, of=of):
                    for kt in range(4):
                        inst = T.matmul(ps[:, kt * 128:(kt + 1) * 128], lhsT=of[:, kt * 128:(kt + 1) * 128], rhs=K.identf[:], start=True, stop=True)
                    return inst
                p.op("pe", emit, reads=[ok, "identf"], writes=[pk])
                bt = BsRT[g % 2]; btk = "BsRT%d" % (g % 2)
                p.op("act", lambda ps=ps, bt=bt: A.copy(out=bt[:].rearrange("q k n -> q (k n)"), in_=ps[:]), reads=[pk], writes=[btk])
                ps2, pk2 = ps_next(K)

                def emit2(ps2=ps2, bt=bt, g=g):
                    for kt in range(4):
                        inst = T.matmul(ps2[:, 0:NCH], lhsT=bt[:, kt, :], rhs=UT[:, g, kt, :], start=(kt == 0), stop=(kt == 3))
                    return inst
                p.op("pe", emit2, reads=[btk, "UT"], writes=[pk2])
                p.op("act", lambda ps2=ps2, g=g: A.copy(out=Sa[:, g, :], in_=ps2[:, 0:NCH]), reads=[pk2], writes=["Sa"])
            gb = max(d for d in (1, 2, 3, 4, 6, 8, 12, 24) if d * NCH <= 512)
            for g0 in range(0, 24, gb):
                ps, pk = ps_next(K)
                p.op("pe", lambda ps=ps, g0=g0: T.matmul(ps[:, 0:gb * NCH], lhsT=JT[:], rhs=Sa[:, g0:g0 + gb, :].rearrange("q g n -> q (g n)"), start=True, stop=True), reads=["Sa", tb], writes=[pk])
                p.op("act", lambda ps=ps, g0=g0: A.copy(out=Sbb[:, g0:g0 + gb, :].rearrange("q g n -> q (g n)"), in_=ps[:, 0:gb * NCH]), reads=[pk], writes=["Sbb"])
            shn = [128, 24, NCH]
            with p.scope():
                tS = p.sb("tS", [128, 24, NCH]); tS2 = p.sb("tS2", [128, 24, NCH])
                a31r, a31i, a32r, a32i = Pr[:, :, 31], Pi[:, :, 31], Pr[:, :, 32], Pi[:, :, 32]
                p.op("dve", lambda: V.tensor_tensor(out=tS[:], in0=Sa[:], in1=b2(a31i, shn), op=ALU.mult), reads=["Sa", tb], writes=["tS"])
                p.op("pool", lambda: G.tensor_tensor(out=tS2[:], in0=Sbb[:], in1=b2(a31i, shn), op=ALU.mult), reads=["Sbb", tb], writes=["tS2"])
                p.op("dve", lambda: V.tensor_tensor(out=Sa[:], in0=Sa[:], in1=b2(a31r, shn), op=ALU.mult), reads=["Sa", "tS"], writes=["Sa"])
                p.op("pool", lambda: G.tensor_tensor(out=Sbb[:], in0=Sbb[:], in1=b2(a31r, shn), op=ALU.mult), reads=["Sbb", "tS2"], writes=["Sbb"])
                p.op("dve", lambda: V.tensor_tensor(out=Sa[:], in0=Sa[:], in1=tS2[:], op=ALU.add), reads=["Sa", "tS2"], writes=["Sa"])
                p.op("pool", lambda: G.tensor_tensor(out=Sbb[:], in0=Sbb[:], in1=tS[:], op=ALU.subtract), reads=["Sbb", "tS"], writes=["Sbb"])
            Hab = p.sb("Hab", [128, 24, NCH], BF16)
            ha = [p.sb("ha", [128, 24]) for _ in range(2)]; hbb = [p.sb("hbb", [128, 24]) for _ in range(2)]
            u1 = p.sb("u1", [128, 24]); u2 = p.sb("u2", [128, 24]); u3 = p.sb("u3", [128, 24]); u4 = p.sb("u4", [128, 24])
            p.op("pool", lambda: G.memset(Hab[:], 0.0), writes=["Hab"])
            p.op("pool", lambda: G.memset(ha[0][:], 0.0), writes=["ha0"])
            p.op("pool", lambda: G.memset(hbb[0][:], 0.0), writes=["hb0"])
            for n in range(NCH - 1):
                c_, n_ = n % 2, (n + 1) % 2
                hak, hbk, hak2, hbk2 = "ha%d" % c_, "hb%d" % c_, "ha%d" % n_, "hb%d" % n_
                p.op("dve", lambda c_=c_: V.tensor_tensor(out=u1[:], in0=ha[c_][:], in1=a32r, op=ALU.mult), reads=[hak, tb], writes=["u1"])
                p.op("dve", lambda c_=c_: V.tensor_tensor(out=u2[:], in0=hbb[c_][:], in1=a32i, op=ALU.mult), reads=[hbk, tb], writes=["u2"])
                p.op("pool", lambda c_=c_: G.tensor_tensor(out=u3[:], in0=hbb[c_][:], in1=a32r, op=ALU.mult), reads=[hbk, tb], writes=["u3"])
                p.op("pool", lambda c_=c_: G.tensor_tensor(out=u4[:], in0=ha[c_][:], in1=a32i, op=ALU.mult), reads=[hak, tb], writes=["u4"])
                p.op("dve", lambda: V.tensor_tensor(out=u1[:], in0=u1[:], in1=u2[:], op=ALU.add), reads=["u1", "u2"], writes=["u1"])
                p.op("pool", lambda: G.tensor_tensor(out=u3[:], in0=u3[:], in1=u4[:], op=ALU.subtract), reads=["u3", "u4"], writes=["u3"])
                p.op("dve", lambda n=n, n_=n_: V.tensor_tensor(out=ha[n_][:], in0=u1[:], in1=Sa[:, :, n], op=ALU.add), reads=["u1", "Sa"], writes=[hak2])
                p.op("pool", lambda n=n, n_=n_: G.tensor_tensor(out=hbb[n_][:], in0=u3[:], in1=Sbb[:, :, n], op=ALU.add), reads=["u3", "Sbb"], writes=[hbk2])
                p.op("act", lambda n=n, n_=n_: A.copy(out=Hab[:, :, n + 1], in_=ha[n_][:]), reads=[hak2], writes=["Hab"])
            Tb = [p.sb("Tb", [128, 4, 512], BF16) for _ in range(NGS)]
            CL1 = [p.sb("CL1", [128, 512], BF16) for _ in range(NGS)]
            tmpus = [p.sb("tmpu", [NCH, 32, 16]) for _ in range(NGS)]
            def grp_gen(g, sl_):
                tmpu = tmpus[sl_]; tuk = "tmpu%d" % sl_
                o, ok = make_bsr(g, sl_)
                c, ck = make_clr(g, sl_)
                yield
                of = o[:].rearrange("q s h -> q (s h)")
                cfl = c[:].rearrange("q s h -> q (s h)")
                tb_ = Tb[sl_]; tbk = "Tb%d" % sl_
                for kt in range(4):
                    ps, pk = ps_next(K)
                    p.op("pe", lambda ps=ps, kt=kt, of=of, cfl=cfl: T.matmul(ps[:], lhsT=of[:, kt * 128:(kt + 1) * 128], rhs=cfl[:, 0:512], start=True, stop=True), reads=[ok, ck], writes=[pk])
                    p.op("dve", lambda ps=ps, kt=kt, tb_=tb_: V.tensor_tensor(out=tb_[:, kt, :], in0=ps[:], in1=MK[:, kt].rearrange("q t h -> q (t h)"), op=ALU.mult), reads=[pk, tb], writes=[tbk])
                c1 = CL1[sl_]; c1k = "CL1%d" % sl_
                p.op("act", lambda c1=c1, cfl=cfl: A.copy(out=c1[:], in_=cfl[:, 16:528]), reads=[ck], writes=[c1k])
                yield
                ps, pk = ps_next(K)

                def emit(ps=ps, tb_=tb_, c1=c1, g=g):
                    for kt in range(4):
                        T.matmul(ps[0:NCH, :], lhsT=UT[:, g, kt, :], rhs=tb_[:, kt, :], start=(kt == 0), stop=False)
                    return T.matmul(ps[0:NCH, :], lhsT=Hab[:, g, :], rhs=c1[:], start=False, stop=True)
                p.op("pe", emit, reads=["UT", tbk, c1k, "Hab"], writes=[pk])
                ug = Un[:, :, g * 16:(g + 1) * 16]
                p.op("pool", lambda ug=ug, g=g: G.tensor_tensor(out=tmpu[:], in0=ug, in1=b1(dB[0:NCH, g * 16:(g + 1) * 16], [NCH, 32, 16]), op=ALU.mult), reads=["Un", tb], writes=[tuk])
                p.op("dve", lambda ug=ug, ps=ps: V.tensor_tensor(out=ug, in0=ps[0:NCH, :].rearrange("n (t h) -> n t h", h=16), in1=tmpu[:], op=ALU.add), reads=[pk, tuk], writes=["Un"])
            for g0 in range(0, 24, NGS):
                gens = [grp_gen(g, g - g0) for g in range(g0, min(24, g0 + NGS))]
                while gens:
                    for g_ in list(gens):
                        try:
                            next(g_)
                        except StopIteration:
                            gens.remove(g_)
        for j in range(4):
            p.op("act", lambda j=j: A.activation(out=Un[:, j * 8:(j + 1) * 8, :], in_=Un[:, j * 8:(j + 1) * 8, :], func=AF.Gelu_apprx_tanh), reads=["Un"], writes=["Un"])
        y1T = p.sb("y1T", [128, 3, S], BF16)
        for t in range(32):
            ps, pk = ps_next(K)

            def emit(ps=ps, t=t):
                for c3 in range(3):
                    inst = T.matmul(ps[:, c3 * NCH:(c3 + 1) * NCH], lhsT=Un[:, t, c3 * 128:(c3 + 1) * 128], rhs=K.identb[0:NCH, 0:NCH], start=True, stop=True)
                return inst
            p.op("pe", emit, reads=["Un", "identb"], writes=[pk])
            dst = y1T[:].rearrange("q c (n t) -> q c n t", t=32)[:, :, :, t]
            eng = "act" if t % 2 == 0 else "dve"
            if eng == "act":
                p.op("act", lambda ps=ps, dst=dst: A.copy(out=dst, in_=ps[:, 0:3 * NCH].rearrange("q (c n) -> q c n", c=3)), reads=[pk], writes=["y1T"])
            else:
                p.op("dve", lambda ps=ps, dst=dst: V.tensor_copy(out=dst, in_=ps[:, 0:3 * NCH].rearrange("q (c n) -> q c n", c=3)), reads=[pk], writes=["y1T"])
        Wg = p.sb("Wglu", [128, 3, 384], BF16)
        p.dma("pool", [(Wg[:], K.inp["s5_glu_w"][l].rearrange("(c q) j -> q c j", q=128))], writes=["Wglu"])
        gbv = p.sb("gbv", [128, 3])
        p.dma("sp", [(gbv[:], K.inp["s5_glub"][l])], writes=["gbv"])
        sg = [p.sb("sg", [128, TT]) for _ in range(2)]
        ya = [p.sb("ya", [128, 3, TT], BF16) for _ in range(2)]
        for t in range(S // TT):
            y = ya[t % 2]; yk = "ya%d" % (t % 2)
            for jc in range(3):
                ps, pk = ps_next(K)

                def emit(ps=ps, jc=jc, t=t):
                    for kc in range(3):
                        inst = T.matmul(ps[:], lhsT=Wg[:, kc, jc * 128:(jc + 1) * 128], rhs=y1T[:, kc, t * TT:(t + 1) * TT], start=(kc == 0), stop=(kc == 2))
                    return inst
                p.op("pe", emit, reads=["Wglu", "y1T"], writes=[pk])
                s_ = sg[jc % 2]; sk = "sg%d" % (jc % 2)
                p.op("act", lambda ps=ps, s_=s_, jc=jc: A.activation(out=s_[:], in_=ps[:], func=AF.Sigmoid, bias=gbv[:, jc:jc + 1]), reads=[pk, "gbv"], writes=[sk])
                p.op("dve", lambda s_=s_, jc=jc, t=t, y=y: V.tensor_tensor(out=y[:, jc, :], in0=y1T[:, jc, t * TT:(t + 1) * TT], in1=s_[:], op=ALU.mult), reads=[sk, "y1T"], writes=[yk])
            p.dma("sp", [(K.Y[YCH["a"]:YCH["a"] + 3, :, t * TT:(t + 1) * TT].rearrange("c p t -> p c t"), y[:])], reads=[yk], writes=["Y"])
```
